# Optimizing a Trainium2 kernel written in Bass

```python
import math
import jax, jax.numpy as jnp
from jax import lax
import numpy as np

D_MODEL = 1024
BATCH = 4
SEQ = 4096
DEPTH = 2

MEM_LEN = 256
N_BRANCH = 3
MIX_W = D_MODEL // 2
CONV_K = 3
SB_HEADS = 8
SB_HEAD_DIM = MIX_W // SB_HEADS
SB_BLOCK = 128
RET_HEADS = 4
RET_HEAD_DIM = MIX_W // RET_HEADS
RET_CHUNK = 128
XA_HEADS = 4
XA_HEAD_DIM = D_MODEL // XA_HEADS
D_FF = 4 * D_MODEL
ROPE_BASE = 10000.0
EPS = 1e-6
SPLIT_POINTS = tuple(MIX_W * i for i in range(1, 11))
IN_COLS = 10 * MIX_W + N_BRANCH * D_MODEL

kernel_name = "hybrid_gated_conv_stickbreak_retention_block"


def rmsnorm(x, g):
    xf = x.astype(jnp.float32)
    y = xf * lax.rsqrt(jnp.mean(xf * xf, axis=-1, keepdims=True) + EPS)
    return (y * g.astype(jnp.float32)).astype(x.dtype)


def short_conv_mixer(gate_b, gate_c, h_in, conv_w, conv_b):
    s = h_in.shape[1]
    u = gate_c * h_in
    up = jnp.pad(u, ((0, 0), (CONV_K - 1, 0), (0, 0)))
    conv = conv_b + sum(up[:, j:j + s] * conv_w[j] for j in range(CONV_K))
    return gate_b * conv


def stick_breaking_attention(q, k, v):
    b, s, _ = q.shape
    def heads(t):
        return t.reshape(b, s, SB_HEADS, SB_HEAD_DIM).transpose(0, 2, 1, 3)
    q, k, v = heads(q), heads(k), heads(v)
    scale = SB_HEAD_DIM ** -0.5
    outs = []
    for start in range(0, s, SB_BLOCK):
        end = start + SB_BLOCK
        qb = q[:, :, start:end]
        kb = k[:, :, :end]
        vb = v[:, :, :end]
        z = jnp.einsum('bhtd,bhsd->bhts', qb, kb).astype(jnp.float32) * scale
        t_idx = start + jnp.arange(SB_BLOCK)[:, None]
        s_idx = jnp.arange(end)[None, :]
        causal = s_idx < t_idx
        log_not = jnp.where(causal, jax.nn.log_sigmoid(-z), 0.0)
        between = lax.cumsum(log_not, axis=3, reverse=True) - log_not
        a = jnp.where(causal, jnp.exp(jax.nn.log_sigmoid(z) + between), 0.0)
        outs.append(jnp.einsum('bhts,bhsd->bhtd', a.astype(vb.dtype), vb))
    o = jnp.concatenate(outs, axis=2)
    return o.transpose(0, 2, 1, 3).reshape(b, s, MIX_W)


def rotary(t):
    s, d = t.shape[1], t.shape[3]
    pos = jnp.arange(s, dtype=jnp.float32)
    inv_freq = ROPE_BASE ** (-jnp.arange(0, d, 2, dtype=jnp.float32) / d)
    ang = pos[:, None] * inv_freq[None, :]
    cos = jnp.cos(ang)[None, :, None, :]
    sin = jnp.sin(ang)[None, :, None, :]
    t1, t2 = jnp.split(t, 2, axis=-1)
    return jnp.concatenate([t1 * cos - t2 * sin, t1 * sin + t2 * cos], axis=-1)


def retention(q, k, v, g, gn_gain):
    b, s, _ = q.shape
    nc = s // RET_CHUNK
    f32 = jnp.float32
    def heads(t):
        return t.astype(f32).reshape(b, s, RET_HEADS, RET_HEAD_DIM)
    qh = rotary(heads(q))
    kh = rotary(heads(k)) * RET_HEAD_DIM ** -0.5
    vh = heads(v)
    def chunks(t):
        return t.reshape(b, nc, RET_CHUNK, RET_HEADS, RET_HEAD_DIM).transpose(0, 3, 1, 2, 4)
    qc, kc, vc = chunks(qh), chunks(kh), chunks(vh)
    log_gamma = jnp.log1p(-jnp.exp2(-5.0 - jnp.arange(RET_HEADS, dtype=f32)))
    idx = jnp.arange(RET_CHUNK, dtype=f32)
    rel = idx[:, None] - idx[None, :]
    decay_intra = jnp.where(rel >= 0, jnp.exp(jnp.maximum(rel, 0.0)[None] * log_gamma[:, None, None]), 0.0)
    scores = jnp.einsum('bhnid,bhnjd->bhnij', qc, kc) * decay_intra[None, :, None]
    o_inner = jnp.einsum('bhnij,bhnje->bhnie', scores, vc)
    k_decay = jnp.exp((RET_CHUNK - 1 - idx)[None] * log_gamma[:, None])
    kv = jnp.einsum('bhnjd,bhnje->bhnde', kc * k_decay[None, :, None, :, None], vc)
    chunk_decay = jnp.exp(RET_CHUNK * log_gamma)[None, :, None, None]
    def step(state, kv_n):
        return chunk_decay * state + kv_n, state
    init = jnp.zeros((b, RET_HEADS, RET_HEAD_DIM, RET_HEAD_DIM), f32)
    _, prev = lax.scan(step, init, jnp.moveaxis(kv, 2, 0))
    prev = jnp.moveaxis(prev, 0, 2)
    q_decay = jnp.exp((idx + 1)[None] * log_gamma[:, None])
    o_cross = jnp.einsum('bhnid,bhnde->bhnie', qc, prev) * q_decay[None, :, None, :, None]
    o = o_inner + o_cross
    mu = jnp.mean(o, axis=-1, keepdims=True)
    var = jnp.mean(jnp.square(o - mu), axis=-1, keepdims=True)
    o = (o - mu) * lax.rsqrt(var + EPS)
    o = o.transpose(0, 2, 3, 1, 4).reshape(b, s, MIX_W) * gn_gain.astype(f32)
    return (jax.nn.silu(g.astype(f32)) * o).astype(g.dtype)


def memory_attention(h, mem_n, w_q, w_kv, w_out):
    b, s, _ = h.shape
    m = mem_n.shape[1]
    q = (h @ w_q).reshape(b, s, XA_HEADS, XA_HEAD_DIM)
    k, v = jnp.split(mem_n @ w_kv, 2, axis=-1)
    k = k.reshape(b, m, XA_HEADS, XA_HEAD_DIM)
    v = v.reshape(b, m, XA_HEADS, XA_HEAD_DIM)
    sc = jnp.einsum('bshd,bmhd->bhsm', q, k).astype(jnp.float32) * XA_HEAD_DIM ** -0.5
    p = jax.nn.softmax(sc, axis=-1).astype(v.dtype)
    o = jnp.einsum('bhsm,bmhd->bshd', p, v).reshape(b, s, D_MODEL)
    return o @ w_out


def squared_relu_mlp(h, w_up, w_down):
    return jnp.square(jax.nn.relu(h @ w_up)) @ w_down


def setup_inputs(seed: int = 0) -> dict:
    key = jax.random.key(seed)
    ks = jax.random.split(key, 20)
    f32 = jnp.float32
    def nrm(k, shape, scale):
        return jax.random.normal(k, shape, f32) * scale
    return {
        "x": nrm(ks[0], (BATCH, SEQ, D_MODEL), 1.0),
        "mem": nrm(ks[1], (BATCH, MEM_LEN, D_MODEL), 1.0),
        "norm_mix_g": 1.0 + nrm(ks[2], (DEPTH, D_MODEL), 0.02),
        "w_in": nrm(ks[3], (DEPTH, D_MODEL, IN_COLS), D_MODEL ** -0.5),
        "b_gate": nrm(ks[4], (DEPTH, N_BRANCH, D_MODEL), 0.02),
        "conv_w": nrm(ks[5], (DEPTH, CONV_K, MIX_W), CONV_K ** -0.5),
        "conv_b": nrm(ks[6], (DEPTH, MIX_W), 0.02),
        "ret_norm_g": 1.0 + nrm(ks[7], (DEPTH, MIX_W), 0.02),
        "w_branch": nrm(ks[8], (DEPTH, N_BRANCH, MIX_W, D_MODEL), MIX_W ** -0.5),
        "w_o": nrm(ks[9], (DEPTH, D_MODEL, D_MODEL), D_MODEL ** -0.5),
        "norm_xa_g": 1.0 + nrm(ks[10], (DEPTH, D_MODEL), 0.02),
        "norm_mem_g": 1.0 + nrm(ks[11], (DEPTH, D_MODEL), 0.02),
        "w_xq": nrm(ks[12], (DEPTH, D_MODEL, D_MODEL), D_MODEL ** -0.5),
        "w_xkv": nrm(ks[13], (DEPTH, D_MODEL, 2 * D_MODEL), D_MODEL ** -0.5),
        "w_xo": nrm(ks[14], (DEPTH, D_MODEL, D_MODEL), D_MODEL ** -0.5),
        "norm_mlp_g": 1.0 + nrm(ks[15], (DEPTH, D_MODEL), 0.02),
        "w_up": nrm(ks[16], (DEPTH, D_MODEL, D_FF), D_MODEL ** -0.5),
        "w_down": nrm(ks[17], (DEPTH, D_FF, D_MODEL), D_FF ** -0.5),
        "final_g": 1.0 + nrm(ks[18], (D_MODEL,), 0.02),
    }


def reference(x, mem, norm_mix_g, w_in, b_gate, conv_w, conv_b, ret_norm_g, w_branch, w_o,
              norm_xa_g, norm_mem_g, w_xq, w_xkv, w_xo, norm_mlp_g, w_up, w_down, final_g):
    b, s, _ = x.shape
    for l in range(DEPTH):
        h = rmsnorm(x, norm_mix_g[l])
        proj = h @ w_in[l]
        cb, cc, ch, sq, sk, sv, rq, rk, rv, rg, gate_logits = jnp.split(proj, SPLIT_POINTS, axis=-1)
        conv_out = short_conv_mixer(cb, cc, ch, conv_w[l], conv_b[l])
        sb_out = stick_breaking_attention(sq, sk, sv)
        ret_out = retention(rq, rk, rv, rg, ret_norm_g[l])
        branches = jnp.stack([conv_out, sb_out, ret_out], axis=2)
        up = jnp.einsum('bsnw,nwd->bsnd', branches, w_branch[l])
        gates = jax.nn.sigmoid(gate_logits.reshape(b, s, N_BRANCH, D_MODEL) + b_gate[l])
        merged = jnp.sum(gates * up, axis=2)
        x = x + merged @ w_o[l]
        h = rmsnorm(x, norm_xa_g[l])
        mem_n = rmsnorm(mem, norm_mem_g[l])
        x = x + memory_attention(h, mem_n, w_xq[l], w_xkv[l], w_xo[l])
        h = rmsnorm(x, norm_mlp_g[l])
        x = x + squared_relu_mlp(h, w_up[l], w_down[l])
    return rmsnorm(x, final_g)
```

```python
import numpy as np
import ml_dtypes
import concourse.bass as bass
import concourse.mybir as mybir
from concourse.bass_utils import run_bass_kernel_spmd
from contextlib import ExitStack

F32 = mybir.dt.float32
BF16 = mybir.dt.bfloat16
ALU = mybir.AluOpType
AF = mybir.ActivationFunctionType

D_MODEL = 1024
MEM_LEN = 256
IN_COLS = 8192
D_FF = 4096
EPS = 1e-6
TG = 512
ACTIVE_CORES = (0, 1, 4, 5)
DISABLE = set()
CUT = 0


class StopBuild(Exception):
    pass


def cut(n):
    if CUT == n:
        raise StopBuild()


class Buf:
    __slots__ = ("name", "lw", "rd", "sem", "dcount", "last_dma")

    def __init__(self, name):
        self.name = name
        self.lw = None
        self.rd = []
        self.sem = None
        self.dcount = 0
        self.last_dma = None


class Inst:
    __slots__ = ("eng", "fn", "hard", "soft", "sig", "is_dma", "owner", "dval", "semval")

    def __init__(self, eng, fn, is_dma=False):
        self.eng = eng
        self.fn = fn
        self.hard = []
        self.soft = []
        self.sig = False
        self.is_dma = is_dma
        self.owner = None
        self.dval = 0
        self.semval = 0


class Prog:
    ENGS = ("pe", "act", "dve", "pool", "sp")

    def __init__(self, nc, es):
        self.nc = nc
        self.es = es
        self.q = {e: [] for e in self.ENGS}
        self.uid = 0
        self.sbytes = 0

    def sb(self, shape, dtype, name=None):
        self.uid += 1
        n = 1
        for s in shape[1:]:
            n *= s
        self.sbytes += n * (2 if dtype == BF16 else 4)
        return self.es.enter_context(self.nc.sbuf_tensor(name or f"sb{self.uid}", list(shape), dtype))

    def ps(self, shape, dtype=F32, name=None):
        self.uid += 1
        return self.es.enter_context(self.nc.psum_tensor(name or f"ps{self.uid}", list(shape), dtype))

    def buf(self, name=None):
        self.uid += 1
        return Buf(name or f"b{self.uid}")

    def new_sem(self, name):
        return self.es.enter_context(self.nc.semaphore(name))

    def _track(self, inst, reads, writes):
        wset = set(id(b) for b in writes)
        for b in reads:
            if b.lw is not None:
                inst.hard.append(b.lw)
        for b in writes:
            if b.lw is not None:
                inst.hard.append(b.lw)
            inst.soft.extend(b.rd)
        for b in reads:
            if id(b) not in wset:
                b.rd.append(inst)
        for b in writes:
            b.lw = inst
            b.rd = []

    def op(self, eng, fn, reads=(), writes=()):
        inst = Inst(eng, fn)
        self._track(inst, reads, writes)
        self.q[eng].append(inst)
        return inst

    def dma(self, queue, fn, owner, reads=(), writes=(), npieces=1):
        inst = Inst(queue, fn, is_dma=True)
        self._track(inst, reads, writes)
        if owner.last_dma is not None:
            inst.hard.append(owner.last_dma)
        owner.last_dma = inst
        if owner.sem is None:
            self.uid += 1
            owner.sem = self.new_sem(f"dsem{self.uid}")
        owner.dcount += 16 * npieces
        inst.owner = owner
        inst.dval = owner.dcount
        self.q[queue].append(inst)
        return inst

    def generate(self, block):
        esem = {e: self.new_sem(f"esem_{e}") for e in ("pe", "act", "dve", "pool")}
        for e in self.ENGS:
            for inst in self.q[e]:
                deps = []
                for p in inst.hard:
                    if p is inst:
                        continue
                    if p.is_dma or inst.is_dma or p.eng != inst.eng or inst.eng != "pe":
                        deps.append(p)
                for p in inst.soft:
                    if p is inst:
                        continue
                    if p.is_dma or inst.is_dma or p.eng != inst.eng:
                        deps.append(p)
                inst.hard = deps
                inst.soft = None
                for p in deps:
                    if not p.is_dma:
                        p.sig = True
        for e in ("pe", "act", "dve", "pool"):
            c = 0
            for inst in self.q[e]:
                if inst.is_dma:
                    continue
                if inst.sig:
                    c += 1
                    inst.semval = c
        self.stats = {}

        def gen(e, eng):
            waited = {}
            nw = 0
            for inst in self.q[e]:
                need = {}
                for p in inst.hard:
                    if p.is_dma:
                        s, v = p.owner.sem, p.dval
                    else:
                        s, v = esem[p.eng], p.semval
                    k = id(s)
                    if waited.get(k, 0) >= v:
                        continue
                    if k not in need or need[k][1] < v:
                        need[k] = (s, v)
                for k, (s, v) in need.items():
                    eng.wait_ge(s, v)
                    waited[k] = v
                    nw += 1
                r = inst.fn(eng)
                if inst.is_dma:
                    for h in r:
                        h.then_inc(inst.owner.sem, 16)
                elif inst.sig:
                    r.then_inc(esem[e], 1)
            self.stats[e] = (len(self.q[e]), nw)

        @block.sync
        def _(eng):
            gen("sp", eng)

        @block.scalar
        def _(eng):
            gen("act", eng)

        @block.vector
        def _(eng):
            gen("dve", eng)

        @block.gpsimd
        def _(eng):
            gen("pool", eng)

        @block.tensor
        def _(eng):
            gen("pe", eng)


PV_L = 8 * 4 + 24 + 12 + 4


def pv_off(l, name):
    base = l * PV_L
    offs = {"nmix": 0, "nxa": 8, "nmem": 16, "nmlp": 24, "bgate": 32, "convw": 56, "convb": 68}
    return base + offs[name]


def host_constants(T):
    f32 = np.float32
    idx = np.arange(128, dtype=f32)
    log_gamma = np.log1p(-np.exp2(-5.0 - np.arange(4, dtype=f32))).astype(f32)
    j = idx[:, None]
    i = idx[None, :]
    decayT = np.zeros((128, 4, 128), f32)
    for h in range(4):
        rel = i - j
        decayT[:, h, :] = np.where(rel >= 0, np.exp(np.maximum(rel, 0.0) * log_gamma[h]), 0.0)
    ks = f32(128.0 ** -0.5)
    decayT = (decayT * ks).astype(f32)
    qdec = np.zeros((128, 4, 128), f32)
    for h in range(4):
        qdec[:, h, :] = np.exp((idx + 1) * log_gamma[h])[None, :]
    mask = (j < i).astype(f32)
    kdecay = (np.exp((127 - idx)[:, None] * log_gamma[None, :]) * ks).astype(f32)
    cf = np.concatenate([decayT.reshape(128, 512), qdec.reshape(128, 512), mask, kdecay], axis=1).astype(f32)
    cd = [float(np.exp(f32(128.0) * log_gamma[h])) for h in range(4)]
    ident = np.eye(128, dtype=f32)
    ones = np.ones((128, 128), f32)
    lincl = (j >= i).astype(f32)
    lc = 1.0 - lincl
    cb = np.concatenate([ident, ones, lincl, lc, mask], axis=1).astype(ml_dtypes.bfloat16)
    pos = np.arange(T, dtype=f32)
    inv_freq = (f32(10000.0) ** (-np.arange(0, 128, 2, dtype=f32) / f32(128))).astype(f32)
    ang = (pos[:, None] * inv_freq[None, :]).astype(f32)
    cos = np.cos(ang).astype(f32).T
    sin = np.sin(ang).astype(f32).T
    cosf = np.concatenate([cos, cos], 0)
    sinf = np.concatenate([sin, -sin], 0)
    rot = np.stack([cosf, sinf], 0).astype(f32)
    return cf, cb, rot, cd


CF_DECAY, CF_QDEC, CF_MASK, CF_KDEC, CF_N = 0, 512, 1024, 1024 + 128, 1024 + 128 + 4
CB_ID, CB_ONES, CB_LINCL, CB_LC, CB_MASK, CB_N = 0, 128, 256, 384, 512, 640


def col8(v):
    return np.ascontiguousarray(v.reshape(-1, 128).T)


def host_pvec(inp, L):
    cols = []
    for l in range(L):
        cols += [col8(inp["norm_mix_g"][l]), col8(inp["norm_xa_g"][l]), col8(inp["norm_mem_g"][l]),
                 col8(inp["norm_mlp_g"][l])]
        cols.append(np.concatenate([col8(inp["b_gate"][l][n]) for n in range(3)], axis=1))
        cols.append(np.concatenate([col8(inp["conv_w"][l][jj]) for jj in range(3)], axis=1))
        cols.append(col8(inp["conv_b"][l]))
    cols.append(col8(inp["final_g"]))
    return np.ascontiguousarray(np.concatenate(cols, axis=1).astype(np.float32))


def build(T, L, cd, dbg=False):
    NG = T // TG
    NB = T // 128
    nc = bass.Bass("TRN2", target_bir_lowering=False)

    def din(name, shape, dt=F32):
        return nc.dram_tensor(name, list(shape), dt, kind="ExternalInput").ap()

    xT_d = din("xT", [1024, T])
    memT_d = din("memT", [1024, MEM_LEN])
    pv_d = din("pv", [128, L * PV_L + 8])
    cf_d = din("cf", [128, CF_N])
    cb_d = din("cb", [128, CB_N], BF16)
    rot_d = din("rot", [2, 128, T])
    rgain_d = din("rgain", [L, 128, 512])
    w_in_d = din("w_in", [L, 1024, IN_COLS])
    w_br_d = din("w_branch", [L, 3, 512, 1024])
    w_o_d = din("w_o", [L, 1024, 1024])
    w_xq_d = din("w_xq", [L, 1024, 1024])
    w_xkv_d = din("w_xkv", [L, 1024, 2048])
    w_xo_d = din("w_xo", [L, 1024, 1024])
    w_up_d = din("w_up", [L, 1024, D_FF])
    w_dn_d = din("w_down", [L, D_FF, 1024])
    out_d = nc.dram_tensor("outT", [1024, T], F32, kind="ExternalOutput").ap()

    def dint(name, shape, dt=BF16):
        return nc.dram_tensor(name, list(shape), dt).ap()

    wb_in = dint("wb_in", [L, 1024, IN_COLS])
    wb_br = dint("wb_br", [L, 3, 512, 1024])
    wb_o = dint("wb_o", [L, 1024, 1024])
    wb_xq = dint("wb_xq", [L, 1024, 1024])
    wb_xkv = dint("wb_xkv", [L, 1024, 2048])
    wb_xo = dint("wb_xo", [L, 1024, 1024])
    wb_up = dint("wb_up", [L, 1024, D_FF])
    wb_dn = dint("wb_dn", [L, 8, 128, 32 * 128])
    xs_d = dint("xs", [1024, T], F32)

    with ExitStack() as es:
        P = Prog(nc, es)
        block = es.enter_context(nc.Block())

        pv = P.sb([128, L * PV_L + 8], F32); PVB = P.buf("pv")
        cf = P.sb([128, CF_N], F32); CFB = P.buf("cf")
        cb = P.sb([128, CB_N], BF16); CBB = P.buf("cb")
        zeros = P.sb([128, 512], BF16); ZB = P.buf("zeros")
        rgain = P.sb([128, 512], F32); RGB = P.buf("rgain")
        NW = 4
        wslots = [(P.sb([128, 4096], BF16), P.buf(f"ws{i}")) for i in range(NW)]
        xts = [(P.sb([128, 8, 512], F32), [P.buf(f"x{i}_{c}") for c in range(8)]) for i in range(1)]
        ht = P.sb([128, 8, 512], BF16); HB = [P.buf(f"h{c}") for c in range(8)]
        rstd = P.sb([128, 512], F32); RSB = P.buf("rstd")
        NA = 32
        arena = P.sb([128, NA, 512], BF16); AB = [P.buf(f"ar{i}") for i in range(NA)]
        sq = arena[:, 0:8, :]; SQB = AB[0:8]
        marena = arena[:, 20:28, :].rearrange("p a n -> p (a n)").bitcast(F32)
        brT = P.sb([128, 12, 512], BF16); BRB = [P.buf(f"br{i}") for i in range(12)]
        kc = P.sb([128, 4, T], BF16); KCB = [[P.buf(f"kc{c}_{g}") for g in range(NG)] for c in range(4)]
        vc = P.sb([128, NB, 512], BF16); VCB = [P.buf(f"vc{b}") for b in range(NB)]
        rot_t = P.sb([128, 2, 512], F32); ROTB = P.buf("rot")
        kmem = P.sb([128, 8, 256], BF16); KMB = P.buf("kmem")
        vmem = P.sb([128, 2, 1024], BF16); VMB = P.buf("vmem")
        state = P.sb([128, 4, 128], F32); STB = P.buf("state")
        state_bf = P.sb([128, 4, 128], BF16); STBB = P.buf("statebf")
        halo = P.sb([128, 4, 2], F32); HALB = [P.buf(f"halo{c}") for c in range(4)]
        NF = 5
        ftiles = [(P.sb([128, 514], F32), P.buf(f"ft{i}")) for i in range(NF)]
        NBT = 7
        btiles = [(P.sb([128, 512], BF16), P.buf(f"bt{i}")) for i in range(NBT)]
        small = P.sb([128, 64], F32); SMB = P.buf("small")

        PB = [(P.ps([128, 512], F32), P.buf(f"pb{i}")) for i in range(7)]
        tb_ps = P.ps([128, 1024], BF16); _tb = P.buf("tb"); TBB = [_tb, _tb]
        ROT_BANKS = [0, 1, 2, 3]
        B_BANKS = [4, 5]
        OUT_BANK = 6
        st = {"rb": 0, "ws": 0, "ft": 0, "bt": 0, "tb": 0}

        def rbank():
            i = ROT_BANKS[st["rb"] % len(ROT_BANKS)]
            st["rb"] += 1
            return PB[i]

        def wslot():
            s = wslots[st["ws"] % NW]
            st["ws"] += 1
            return s

        def ftile():
            s = ftiles[st["ft"] % NF]
            st["ft"] += 1
            return s

        def btile():
            s = btiles[st["bt"] % NBT]
            st["bt"] += 1
            return s

        def tbank():
            i = st["tb"] % 2
            st["tb"] += 1
            return tb_ps[:, i * 512:(i + 1) * 512], TBB[i]

        ident = cb[:, CB_ID:CB_ID + 128]
        ones_bf = cb[:, CB_ONES:CB_ONES + 128]
        lincl = cb[:, CB_LINCL:CB_LINCL + 128]
        lcomp = cb[:, CB_LC:CB_LC + 128]
        mask_bf = cb[:, CB_MASK:CB_MASK + 128]

        def mm(out, lhsT, rhs, start, stop, R, Wb):
            P.op("pe", lambda e: e.matmul(out, lhsT=lhsT, rhs=rhs, start=start, stop=stop), reads=R, writes=Wb)

        def tr(out, in_, R, Wb):
            P.op("pe", lambda e: e.transpose(out, in_, ident), reads=R + [CBB], writes=Wb)

        def act(out, in_, func, R, Wb, scale=None, bias=None):
            kw = {}
            if scale is not None:
                kw["scale"] = scale
            if bias is not None:
                kw["bias"] = bias
            P.op("act", lambda e: e.activation(out=out, in_=in_, func=func, **kw), reads=R, writes=Wb)

        def tt(eng, out, in0, in1, op, R, Wb):
            P.op(eng, lambda e: e.tensor_tensor(out=out, in0=in0, in1=in1, op=op), reads=R, writes=Wb)

        def ts(eng, out, in0, s1, s2, op0, op1, R, Wb):
            if op1 is None:
                P.op(eng, lambda e: e.tensor_scalar(out=out, in0=in0, scalar1=s1, scalar2=None, op0=op0), reads=R, writes=Wb)
            else:
                P.op(eng, lambda e: e.tensor_scalar(out=out, in0=in0, scalar1=s1, scalar2=s2, op0=op0, op1=op1), reads=R, writes=Wb)

        def stt(out, in0, scalar, in1, op0, op1, R, Wb):
            P.op("dve", lambda e: e.scalar_tensor_tensor(out=out, in0=in0, scalar=scalar, in1=in1, op0=op0, op1=op1), reads=R, writes=Wb)

        def cp(eng, out, in_, R, Wb):
            P.op(eng, lambda e: e.tensor_copy(out=out, in_=in_), reads=R, writes=Wb)

        def load(queue, out, in_, owner, R=(), Wb=None):
            P.dma(queue, lambda e: [e.dma_start(out=out, in_=in_)], owner, reads=list(R), writes=[owner] if Wb is None else Wb)

        load("sp", pv[:], pv_d, PVB)
        load("sp", cf[:], cf_d, CFB)
        load("sp", cb[:], cb_d, CBB)
        P.op("dve", lambda e: e.memset(zeros[:], 0.0), writes=[ZB])

        WB = {}

        PCB = P.buf("precast_chain")

        def precast(name, l, pieces):
            b = P.buf(f"wb_{name}{l}")
            for (o, i) in pieces:
                P.dma("pool", lambda e, o=o, i=i: [e.dma_start(out=o, in_=i)], PCB, writes=[b])
            WB[(name, l)] = b

        for l in range(L):
            precast("in", l, [(wb_in[l, r * 128:(r + 1) * 128, :], w_in_d[l, r * 128:(r + 1) * 128, :]) for r in range(8)])
            precast("xkv", l, [(wb_xkv[l, r * 256:(r + 1) * 256, :], w_xkv_d[l, r * 256:(r + 1) * 256, :]) for r in range(4)])
            precast("br", l, [(wb_br[l, n], w_br_d[l, n]) for n in range(3)])
            precast("o", l, [(wb_o[l, r * 512:(r + 1) * 512, :], w_o_d[l, r * 512:(r + 1) * 512, :]) for r in range(2)])
            precast("xq", l, [(wb_xq[l, r * 512:(r + 1) * 512, :], w_xq_d[l, r * 512:(r + 1) * 512, :]) for r in range(2)])
            precast("xo", l, [(wb_xo[l, r * 512:(r + 1) * 512, :], w_xo_d[l, r * 512:(r + 1) * 512, :]) for r in range(2)])
            precast("up", l, [(wb_up[l, r * 256:(r + 1) * 256, :], w_up_d[l, r * 256:(r + 1) * 256, :]) for r in range(4)])
            precast("dn", l, [(wb_dn[l, d].rearrange("p (c n) -> p c n", n=128),
                               w_dn_d[l, :, d * 128:(d + 1) * 128].rearrange("(c p) n -> p c n", p=128)) for d in range(8)])

        def wload_cols(name, l, src2d, c0, ncols, kchunks):
            slot, sb_ = wslot()
            view = slot[:, 0:kchunks * ncols].rearrange("p (c n) -> p c n", n=ncols)
            src = src2d.rearrange("(c p) n -> p c n", p=128)[:, :, c0:c0 + ncols]
            load("sp", view, src, sb_, R=[WB[(name, l)]])
            return view, sb_

        def rmsnorm(xt, XB, ncol, gcol, out_t, OB, out_is_f32=False):
            for c in range(8):
                tt("pool", sq[:, c, 0:ncol], xt[:, c, :], xt[:, c, :], ALU.mult, [XB[c]], [SQB[c]])
            pst, psb = rbank()
            for c in range(8):
                mm(pst[:, 0:ncol], ones_bf, sq[:, c, 0:ncol], c == 0, c == 7, [SQB[c], CBB], [psb])
            act(rstd[:, 0:ncol], pst[:, 0:ncol], AF.Ln, [psb], [RSB], scale=1.0 / 1024.0, bias=EPS)
            act(rstd[:, 0:ncol], rstd[:, 0:ncol], AF.Exp, [RSB], [RSB], scale=-0.5)
            for c in range(8):
                stt(out_t[:, c, :], xt[:, c, :], pv[:, gcol + c:gcol + c + 1], rstd[:, 0:ncol], ALU.mult, ALU.mult,
                    [XB[c], PVB, RSB], [OB[c]])

        def linear_fm(name, l, src2d, col0, nchunks, kchunks, rhs_fn, consume):
            done = 0
            while done < nchunks:
                n = min(4, nchunks - done)
                view, wsb = wload_cols(name, l, src2d, col0 + done * 128, n * 128, kchunks)
                for cc in range(n):
                    pst, psb = rbank()
                    for k in range(kchunks):
                        r_ap, r_b = rhs_fn(k)
                        mm(pst[:, :], view[:, k, cc * 128:(cc + 1) * 128], r_ap, k == 0, k == kchunks - 1, [wsb, r_b], [psb])
                    consume(done + cc, pst, psb)
                done += n

        def linear_tm(name, l, src2d, col0, consume):
            view, wsb = wload_cols(name, l, src2d, col0, 512, 8)
            for tb in range(4):
                pst, psb = rbank()
                for k in range(8):
                    mm(pst[:, :], ht[:, k, tb * 128:(tb + 1) * 128], view[:, k, :], k == 0, k == 7, [wsb, HB[k]], [psb])
                consume(tb, pst, psb)

        h_rhs = lambda k: (ht[:, k, :], HB[k])

        A_CB, A_CC, A_SQ, A_SQN, A_RQ, A_RK = 0, 4, 8, 12, 16, 20
        A_RV, A_GS = 24, 28
        A_G, A_MG = 0, 12
        A_Q2, A_OT = 16, 24
        BR_CONV, BR_SB, BR_RET = 0, 4, 8

        def arena_tm(a0):
            return arena[:, a0:a0 + 4, :], AB[a0:a0 + 4]

        XSD = [P.buf(f"xsd{i}") for i in range(NG)]

        try:
            cut(1)
            for l in range(L):
                load("sp", rgain[:], rgain_d[l], RGB)
                P.op("pool", lambda e: e.memset(state[:], 0.0), writes=[STB])
                P.op("pool", lambda e: e.memset(state_bf[:], 0.0), writes=[STBB])
                P.op("pool", lambda e: e.memset(halo[:], 0.0), writes=HALB)
                xt0, XB0 = xts[0]
                memx = xt0[:, :, 0:MEM_LEN]
                P.dma("sp", lambda e, memx=memx: [e.dma_start(out=memx, in_=memT_d.rearrange("(c p) n -> p c n", p=128))],
                      XB0[0], writes=XB0)
                rmsnorm(memx, XB0, MEM_LEN, pv_off(l, "nmem"), ht[:, :, 0:MEM_LEN], HB)
                def kmem_consume(ci, pst, psb):
                    cp("dve", kmem[:, ci, :], pst[:, 0:MEM_LEN], [psb], [KMB])
                done = 0
                for half in range(2):
                    view, wsb = wload_cols("xkv", l, wb_xkv[l], half * 512, 512, 8)
                    for cc in range(4):
                        pst, psb = rbank()
                        for k in range(8):
                            mm(pst[:, 0:MEM_LEN], view[:, k, cc * 128:(cc + 1) * 128], ht[:, k, 0:MEM_LEN], k == 0, k == 7, [wsb, HB[k]], [psb])
                        kmem_consume(half * 4 + cc, pst, psb)
                for half in range(2):
                    view, wsb = wload_cols("xkv", l, wb_xkv[l], 1024 + half * 512, 512, 8)
                    for mb in range(2):
                        pst, psb = rbank()
                        for k in range(8):
                            mm(pst[:, :], ht[:, k, mb * 128:(mb + 1) * 128], view[:, k, :], k == 0, k == 7, [wsb, HB[k]], [psb])
                        cp("dve", vmem[:, mb, half * 512:(half + 1) * 512], pst[:, :], [psb], [VMB])

                cut(2)
                for g in range(NG):
                    xt, XB = xts[0]
                    t0 = g * TG
                    src = xT_d if l == 0 else xs_d
                    rd = [] if l == 0 else [XSD[g]]
                    P.dma("sp", lambda e, xt=xt, src=src, t0=t0: [e.dma_start(out=xt[:], in_=src.rearrange("(c p) n -> p c n", p=128)[:, :, t0:t0 + TG])],
                          XB[0], reads=rd, writes=XB)
                    load("sp", rot_t[:], rot_d[:, :, t0:t0 + TG].rearrange("f p n -> p f n"), ROTB)

                    rmsnorm(xt, XB, TG, pv_off(l, "nmix"), ht, HB)
                    win = wb_in[l]
                    cut(3)

                    def ev_bf(a0, alt=False):
                        def f(ci, pst, psb):
                            if (ci % 2 == 0) != alt:
                                act(arena[:, a0 + ci, :], pst[:, :], AF.Copy, [psb], [AB[a0 + ci]])
                            else:
                                cp("dve", arena[:, a0 + ci, :], pst[:, :], [psb], [AB[a0 + ci]])
                        return f

                    linear_fm("in", l, win, 0, 4, 8, h_rhs, ev_bf(A_CB))
                    linear_fm("in", l, win, 512, 4, 8, h_rhs, ev_bf(A_CC, True))
                    cut(4)

                    def conv_consume(ci, pst, psb):
                        u, ub = ftile()
                        cp("pool", u[:, 0:2], halo[:, ci, :], [HALB[ci]], [ub])
                        tt("dve", u[:, 2:514], pst[:, :], arena[:, A_CC + ci, :], ALU.mult, [psb, AB[A_CC + ci]], [ub])
                        cp("pool", halo[:, ci, :], u[:, 512:514], [ub], [HALB[ci]])
                        a, ab_ = ftile()
                        cw = pv_off(l, "convw")
                        cbias = pv_off(l, "convb")
                        ts("pool", a[:, 0:512], u[:, 2:514], pv[:, cw + 8 + ci:cw + 9 + ci], pv[:, cbias + ci:cbias + ci + 1],
                           ALU.mult, ALU.add, [ub, PVB], [ab_])
                        t1, t1b = ftile()
                        ts("pool", t1[:, 0:512], u[:, 1:513], pv[:, cw + 4 + ci:cw + 5 + ci], None, ALU.mult, None, [ub, PVB], [t1b])
                        tt("pool", a[:, 0:512], a[:, 0:512], t1[:, 0:512], ALU.add, [ab_, t1b], [ab_])
                        t2, t2b = ftile()
                        ts("pool", t2[:, 0:512], u[:, 0:512], pv[:, cw + ci:cw + 1 + ci], None, ALU.mult, None, [ub, PVB], [t2b])
                        tt("pool", a[:, 0:512], a[:, 0:512], t2[:, 0:512], ALU.add, [ab_, t2b], [ab_])
                        tt("pool", brT[:, BR_CONV + ci, :], a[:, 0:512], arena[:, A_CB + ci, :], ALU.mult, [ab_, AB[A_CB + ci]], [BRB[BR_CONV + ci]])

                    linear_fm("in", l, win, 1024, 4, 8, h_rhs, conv_consume)
                    cut(5)

                    def sq_consume(ci, pst, psb):
                        if "sqa" not in DISABLE:
                            act(arena[:, A_SQ + ci, :], pst[:, :], AF.Copy, [psb], [AB[A_SQ + ci]], scale=0.125)
                        if "sqb" not in DISABLE:
                            ts("pool", arena[:, A_SQN + ci, :], arena[:, A_SQ + ci, :], -1.0, None, ALU.mult, None, [AB[A_SQ + ci]], [AB[A_SQN + ci]])

                    linear_fm("in", l, win, 1536, 4, 8, h_rhs, sq_consume)
                    cut(51)

                    def sk_consume(ci, pst, psb):
                        if ci % 2 == 0:
                            act(kc[:, ci, t0:t0 + TG], pst[:, :], AF.Copy, [psb], [KCB[ci][g]])
                        else:
                            cp("dve", kc[:, ci, t0:t0 + TG], pst[:, :], [psb], [KCB[ci][g]])

                    linear_fm("in", l, win, 2048, 4, 8, h_rhs, sk_consume)
                    cut(52)

                    def sv_consume(tb, pst, psb):
                        if tb % 2 == 0:
                            act(vc[:, g * 4 + tb, :], pst[:, :], AF.Copy, [psb], [VCB[g * 4 + tb]])
                        else:
                            cp("dve", vc[:, g * 4 + tb, :], pst[:, :], [psb], [VCB[g * 4 + tb]])

                    linear_tm("in", l, win, 2560, sv_consume)
                    cut(6)

                    def rot_consume(a0, fc, fs):
                        def f(ci, pst, psb):
                            t1, t1b = ftile()
                            t2, t2b = ftile()
                            tt("dve", t1[:, 0:512], pst[:, :], rot_t[:, fc, :], ALU.mult, [psb, ROTB], [t1b])
                            tt("dve", t2[0:64, 0:512], pst[64:128, :], rot_t[64:128, fs, :], ALU.mult, [psb, ROTB], [t2b])
                            tt("dve", t2[64:128, 0:512], pst[0:64, :], rot_t[0:64, fs, :], ALU.mult, [psb, ROTB], [t2b])
                            tt("pool", arena[:, a0 + ci, :], t1[:, 0:512], t2[:, 0:512], ALU.add, [t1b, t2b], [AB[a0 + ci]])
                        return f

                    linear_fm("in", l, win, 3072, 4, 8, h_rhs, rot_consume(A_RQ, 0, 1))
                    linear_fm("in", l, win, 3584, 4, 8, h_rhs, rot_consume(A_RK, 0, 1))

                    rv_t, rv_b = arena_tm(A_RV)
                    gs_t, gs_b = arena_tm(A_GS)

                    def rv_consume(tb, pst, psb):
                        if tb % 2 == 0:
                            act(rv_t[:, tb, :], pst[:, :], AF.Copy, [psb], [rv_b[tb]])
                        else:
                            cp("dve", rv_t[:, tb, :], pst[:, :], [psb], [rv_b[tb]])

                    linear_tm("in", l, win, 4096, rv_consume)

                    def rg_consume(tb, pst, psb):
                        t1, t1b = ftile()
                        act(t1[:, 0:512], pst[:, :], AF.Silu, [psb], [t1b])
                        tt("pool", gs_t[:, tb, :], t1[:, 0:512], rgain[:], ALU.mult, [t1b, RGB], [gs_b[tb]])

                    linear_tm("in", l, win, 4608, rg_consume)
                    cut(7)

                    for c in (range(4) if 'ret' not in DISABLE else []):
                        cs = slice(c * 128, (c + 1) * 128)
                        S, Sb = rbank()
                        for hr in range(4):
                            mm(S[:, hr * 128:(hr + 1) * 128], arena[:, A_RK + hr, cs], arena[:, A_RQ + hr, cs], True, True,
                               [AB[A_RK + hr], AB[A_RQ + hr]], [Sb])
                        sm, smb = btile()
                        tt("dve", sm[:, :], S[:, :], cf[:, CF_DECAY:CF_DECAY + 512], ALU.mult, [Sb, CFB], [smb])
                        qd, qdb = btile()
                        for hr in range(4):
                            hs = slice(hr * 128, (hr + 1) * 128)
                            tt("pool", qd[:, hs], arena[:, A_RQ + hr, cs], cf[:, CF_QDEC + hr * 128:CF_QDEC + (hr + 1) * 128],
                               ALU.mult, [AB[A_RQ + hr], CFB], [qdb])
                        O, Ob = rbank()
                        for hr in range(4):
                            hs = slice(hr * 128, (hr + 1) * 128)
                            mm(O[:, hs], sm[:, hs], rv_t[:, c, hs], True, False, [smb, rv_b[c]], [Ob])
                            mm(O[:, hs], qd[:, hs], state_bf[:, hr, :], False, True, [qdb, STBB], [Ob])
                        for hr in range(4):
                            hs = slice(hr * 128, (hr + 1) * 128)
                            P.op("dve", lambda e, hr=hr, hs=hs, O=O: e.bn_stats(out=small[:, hr * 6:hr * 6 + 6], in_=O[:, hs]), reads=[Ob], writes=[SMB])
                        for hr in range(4):
                            P.op("dve", lambda e, hr=hr: e.bn_aggr(out=small[:, 24 + hr * 2:26 + hr * 2], in_=small[:, hr * 6:hr * 6 + 6]), reads=[SMB], writes=[SMB])
                        for hr in range(4):
                            act(small[:, 32 + hr:33 + hr], small[:, 25 + hr * 2:26 + hr * 2], AF.Ln, [SMB], [SMB], bias=EPS)
                        act(small[:, 32:36], small[:, 32:36], AF.Exp, [SMB], [SMB], scale=-0.5)
                        on, onb = ftile()
                        for hr in range(4):
                            hs = slice(hr * 128, (hr + 1) * 128)
                            ts("dve", on[:, hs], O[:, hs], small[:, 24 + hr * 2:25 + hr * 2], small[:, 32 + hr:33 + hr],
                               ALU.subtract, ALU.mult, [Ob, SMB], [onb])
                        rt_, rtb = btile()
                        tt("pool", rt_[:, :], on[:, 0:512], gs_t[:, c, :], ALU.mult, [onb, gs_b[c]], [rtb])
                        tp, tpb = tbank()
                        for hr in range(4):
                            hs = slice(hr * 128, (hr + 1) * 128)
                            tr(tp[:, hs], rt_[:, hs], [rtb], [tpb])
                        P.op("dve", lambda e, tp=tp, cs=cs: e.tensor_copy(out=brT[:, BR_RET:BR_RET + 4, cs],
                                                                       in_=tp.rearrange("p (h n) -> p h n", n=128)),
                             reads=[tpb], writes=BRB[BR_RET:BR_RET + 4])
                        tp2, tp2b = tbank()
                        for hr in range(4):
                            hs = slice(hr * 128, (hr + 1) * 128)
                            tr(tp2[:, hs], arena[:, A_RK + hr, cs], [AB[A_RK + hr]], [tp2b])
                        kdk, kdkb = btile()
                        for hr in range(4):
                            hs = slice(hr * 128, (hr + 1) * 128)
                            ts("dve", kdk[:, hs], tp2[:, hs], cf[:, CF_KDEC + hr:CF_KDEC + hr + 1], None, ALU.mult, None, [tp2b, CFB], [kdkb])
                        KV, KVb = rbank()
                        for hr in range(4):
                            hs = slice(hr * 128, (hr + 1) * 128)
                            mm(KV[:, hs], kdk[:, hs], rv_t[:, c, hs], True, True, [kdkb, rv_b[c]], [KVb])
                        for hr in range(4):
                            hs = slice(hr * 128, (hr + 1) * 128)
                            stt(state[:, hr, :], state[:, hr, :], cd[hr], KV[:, hs], ALU.mult, ALU.add, [STB, KVb], [STB])
                        cp("pool", state_bf[:], state[:], [STB], [STBB])

                    nblk = 4 * g + 4
                    for hp in (range(4) if 'sb' not in DISABLE else []):
                        Bt = [PB[B_BANKS[0]], PB[B_BANKS[1]]]
                        OUT, OUTb = PB[OUT_BANK]
                        for hh in range(2):
                            mm(Bt[hh][0][:, :], cb[:, CB_ONES:CB_ONES + 128], zeros[:], True, False, [CBB, ZB], [Bt[hh][1]])
                        mm(OUT[:, :], cb[:, CB_ONES:CB_ONES + 128], zeros[:], True, False, [CBB, ZB], [OUTb])
                        items = [(j, hh) for j in reversed(range(nblk)) for hh in range(2)]
                        n_it = len(items)
                        ctx = [None] * n_it

                        def S1(s):
                            j, hh = items[s]
                            jj = j - 4 * g
                            c0 = max(jj, 0) * 128
                            po = hh * 64
                            kT = kc[po:po + 64, hp, j * 128:(j + 1) * 128]
                            kb = KCB[hp][j // 4]
                            Z, Zb = rbank()
                            mm(Z[:, c0:512], kT, arena[po:po + 64, A_SQ + hp, c0:512], True, True, [kb, AB[A_SQ + hp]], [Zb])
                            e_, eb = ftile()
                            act(e_[:, c0:512], Z[:, c0:512], AF.Exp, [Zb], [eb])
                            l_, lb = btile()
                            act(l_[:, c0:512], e_[:, c0:512], AF.Ln, [eb], [lb], bias=1.0)
                            if jj >= 0:
                                tt("pool", l_[:, c0:c0 + 128], l_[:, c0:c0 + 128], mask_bf, ALU.mult, [lb, CBB], [lb])
                            ctx[s] = dict(j=j, hh=hh, jj=jj, c0=c0, po=po, kT=kT, kb=kb, l=l_, lb=lb)

                        def S2(s):
                            c = ctx[s]
                            B_, Bb = Bt[c["hh"]]
                            c0 = c["c0"]
                            po = c["po"]
                            mm(B_[:, c0:512], lincl, c["l"][:, c0:512], False, False, [CBB, c["lb"]], [Bb])
                            mm(B_[:, c0:512], c["kT"], arena[po:po + 64, A_SQN + hp, c0:512], False, False, [c["kb"], AB[A_SQN + hp]], [Bb])
                            a_, ab_ = btile()
                            act(a_[:, c0:512], B_[:, c0:512], AF.Exp, [Bb], [ab_], scale=-1.0)
                            if c["jj"] >= 0:
                                tt("pool", a_[:, c0:c0 + 128], a_[:, c0:c0 + 128], mask_bf, ALU.mult, [ab_, CBB], [ab_])
                            c["a"] = a_
                            c["ab"] = ab_

                        def S3(s):
                            c = ctx[s]
                            B_, Bb = Bt[c["hh"]]
                            c0 = c["c0"]
                            po = c["po"]
                            j = c["j"]
                            mm(B_[:, c0:512], lcomp, c["l"][:, c0:512], False, False, [CBB, c["lb"]], [Bb])
                            mm(B_[:, c0:512], c["kT"], arena[po:po + 64, A_SQ + hp, c0:512], False, False, [c["kb"], AB[A_SQ + hp]], [Bb])
                            h8 = 2 * hp + c["hh"]
                            mm(OUT[po:po + 64, c0:512], vc[:, j, h8 * 64:(h8 + 1) * 64], c["a"][:, c0:512], False, j == 0,
                               [VCB[j], c["ab"]], [OUTb])
                            ctx[s] = None

                        for s in range(n_it + 2):
                            if s < n_it:
                                S1(s)
                            if 0 <= s - 1 < n_it:
                                S2(s - 1)
                            if 0 <= s - 2 < n_it:
                                S3(s - 2)
                        act(brT[:, BR_SB + hp, :], OUT[:, :], AF.Copy, [OUTb], [BRB[BR_SB + hp]])

                    for dg in (range(2) if 'merge' not in DISABLE else []):
                        for n in range(3):
                            def gate_consume(ci, pst, psb, n=n):
                                bc = pv_off(l, "bgate") + n * 8 + dg * 4 + ci
                                act(arena[:, A_G + n * 4 + ci, :], pst[:, :], AF.Sigmoid, [psb, PVB], [AB[A_G + n * 4 + ci]],
                                    bias=pv[:, bc:bc + 1])
                            linear_fm("in", l, win, 5120 + n * 1024 + dg * 512, 4, 8, h_rhs, gate_consume)
                        mt = [(marena[:, i * 512:(i + 1) * 512], None) for i in range(4)]
                        for n in range(3):
                            def up_consume(ci, pst, psb, n=n):
                                m_ = mt[ci][0]
                                mbs = [AB[20 + 2 * ci], AB[21 + 2 * ci]]
                                gsl = arena[:, A_G + n * 4 + ci, :]
                                gb = AB[A_G + n * 4 + ci]
                                if n == 0:
                                    tt("dve", m_[:, 0:512], pst[:, :], gsl, ALU.mult, [psb, gb], mbs)
                                else:
                                    t1, t1b = ftile()
                                    tt("dve", t1[:, 0:512], pst[:, :], gsl, ALU.mult, [psb, gb], [t1b])
                                    if n == 1:
                                        tt("pool", m_[:, 0:512], m_[:, 0:512], t1[:, 0:512], ALU.add, mbs + [t1b], mbs)
                                    else:
                                        d = dg * 4 + ci
                                        tt("pool", arena[:, A_MG + d, :], m_[:, 0:512], t1[:, 0:512], ALU.add, mbs + [t1b], [AB[A_MG + d]])
                            br_rhs = lambda k, n=n: (brT[:, n * 4 + k, :], BRB[n * 4 + k])
                            linear_fm("br", l, wb_br[l, n], dg * 512, 4, 4, br_rhs, up_consume)

                    def resid_consume(ci, pst, psb):
                        tt("dve", xt[:, ci, :], pst[:, :], xt[:, ci, :], ALU.add, [psb, XB[ci]], [XB[ci]])

                    mg_rhs = lambda k: (arena[:, A_MG + k, :], AB[A_MG + k])
                    if "merge" not in DISABLE:
                        linear_fm("o", l, wb_o[l], 0, 8, 8, mg_rhs, resid_consume)

                    if 'xattn' not in DISABLE:
                        rmsnorm(xt, XB, TG, pv_off(l, "nxa"), ht, HB)

                        def q2_consume(ci, pst, psb):
                            act(arena[:, A_Q2 + ci, :], pst[:, :], AF.Copy, [psb], [AB[A_Q2 + ci]], scale=1.0 / 16.0)

                        linear_fm("xq", l, wb_xq[l], 0, 8, 8, h_rhs, q2_consume)
                        for hx in range(4):
                            pts = []
                            for mb in range(2):
                                SC, SCb = rbank()
                                for kk in range(2):
                                    mm(SC[:, :], kmem[:, 2 * hx + kk, mb * 128:(mb + 1) * 128], arena[:, A_Q2 + 2 * hx + kk, :], kk == 0, kk == 1,
                                       [KMB, AB[A_Q2 + 2 * hx + kk]], [SCb])
                                p_, pb_ = btile()
                                act(p_[:, :], SC[:, :], AF.Exp, [SCb], [pb_])
                                pts.append((p_, pb_))
                            DEN, DENb = rbank()
                            for mb in range(2):
                                mm(DEN[:, :], ones_bf, pts[mb][0][:, :], mb == 0, mb == 1, [CBB, pts[mb][1]], [DENb])
                            rd_, rdb = ftile()
                            P.op("dve", lambda e, rd_=rd_, DEN=DEN: e.reciprocal(out=rd_[:, 0:512], in_=DEN[:, :]), reads=[DENb], writes=[rdb])
                            for kk in range(2):
                                OT, OTb = rbank()
                                dch = 2 * hx + kk
                                for mb in range(2):
                                    mm(OT[:, :], vmem[:, mb, dch * 128:(dch + 1) * 128], pts[mb][0][:, :], mb == 0, mb == 1, [VMB, pts[mb][1]], [OTb])
                                tt("dve", arena[:, A_OT + dch, :], OT[:, :], rd_[:, 0:512], ALU.mult, [OTb, rdb], [AB[A_OT + dch]])
                        ot_rhs = lambda k: (arena[:, A_OT + k, :], AB[A_OT + k])
                        linear_fm("xo", l, wb_xo[l], 0, 8, 8, ot_rhs, resid_consume)

                    if 'mlp' not in DISABLE:
                        rmsnorm(xt, XB, TG, pv_off(l, "nmlp"), ht, HB)

                        def up_mlp_consume(ci, pst, psb):
                            t1, t1b = ftile()
                            act(t1[:, 0:512], pst[:, :], AF.Relu, [psb], [t1b])
                            tt("pool", arena[:, ci, :], t1[:, 0:512], t1[:, 0:512], ALU.mult, [t1b], [AB[ci]])

                        linear_fm("up", l, wb_up[l], 0, 32, 8, h_rhs, up_mlp_consume)
                        for d in range(8):
                            slot, wsb = wslot()
                            view = slot[:, :].rearrange("p (c n) -> p c n", n=128)
                            load("sp", slot[:, :], wb_dn[l, d], wsb, R=[WB[("dn", l)]])
                            pst, psb = rbank()
                            for k in range(32):
                                mm(pst[:, :], view[:, k, :], arena[:, k, :], k == 0, k == 31, [wsb, AB[k]], [psb])
                            resid_consume(d, pst, psb)

                    if l == L - 1:
                        rmsnorm_final(P, xt, XB, sq, SQB, rstd, RSB, pv, PVB, L * PV_L, rbank, mm, act, stt, tt, ones_bf, CBB,
                                      ftile, out_d, t0, load)
                    else:
                        P.dma("pool", lambda e, xt=xt, t0=t0: [e.dma_start(out=xs_d.rearrange("(c p) n -> p c n", p=128)[:, :, t0:t0 + TG], in_=xt[:])],
                              XB[0], reads=XB, writes=[XSD[g]])


        except StopBuild:
            pass
        P.op("sp", lambda e: None, reads=OUTD)
        P.generate(block)
        build.stats = (P.stats, P.sbytes)
    return nc


OUTD = []


def rmsnorm_final(P, xt, XB, sq, SQB, rstd, RSB, pv, PVB, gcol, rbank, mm, act, stt, tt, ones_bf, CBB, ftile, out_d, t0, load):
    for c in range(8):
        tt("pool", sq[:, c, :], xt[:, c, :], xt[:, c, :], ALU.mult, [XB[c]], [SQB[c]])
    pst, psb = rbank()
    for c in range(8):
        mm(pst[:, :], ones_bf, sq[:, c, :], c == 0, c == 7, [SQB[c], CBB], [psb])
    act(rstd[:, :], pst[:, :], AF.Ln, [psb], [RSB], scale=1.0 / 1024.0, bias=EPS)
    act(rstd[:, :], rstd[:, :], AF.Exp, [RSB], [RSB], scale=-0.5)
    for c in range(8):
        o, ob = ftile()
        stt(o[:, 0:512], xt[:, c, :], pv[:, gcol + c:gcol + c + 1], rstd[:, :], ALU.mult, ALU.mult, [XB[c], PVB, RSB], [ob])
        od = P.buf("outd")
        P.dma("pool", lambda e, o=o, c=c: [e.dma_start(out=out_d[c * 128:(c + 1) * 128, t0:t0 + 512], in_=o[:, 0:512])],
              ob, reads=[ob], writes=[od])
        OUTD.append(od)


_CACHE = {}


def run(inputs, T, L=2, n_cores=8, active=ACTIVE_CORES):
    f32 = np.float32
    x = np.asarray(inputs["x"], f32)
    mem = np.asarray(inputs["mem"], f32)
    B = x.shape[0]
    cf, cb, rot, cd = host_constants(T)
    key = (T, L)
    OUTD.clear()
    nc = build(T, L, cd)
    pvec = host_pvec(inputs, L)
    rgain = np.ascontiguousarray(np.broadcast_to(np.asarray(inputs["ret_norm_g"], f32)[:, None, :], (L, 128, 512)))
    common = {
        "pv": pvec, "cf": cf, "cb": cb, "rot": rot, "rgain": rgain,
        "w_in": np.asarray(inputs["w_in"], f32), "w_branch": np.asarray(inputs["w_branch"], f32),
        "w_o": np.asarray(inputs["w_o"], f32), "w_xq": np.asarray(inputs["w_xq"], f32),
        "w_xkv": np.asarray(inputs["w_xkv"], f32), "w_xo": np.asarray(inputs["w_xo"], f32),
        "w_up": np.asarray(inputs["w_up"], f32), "w_down": np.asarray(inputs["w_down"], f32),
    }
    in_maps = []
    zx = np.zeros((1024, T), f32)
    zm = np.zeros((1024, MEM_LEN), f32)
    slot = {c: i for i, c in enumerate(active)}
    for c in range(n_cores):
        m = dict(common)
        if c in slot and slot[c] < B:
            b = slot[c]
            m["xT"] = np.ascontiguousarray(x[b].T)
            m["memT"] = np.ascontiguousarray(mem[b].T)
        else:
            m["xT"] = zx
            m["memT"] = zm
        in_maps.append(m)
    res = run_bass_kernel_spmd(nc, in_maps, core_ids=list(range(n_cores)))
    out = np.empty((B, T, 1024), f32)
    for c, b in slot.items():
        if b < B:
            out[b] = np.asarray(res.results[c]["outT"], f32).T
    return out


def kernel(**inputs):
    T = inputs["x"].shape[1]
    return run(inputs, T)
```

```python
import numpy as np
import ml_dtypes
import concourse.bass as bass
import concourse.mybir as mybir
from concourse.bass_utils import run_bass_kernel_spmd
from contextlib import ExitStack

F32 = mybir.dt.float32
BF16 = mybir.dt.bfloat16
ALU = mybir.AluOpType
AF = mybir.ActivationFunctionType

D_MODEL = 1024
MEM_LEN = 256
IN_COLS = 8192
D_FF = 4096
EPS = 1e-6
TG = 512
ACTIVE_CORES = (0, 1, 4, 5)
DISABLE = set()
CUT = 0


class StopBuild(Exception):
    pass


def cut(n):
    if CUT == n:
        raise StopBuild()


class Buf:
    __slots__ = ("name", "lw", "rd", "sem", "dcount", "last_dma")

    def __init__(self, name):
        self.name = name
        self.lw = None
        self.rd = []
        self.sem = None
        self.dcount = 0
        self.last_dma = None


class Inst:
    __slots__ = ("eng", "fn", "hard", "soft", "sig", "is_dma", "owner", "dval", "semval")

    def __init__(self, eng, fn, is_dma=False):
        self.eng = eng
        self.fn = fn
        self.hard = []
        self.soft = []
        self.sig = False
        self.is_dma = is_dma
        self.owner = None
        self.dval = 0
        self.semval = 0


class Prog:
    ENGS = ("pe", "act", "dve", "pool", "sp")

    def __init__(self, nc, es):
        self.nc = nc
        self.es = es
        self.q = {e: [] for e in self.ENGS}
        self.uid = 0
        self.sbytes = 0

    def sb(self, shape, dtype, name=None):
        self.uid += 1
        n = 1
        for s in shape[1:]:
            n *= s
        self.sbytes += n * (2 if dtype == BF16 else 4)
        return self.es.enter_context(self.nc.sbuf_tensor(name or f"sb{self.uid}", list(shape), dtype))

    def ps(self, shape, dtype=F32, name=None):
        self.uid += 1
        return self.es.enter_context(self.nc.psum_tensor(name or f"ps{self.uid}", list(shape), dtype))

    def buf(self, name=None):
        self.uid += 1
        return Buf(name or f"b{self.uid}")

    def new_sem(self, name):
        return self.es.enter_context(self.nc.semaphore(name))

    def _track(self, inst, reads, writes):
        wset = set(id(b) for b in writes)
        for b in reads:
            if b.lw is not None:
                inst.hard.append(b.lw)
        for b in writes:
            if b.lw is not None:
                inst.hard.append(b.lw)
            inst.soft.extend(b.rd)
        for b in reads:
            if id(b) not in wset:
                b.rd.append(inst)
        for b in writes:
            b.lw = inst
            b.rd = []

    def op(self, eng, fn, reads=(), writes=()):
        inst = Inst(eng, fn)
        self._track(inst, reads, writes)
        self.q[eng].append(inst)
        return inst

    def dma(self, queue, fn, owner, reads=(), writes=(), npieces=1):
        inst = Inst(queue, fn, is_dma=True)
        self._track(inst, reads, writes)
        if owner.last_dma is not None:
            inst.hard.append(owner.last_dma)
        owner.last_dma = inst
        if owner.sem is None:
            self.uid += 1
            owner.sem = self.new_sem(f"dsem{self.uid}")
        owner.dcount += 16 * npieces
        inst.owner = owner
        inst.dval = owner.dcount
        self.q[queue].append(inst)
        return inst

    def generate(self, block):
        esem = {e: self.new_sem(f"esem_{e}") for e in ("pe", "act", "dve", "pool")}
        for e in self.ENGS:
            for inst in self.q[e]:
                deps = []
                for p in inst.hard:
                    if p is inst:
                        continue
                    if p.is_dma or inst.is_dma or p.eng != inst.eng or inst.eng != "pe":
                        deps.append(p)
                for p in inst.soft:
                    if p is inst:
                        continue
                    if p.is_dma or inst.is_dma or p.eng != inst.eng:
                        deps.append(p)
                inst.hard = deps
                inst.soft = None
                for p in deps:
                    if not p.is_dma:
                        p.sig = True
        for e in ("pe", "act", "dve", "pool"):
            c = 0
            for inst in self.q[e]:
                if inst.is_dma:
                    continue
                if inst.sig:
                    c += 1
                    inst.semval = c
        self.stats = {}

        def gen(e, eng):
            waited = {}
            nw = 0
            for inst in self.q[e]:
                need = {}
                for p in inst.hard:
                    if p.is_dma:
                        s, v = p.owner.sem, p.dval
                    else:
                        s, v = esem[p.eng], p.semval
                    k = id(s)
                    if waited.get(k, 0) >= v:
                        continue
                    if k not in need or need[k][1] < v:
                        need[k] = (s, v)
                for k, (s, v) in need.items():
                    eng.wait_ge(s, v)
                    waited[k] = v
                    nw += 1
                r = inst.fn(eng)
                if inst.is_dma:
                    for h in r:
                        h.then_inc(inst.owner.sem, 16)
                elif inst.sig:
                    r.then_inc(esem[e], 1)
            self.stats[e] = (len(self.q[e]), nw)

        @block.sync
        def _(eng):
            gen("sp", eng)

        @block.scalar
        def _(eng):
            gen("act", eng)

        @block.vector
        def _(eng):
            gen("dve", eng)

        @block.gpsimd
        def _(eng):
            gen("pool", eng)

        @block.tensor
        def _(eng):
            gen("pe", eng)


PV_L = 8 * 4 + 24 + 12 + 4


def pv_off(l, name):
    base = l * PV_L
    offs = {"nmix": 0, "nxa": 8, "nmem": 16, "nmlp": 24, "bgate": 32, "convw": 56, "convb": 68}
    return base + offs[name]


def host_constants(T):
    f32 = np.float32
    idx = np.arange(128, dtype=f32)
    log_gamma = np.log1p(-np.exp2(-5.0 - np.arange(4, dtype=f32))).astype(f32)
    j = idx[:, None]
    i = idx[None, :]
    decayT = np.zeros((128, 4, 128), f32)
    for h in range(4):
        rel = i - j
        decayT[:, h, :] = np.where(rel >= 0, np.exp(np.maximum(rel, 0.0) * log_gamma[h]), 0.0)
    ks = f32(128.0 ** -0.5)
    decayT = (decayT * ks).astype(f32)
    qdec = np.zeros((128, 4, 128), f32)
    for h in range(4):
        qdec[:, h, :] = np.exp((idx + 1) * log_gamma[h])[None, :]
    mask = (j < i).astype(f32)
    kdecay = (np.exp((127 - idx)[:, None] * log_gamma[None, :]) * ks).astype(f32)
    cf = np.concatenate([decayT.reshape(128, 512), qdec.reshape(128, 512), mask, kdecay], axis=1).astype(f32)
    cd = [float(np.exp(f32(128.0) * log_gamma[h])) for h in range(4)]
    ident = np.eye(128, dtype=f32)
    ones = np.ones((128, 128), f32)
    lincl = (j >= i).astype(f32)
    lc = 1.0 - lincl
    cb = np.concatenate([ident, ones, lincl, lc, mask], axis=1).astype(ml_dtypes.bfloat16)
    pos = np.arange(T, dtype=f32)
    inv_freq = (f32(10000.0) ** (-np.arange(0, 128, 2, dtype=f32) / f32(128))).astype(f32)
    ang = (pos[:, None] * inv_freq[None, :]).astype(f32)
    cos = np.cos(ang).astype(f32).T
    sin = np.sin(ang).astype(f32).T
    cosf = np.concatenate([cos, cos], 0)
    sinf = np.concatenate([sin, -sin], 0)
    rot = np.stack([cosf, sinf], 0).astype(f32)
    return cf, cb, rot, cd


CF_DECAY, CF_QDEC, CF_MASK, CF_KDEC, CF_N = 0, 512, 1024, 1024 + 128, 1024 + 128 + 4
CB_ID, CB_ONES, CB_LINCL, CB_LC, CB_MASK, CB_N = 0, 128, 256, 384, 512, 640


def col8(v):
    return np.ascontiguousarray(v.reshape(-1, 128).T)


def host_pvec(inp, L):
    cols = []
    for l in range(L):
        cols += [col8(inp["norm_mix_g"][l]), col8(inp["norm_xa_g"][l]), col8(inp["norm_mem_g"][l]),
                 col8(inp["norm_mlp_g"][l])]
        cols.append(np.concatenate([col8(inp["b_gate"][l][n]) for n in range(3)], axis=1))
        cols.append(np.concatenate([col8(inp["conv_w"][l][jj]) for jj in range(3)], axis=1))
        cols.append(col8(inp["conv_b"][l]))
    cols.append(col8(inp["final_g"]))
    return np.ascontiguousarray(np.concatenate(cols, axis=1).astype(np.float32))


def build(T, L, cd, dbg=False):
    NG = T // TG
    NB = T // 128
    nc = bass.Bass("TRN2", target_bir_lowering=False)

    def din(name, shape, dt=F32):
        return nc.dram_tensor(name, list(shape), dt, kind="ExternalInput").ap()

    xT_d = din("xT", [1024, T])
    memT_d = din("memT", [1024, MEM_LEN])
    pv_d = din("pv", [128, L * PV_L + 8])
    cf_d = din("cf", [128, CF_N])
    cb_d = din("cb", [128, CB_N], BF16)
    rot_d = din("rot", [2, 128, T])
    rgain_d = din("rgain", [L, 128, 512])
    w_in_d = din("w_in", [L, 1024, IN_COLS])
    w_br_d = din("w_branch", [L, 3, 512, 1024])
    w_o_d = din("w_o", [L, 1024, 1024])
    w_xq_d = din("w_xq", [L, 1024, 1024])
    w_xkv_d = din("w_xkv", [L, 1024, 2048])
    w_xo_d = din("w_xo", [L, 1024, 1024])
    w_up_d = din("w_up", [L, 1024, D_FF])
    w_dn_d = din("w_down", [L, D_FF, 1024])
    out_d = nc.dram_tensor("outT", [1024, T], F32, kind="ExternalOutput").ap()

    def dint(name, shape, dt=BF16):
        return nc.dram_tensor(name, list(shape), dt).ap()

    wb_in = dint("wb_in", [L, 1024, IN_COLS])
    wb_br = dint("wb_br", [L, 3, 512, 1024])
    wb_o = dint("wb_o", [L, 1024, 1024])
    wb_xq = dint("wb_xq", [L, 1024, 1024])
    wb_xkv = dint("wb_xkv", [L, 1024, 2048])
    wb_xo = dint("wb_xo", [L, 1024, 1024])
    wb_up = dint("wb_up", [L, 1024, D_FF])
    wb_dn = dint("wb_dn", [L, 8, 128, 32 * 128])
    xs_d = dint("xs", [1024, T], F32)

    with ExitStack() as es:
        P = Prog(nc, es)
        block = es.enter_context(nc.Block())

        pv = P.sb([128, L * PV_L + 8], F32); PVB = P.buf("pv")
        cf = P.sb([128, CF_N], F32); CFB = P.buf("cf")
        cb = P.sb([128, CB_N], BF16); CBB = P.buf("cb")
        zeros = P.sb([128, 512], BF16); ZB = P.buf("zeros")
        rgain = P.sb([128, 512], F32); RGB = P.buf("rgain")
        NW = 4
        wslots = [(P.sb([128, 4096], BF16), P.buf(f"ws{i}")) for i in range(NW)]
        xts = [(P.sb([128, 8, 512], F32), [P.buf(f"x{i}_{c}") for c in range(8)]) for i in range(1)]
        ht = P.sb([128, 8, 512], BF16); HB = [P.buf(f"h{c}") for c in range(8)]
        rstd = P.sb([128, 512], F32); RSB = P.buf("rstd")
        NA = 32
        arena = P.sb([128, NA, 512], BF16); AB = [P.buf(f"ar{i}") for i in range(NA)]
        sq = arena[:, 0:8, :]; SQB = AB[0:8]
        marena = arena[:, 20:28, :].rearrange("p a n -> p (a n)").bitcast(F32)
        brT = P.sb([128, 12, 512], BF16); BRB = [P.buf(f"br{i}") for i in range(12)]
        kc = P.sb([128, 4, T], BF16); KCB = [[P.buf(f"kc{c}_{g}") for g in range(NG)] for c in range(4)]
        vc = P.sb([128, NB, 512], BF16); VCB = [P.buf(f"vc{b}") for b in range(NB)]
        rot_t = P.sb([128, 2, 512], F32); ROTB = P.buf("rot")
        kmem = P.sb([128, 8, 256], BF16); KMB = P.buf("kmem")
        vmem = P.sb([128, 2, 1024], BF16); VMB = P.buf("vmem")
        state = P.sb([128, 4, 128], F32); STB = P.buf("state")
        state_bf = P.sb([128, 4, 128], BF16); STBB = P.buf("statebf")
        halo = P.sb([128, 4, 2], F32); HALB = [P.buf(f"halo{c}") for c in range(4)]
        NF = 5
        ftiles = [(P.sb([128, 514], F32), P.buf(f"ft{i}")) for i in range(NF)]
        NBT = 7
        btiles = [(P.sb([128, 512], BF16), P.buf(f"bt{i}")) for i in range(NBT)]
        small = P.sb([128, 64], F32); SMB = P.buf("small")

        PB = [(P.ps([128, 512], F32), P.buf(f"pb{i}")) for i in range(7)]
        tb_ps = P.ps([128, 1024], BF16); _tb = P.buf("tb"); TBB = [_tb, _tb]
        ROT_BANKS = [0, 1, 2]
        B_BANKS = [3, 4]
        OUT_BANKS = [5, 6]
        st = {"rb": 0, "ws": 0, "ft": 0, "bt": 0, "tb": 0}

        def rbank():
            i = ROT_BANKS[st["rb"] % len(ROT_BANKS)]
            st["rb"] += 1
            return PB[i]

        def wslot():
            s = wslots[st["ws"] % NW]
            st["ws"] += 1
            return s

        def ftile():
            s = ftiles[st["ft"] % NF]
            st["ft"] += 1
            return s

        def btile():
            s = btiles[st["bt"] % NBT]
            st["bt"] += 1
            return s

        def tbank():
            i = st["tb"] % 2
            st["tb"] += 1
            return tb_ps[:, i * 512:(i + 1) * 512], TBB[i]

        ident = cb[:, CB_ID:CB_ID + 128]
        ones_bf = cb[:, CB_ONES:CB_ONES + 128]
        lincl = cb[:, CB_LINCL:CB_LINCL + 128]
        lcomp = cb[:, CB_LC:CB_LC + 128]
        mask_bf = cb[:, CB_MASK:CB_MASK + 128]

        def mm(out, lhsT, rhs, start, stop, R, Wb):
            P.op("pe", lambda e: e.matmul(out, lhsT=lhsT, rhs=rhs, start=start, stop=stop), reads=R, writes=Wb)

        def tr(out, in_, R, Wb):
            P.op("pe", lambda e: e.transpose(out, in_, ident), reads=R + [CBB], writes=Wb)

        def act(out, in_, func, R, Wb, scale=None, bias=None):
            kw = {}
            if scale is not None:
                kw["scale"] = scale
            if bias is not None:
                kw["bias"] = bias
            P.op("act", lambda e: e.activation(out=out, in_=in_, func=func, **kw), reads=R, writes=Wb)

        def tt(eng, out, in0, in1, op, R, Wb):
            P.op(eng, lambda e: e.tensor_tensor(out=out, in0=in0, in1=in1, op=op), reads=R, writes=Wb)

        def ts(eng, out, in0, s1, s2, op0, op1, R, Wb):
            if op1 is None:
                P.op(eng, lambda e: e.tensor_scalar(out=out, in0=in0, scalar1=s1, scalar2=None, op0=op0), reads=R, writes=Wb)
            else:
                P.op(eng, lambda e: e.tensor_scalar(out=out, in0=in0, scalar1=s1, scalar2=s2, op0=op0, op1=op1), reads=R, writes=Wb)

        def stt(out, in0, scalar, in1, op0, op1, R, Wb):
            P.op("dve", lambda e: e.scalar_tensor_tensor(out=out, in0=in0, scalar=scalar, in1=in1, op0=op0, op1=op1), reads=R, writes=Wb)

        def cp(eng, out, in_, R, Wb):
            P.op(eng, lambda e: e.tensor_copy(out=out, in_=in_), reads=R, writes=Wb)

        def load(queue, out, in_, owner, R=(), Wb=None):
            P.dma(queue, lambda e: [e.dma_start(out=out, in_=in_)], owner, reads=list(R), writes=[owner] if Wb is None else Wb)

        load("sp", pv[:], pv_d, PVB)
        load("sp", cf[:], cf_d, CFB)
        load("sp", cb[:], cb_d, CBB)
        P.op("dve", lambda e: e.memset(zeros[:], 0.0), writes=[ZB])

        WB = {}

        PCB = P.buf("precast_chain")

        def precast(name, l, pieces):
            b = P.buf(f"wb_{name}{l}")
            for (o, i) in pieces:
                P.dma("pool", lambda e, o=o, i=i: [e.dma_start(out=o, in_=i)], PCB, writes=[b])
            WB[(name, l)] = b

        for l in range(L):
            precast("in", l, [(wb_in[l, r * 128:(r + 1) * 128, :], w_in_d[l, r * 128:(r + 1) * 128, :]) for r in range(8)])
            precast("xkv", l, [(wb_xkv[l, r * 256:(r + 1) * 256, :], w_xkv_d[l, r * 256:(r + 1) * 256, :]) for r in range(4)])
            precast("br", l, [(wb_br[l, n], w_br_d[l, n]) for n in range(3)])
            precast("o", l, [(wb_o[l, r * 512:(r + 1) * 512, :], w_o_d[l, r * 512:(r + 1) * 512, :]) for r in range(2)])
            precast("xq", l, [(wb_xq[l, r * 512:(r + 1) * 512, :], w_xq_d[l, r * 512:(r + 1) * 512, :]) for r in range(2)])
            precast("xo", l, [(wb_xo[l, r * 512:(r + 1) * 512, :], w_xo_d[l, r * 512:(r + 1) * 512, :]) for r in range(2)])
            precast("up", l, [(wb_up[l, r * 256:(r + 1) * 256, :], w_up_d[l, r * 256:(r + 1) * 256, :]) for r in range(4)])
            precast("dn", l, [(wb_dn[l, d].rearrange("p (c n) -> p c n", n=128),
                               w_dn_d[l, :, d * 128:(d + 1) * 128].rearrange("(c p) n -> p c n", p=128)) for d in range(8)])

        def wload_cols(name, l, src2d, c0, ncols, kchunks):
            slot, sb_ = wslot()
            view = slot[:, 0:kchunks * ncols].rearrange("p (c n) -> p c n", n=ncols)
            src = src2d.rearrange("(c p) n -> p c n", p=128)[:, :, c0:c0 + ncols]
            load("sp", view, src, sb_, R=[WB[(name, l)]])
            return view, sb_

        def rmsnorm(xt, XB, ncol, gcol, out_t, OB, out_is_f32=False):
            for c in range(8):
                tt("pool", sq[:, c, 0:ncol], xt[:, c, :], xt[:, c, :], ALU.mult, [XB[c]], [SQB[c]])
            pst, psb = rbank()
            for c in range(8):
                mm(pst[:, 0:ncol], ones_bf, sq[:, c, 0:ncol], c == 0, c == 7, [SQB[c], CBB], [psb])
            act(rstd[:, 0:ncol], pst[:, 0:ncol], AF.Ln, [psb], [RSB], scale=1.0 / 1024.0, bias=EPS)
            act(rstd[:, 0:ncol], rstd[:, 0:ncol], AF.Exp, [RSB], [RSB], scale=-0.5)
            for c in range(8):
                stt(out_t[:, c, :], xt[:, c, :], pv[:, gcol + c:gcol + c + 1], rstd[:, 0:ncol], ALU.mult, ALU.mult,
                    [XB[c], PVB, RSB], [OB[c]])

        def linear_fm(name, l, src2d, col0, nchunks, kchunks, rhs_fn, consume):
            done = 0
            while done < nchunks:
                n = min(4, nchunks - done)
                view, wsb = wload_cols(name, l, src2d, col0 + done * 128, n * 128, kchunks)
                for cc in range(n):
                    pst, psb = rbank()
                    for k in range(kchunks):
                        r_ap, r_b = rhs_fn(k)
                        mm(pst[:, :], view[:, k, cc * 128:(cc + 1) * 128], r_ap, k == 0, k == kchunks - 1, [wsb, r_b], [psb])
                    consume(done + cc, pst, psb)
                done += n

        def linear_tm(name, l, src2d, col0, consume):
            view, wsb = wload_cols(name, l, src2d, col0, 512, 8)
            for tb in range(4):
                pst, psb = rbank()
                for k in range(8):
                    mm(pst[:, :], ht[:, k, tb * 128:(tb + 1) * 128], view[:, k, :], k == 0, k == 7, [wsb, HB[k]], [psb])
                consume(tb, pst, psb)

        h_rhs = lambda k: (ht[:, k, :], HB[k])

        A_CB, A_CC, A_SQ, A_SQN, A_RQ, A_RK = 0, 4, 8, 12, 16, 20
        A_RV, A_GS = 24, 28
        A_G, A_MG = 0, 12
        A_Q2, A_OT = 16, 24
        BR_CONV, BR_SB, BR_RET = 0, 4, 8

        def arena_tm(a0):
            return arena[:, a0:a0 + 4, :], AB[a0:a0 + 4]

        XSD = [P.buf(f"xsd{i}") for i in range(NG)]

        try:
            cut(1)
            for l in range(L):
                load("sp", rgain[:], rgain_d[l], RGB)
                P.op("pool", lambda e: e.memset(state[:], 0.0), writes=[STB])
                P.op("pool", lambda e: e.memset(state_bf[:], 0.0), writes=[STBB])
                P.op("pool", lambda e: e.memset(halo[:], 0.0), writes=HALB)
                xt0, XB0 = xts[0]
                memx = xt0[:, :, 0:MEM_LEN]
                P.dma("sp", lambda e, memx=memx: [e.dma_start(out=memx, in_=memT_d.rearrange("(c p) n -> p c n", p=128))],
                      XB0[0], writes=XB0)
                rmsnorm(memx, XB0, MEM_LEN, pv_off(l, "nmem"), ht[:, :, 0:MEM_LEN], HB)
                def kmem_consume(ci, pst, psb):
                    cp("dve", kmem[:, ci, :], pst[:, 0:MEM_LEN], [psb], [KMB])
                done = 0
                for half in range(2):
                    view, wsb = wload_cols("xkv", l, wb_xkv[l], half * 512, 512, 8)
                    for cc in range(4):
                        pst, psb = rbank()
                        for k in range(8):
                            mm(pst[:, 0:MEM_LEN], view[:, k, cc * 128:(cc + 1) * 128], ht[:, k, 0:MEM_LEN], k == 0, k == 7, [wsb, HB[k]], [psb])
                        kmem_consume(half * 4 + cc, pst, psb)
                for half in range(2):
                    view, wsb = wload_cols("xkv", l, wb_xkv[l], 1024 + half * 512, 512, 8)
                    for mb in range(2):
                        pst, psb = rbank()
                        for k in range(8):
                            mm(pst[:, :], ht[:, k, mb * 128:(mb + 1) * 128], view[:, k, :], k == 0, k == 7, [wsb, HB[k]], [psb])
                        cp("dve", vmem[:, mb, half * 512:(half + 1) * 512], pst[:, :], [psb], [VMB])

                cut(2)
                for g in range(NG):
                    xt, XB = xts[0]
                    t0 = g * TG
                    src = xT_d if l == 0 else xs_d
                    rd = [] if l == 0 else [XSD[g]]
                    P.dma("sp", lambda e, xt=xt, src=src, t0=t0: [e.dma_start(out=xt[:], in_=src.rearrange("(c p) n -> p c n", p=128)[:, :, t0:t0 + TG])],
                          XB[0], reads=rd, writes=XB)
                    load("sp", rot_t[:], rot_d[:, :, t0:t0 + TG].rearrange("f p n -> p f n"), ROTB)

                    rmsnorm(xt, XB, TG, pv_off(l, "nmix"), ht, HB)
                    win = wb_in[l]
                    cut(3)

                    def ev_bf(a0, alt=False):
                        def f(ci, pst, psb):
                            if (ci % 2 == 0) != alt:
                                act(arena[:, a0 + ci, :], pst[:, :], AF.Copy, [psb], [AB[a0 + ci]])
                            else:
                                cp("dve", arena[:, a0 + ci, :], pst[:, :], [psb], [AB[a0 + ci]])
                        return f

                    linear_fm("in", l, win, 0, 4, 8, h_rhs, ev_bf(A_CB))
                    linear_fm("in", l, win, 512, 4, 8, h_rhs, ev_bf(A_CC, True))
                    cut(4)

                    def conv_consume(ci, pst, psb):
                        u, ub = ftile()
                        cp("pool", u[:, 0:2], halo[:, ci, :], [HALB[ci]], [ub])
                        tt("dve", u[:, 2:514], pst[:, :], arena[:, A_CC + ci, :], ALU.mult, [psb, AB[A_CC + ci]], [ub])
                        cp("pool", halo[:, ci, :], u[:, 512:514], [ub], [HALB[ci]])
                        a, ab_ = ftile()
                        cw = pv_off(l, "convw")
                        cbias = pv_off(l, "convb")
                        ts("pool", a[:, 0:512], u[:, 2:514], pv[:, cw + 8 + ci:cw + 9 + ci], pv[:, cbias + ci:cbias + ci + 1],
                           ALU.mult, ALU.add, [ub, PVB], [ab_])
                        t1, t1b = ftile()
                        ts("pool", t1[:, 0:512], u[:, 1:513], pv[:, cw + 4 + ci:cw + 5 + ci], None, ALU.mult, None, [ub, PVB], [t1b])
                        tt("pool", a[:, 0:512], a[:, 0:512], t1[:, 0:512], ALU.add, [ab_, t1b], [ab_])
                        t2, t2b = ftile()
                        ts("pool", t2[:, 0:512], u[:, 0:512], pv[:, cw + ci:cw + 1 + ci], None, ALU.mult, None, [ub, PVB], [t2b])
                        tt("pool", a[:, 0:512], a[:, 0:512], t2[:, 0:512], ALU.add, [ab_, t2b], [ab_])
                        tt("pool", brT[:, BR_CONV + ci, :], a[:, 0:512], arena[:, A_CB + ci, :], ALU.mult, [ab_, AB[A_CB + ci]], [BRB[BR_CONV + ci]])

                    linear_fm("in", l, win, 1024, 4, 8, h_rhs, conv_consume)
                    cut(5)

                    def sq_consume(ci, pst, psb):
                        act(arena[0:64, A_SQ + ci, :], pst[0:64, :], AF.Copy, [psb], [AB[A_SQ + ci]], scale=0.125)
                        P.op("pool", lambda e, ci=ci: e.memset(arena[64:128, A_SQ + ci, :], 0.0), writes=[AB[A_SQ + ci]])
                        act(arena[64:128, A_SQN + ci, :], pst[64:128, :], AF.Copy, [psb], [AB[A_SQN + ci]], scale=0.125)
                        P.op("pool", lambda e, ci=ci: e.memset(arena[0:64, A_SQN + ci, :], 0.0), writes=[AB[A_SQN + ci]])

                    linear_fm("in", l, win, 1536, 4, 8, h_rhs, sq_consume)
                    cut(51)

                    def sk_consume(ci, pst, psb):
                        if ci % 2 == 0:
                            act(kc[:, ci, t0:t0 + TG], pst[:, :], AF.Copy, [psb], [KCB[ci][g]])
                        else:
                            cp("dve", kc[:, ci, t0:t0 + TG], pst[:, :], [psb], [KCB[ci][g]])

                    linear_fm("in", l, win, 2048, 4, 8, h_rhs, sk_consume)
                    cut(52)

                    def sv_consume(tb, pst, psb):
                        if tb % 2 == 0:
                            act(vc[:, g * 4 + tb, :], pst[:, :], AF.Copy, [psb], [VCB[g * 4 + tb]])
                        else:
                            cp("dve", vc[:, g * 4 + tb, :], pst[:, :], [psb], [VCB[g * 4 + tb]])

                    linear_tm("in", l, win, 2560, sv_consume)
                    cut(6)

                    def rot_consume(a0, fc, fs):
                        def f(ci, pst, psb):
                            t1, t1b = ftile()
                            t2, t2b = ftile()
                            tt("dve", t1[:, 0:512], pst[:, :], rot_t[:, fc, :], ALU.mult, [psb, ROTB], [t1b])
                            tt("dve", t2[0:64, 0:512], pst[64:128, :], rot_t[64:128, fs, :], ALU.mult, [psb, ROTB], [t2b])
                            tt("dve", t2[64:128, 0:512], pst[0:64, :], rot_t[0:64, fs, :], ALU.mult, [psb, ROTB], [t2b])
                            tt("pool", arena[:, a0 + ci, :], t1[:, 0:512], t2[:, 0:512], ALU.add, [t1b, t2b], [AB[a0 + ci]])
                        return f

                    linear_fm("in", l, win, 3072, 4, 8, h_rhs, rot_consume(A_RQ, 0, 1))
                    linear_fm("in", l, win, 3584, 4, 8, h_rhs, rot_consume(A_RK, 0, 1))

                    rv_t, rv_b = arena_tm(A_RV)
                    gs_t, gs_b = arena_tm(A_GS)

                    def rv_consume(tb, pst, psb):
                        if tb % 2 == 0:
                            act(rv_t[:, tb, :], pst[:, :], AF.Copy, [psb], [rv_b[tb]])
                        else:
                            cp("dve", rv_t[:, tb, :], pst[:, :], [psb], [rv_b[tb]])

                    linear_tm("in", l, win, 4096, rv_consume)

                    def rg_consume(tb, pst, psb):
                        t1, t1b = ftile()
                        act(t1[:, 0:512], pst[:, :], AF.Silu, [psb], [t1b])
                        tt("pool", gs_t[:, tb, :], t1[:, 0:512], rgain[:], ALU.mult, [t1b, RGB], [gs_b[tb]])

                    linear_tm("in", l, win, 4608, rg_consume)
                    cut(7)

                    for c in (range(4) if 'ret' not in DISABLE else []):
                        cs = slice(c * 128, (c + 1) * 128)
                        S, Sb = rbank()
                        for hr in range(4):
                            mm(S[:, hr * 128:(hr + 1) * 128], arena[:, A_RK + hr, cs], arena[:, A_RQ + hr, cs], True, True,
                               [AB[A_RK + hr], AB[A_RQ + hr]], [Sb])
                        sm, smb = btile()
                        tt("dve", sm[:, :], S[:, :], cf[:, CF_DECAY:CF_DECAY + 512], ALU.mult, [Sb, CFB], [smb])
                        qd, qdb = btile()
                        for hr in range(4):
                            hs = slice(hr * 128, (hr + 1) * 128)
                            tt("pool", qd[:, hs], arena[:, A_RQ + hr, cs], cf[:, CF_QDEC + hr * 128:CF_QDEC + (hr + 1) * 128],
                               ALU.mult, [AB[A_RQ + hr], CFB], [qdb])
                        O, Ob = rbank()
                        for hr in range(4):
                            hs = slice(hr * 128, (hr + 1) * 128)
                            mm(O[:, hs], sm[:, hs], rv_t[:, c, hs], True, False, [smb, rv_b[c]], [Ob])
                            mm(O[:, hs], qd[:, hs], state_bf[:, hr, :], False, True, [qdb, STBB], [Ob])
                        for hr in range(4):
                            hs = slice(hr * 128, (hr + 1) * 128)
                            P.op("dve", lambda e, hr=hr, hs=hs, O=O: e.bn_stats(out=small[:, hr * 6:hr * 6 + 6], in_=O[:, hs]), reads=[Ob], writes=[SMB])
                        for hr in range(4):
                            P.op("dve", lambda e, hr=hr: e.bn_aggr(out=small[:, 24 + hr * 2:26 + hr * 2], in_=small[:, hr * 6:hr * 6 + 6]), reads=[SMB], writes=[SMB])
                        for hr in range(4):
                            act(small[:, 32 + hr:33 + hr], small[:, 25 + hr * 2:26 + hr * 2], AF.Ln, [SMB], [SMB], bias=EPS)
                        act(small[:, 32:36], small[:, 32:36], AF.Exp, [SMB], [SMB], scale=-0.5)
                        on, onb = ftile()
                        for hr in range(4):
                            hs = slice(hr * 128, (hr + 1) * 128)
                            ts("dve", on[:, hs], O[:, hs], small[:, 24 + hr * 2:25 + hr * 2], small[:, 32 + hr:33 + hr],
                               ALU.subtract, ALU.mult, [Ob, SMB], [onb])
                        rt_, rtb = btile()
                        tt("pool", rt_[:, :], on[:, 0:512], gs_t[:, c, :], ALU.mult, [onb, gs_b[c]], [rtb])
                        tp, tpb = tbank()
                        for hr in range(4):
                            hs = slice(hr * 128, (hr + 1) * 128)
                            tr(tp[:, hs], rt_[:, hs], [rtb], [tpb])
                        P.op("dve", lambda e, tp=tp, cs=cs: e.tensor_copy(out=brT[:, BR_RET:BR_RET + 4, cs],
                                                                       in_=tp.rearrange("p (h n) -> p h n", n=128)),
                             reads=[tpb], writes=BRB[BR_RET:BR_RET + 4])
                        tp2, tp2b = tbank()
                        for hr in range(4):
                            hs = slice(hr * 128, (hr + 1) * 128)
                            tr(tp2[:, hs], arena[:, A_RK + hr, cs], [AB[A_RK + hr]], [tp2b])
                        kdk, kdkb = btile()
                        for hr in range(4):
                            hs = slice(hr * 128, (hr + 1) * 128)
                            ts("dve", kdk[:, hs], tp2[:, hs], cf[:, CF_KDEC + hr:CF_KDEC + hr + 1], None, ALU.mult, None, [tp2b, CFB], [kdkb])
                        KV, KVb = rbank()
                        for hr in range(4):
                            hs = slice(hr * 128, (hr + 1) * 128)
                            mm(KV[:, hs], kdk[:, hs], rv_t[:, c, hs], True, True, [kdkb, rv_b[c]], [KVb])
                        for hr in range(4):
                            hs = slice(hr * 128, (hr + 1) * 128)
                            stt(state[:, hr, :], state[:, hr, :], cd[hr], KV[:, hs], ALU.mult, ALU.add, [STB, KVb], [STB])
                        cp("pool", state_bf[:], state[:], [STB], [STBB])

                    nblk = 4 * g + 4
                    for hp in (range(4) if 'sb' not in DISABLE else []):
                        Bt = [PB[B_BANKS[0]], PB[B_BANKS[1]]]
                        Ot = [PB[OUT_BANKS[0]], PB[OUT_BANKS[1]]]
                        for hh in range(2):
                            mm(Bt[hh][0][:, :], ones_bf, zeros[:], True, False, [CBB, ZB], [Bt[hh][1]])
                            mm(Ot[hh][0][:, :], ones_bf, zeros[:], True, False, [CBB, ZB], [Ot[hh][1]])
                        items = [(j, hh) for j in reversed(range(nblk)) for hh in range(2)]
                        n_it = len(items)
                        ctx = [None] * n_it

                        def S1(s):
                            j, hh = items[s]
                            jj = j - 4 * g
                            c0 = max(jj, 0) * 128
                            kT = kc[:, hp, j * 128:(j + 1) * 128]
                            kb = KCB[hp][j // 4]
                            qa = (A_SQ if hh == 0 else A_SQN) + hp
                            Z, Zb = rbank()
                            mm(Z[:, c0:512], kT, arena[:, qa, c0:512], True, True, [kb, AB[qa]], [Zb])
                            e_, eb = ftile()
                            act(e_[:, c0:512], Z[:, c0:512], AF.Exp, [Zb], [eb])
                            if jj >= 0:
                                tt("pool", e_[:, c0:c0 + 128], e_[:, c0:c0 + 128], cf[:, CF_MASK:CF_MASK + 128], ALU.mult, [eb, CFB], [eb])
                            l_, lb = btile()
                            act(l_[:, c0:512], e_[:, c0:512], AF.Ln, [eb], [lb], bias=1.0)
                            ctx[s] = dict(j=j, hh=hh, c0=c0, l=l_, lb=lb, e=e_, eb=eb)

                        def S2(s):
                            c = ctx[s]
                            B_, Bb = Bt[c["hh"]]
                            c0 = c["c0"]
                            mm(B_[:, c0:512], lincl, c["l"][:, c0:512], False, False, [CBB, c["lb"]], [Bb])
                            en, enb = ftile()
                            act(en[:, c0:512], B_[:, c0:512], AF.Exp, [Bb], [enb], scale=-1.0)
                            a_, ab_ = btile()
                            tt("dve", a_[:, c0:512], c["e"][:, c0:512], en[:, c0:512], ALU.mult, [c["eb"], enb], [ab_])
                            c["a"] = a_
                            c["ab"] = ab_

                        def S3(s):
                            c = ctx[s]
                            B_, Bb = Bt[c["hh"]]
                            O_, Ob_ = Ot[c["hh"]]
                            c0 = c["c0"]
                            j = c["j"]
                            mm(B_[:, c0:512], lcomp, c["l"][:, c0:512], False, False, [CBB, c["lb"]], [Bb])
                            mm(O_[:, c0:512], vc[:, j, hp * 128:(hp + 1) * 128], c["a"][:, c0:512], False, j == 0,
                               [VCB[j], c["ab"]], [Ob_])
                            ctx[s] = None

                        for s_ in range(n_it + 2):
                            if s_ < n_it:
                                S1(s_)
                            if 0 <= s_ - 1 < n_it:
                                S2(s_ - 1)
                            if 0 <= s_ - 2 < n_it:
                                S3(s_ - 2)
                        act(brT[0:64, BR_SB + hp, :], Ot[0][0][0:64, :], AF.Copy, [Ot[0][1]], [BRB[BR_SB + hp]])
                        act(brT[64:128, BR_SB + hp, :], Ot[1][0][64:128, :], AF.Copy, [Ot[1][1]], [BRB[BR_SB + hp]])

                    for dg in (range(2) if 'merge' not in DISABLE else []):
                        for n in range(3):
                            def gate_consume(ci, pst, psb, n=n):
                                bc = pv_off(l, "bgate") + n * 8 + dg * 4 + ci
                                act(arena[:, A_G + n * 4 + ci, :], pst[:, :], AF.Sigmoid, [psb, PVB], [AB[A_G + n * 4 + ci]],
                                    bias=pv[:, bc:bc + 1])
                            linear_fm("in", l, win, 5120 + n * 1024 + dg * 512, 4, 8, h_rhs, gate_consume)
                        mt = [(marena[:, i * 512:(i + 1) * 512], None) for i in range(4)]
                        for n in range(3):
                            def up_consume(ci, pst, psb, n=n):
                                m_ = mt[ci][0]
                                mbs = [AB[20 + 2 * ci], AB[21 + 2 * ci]]
                                gsl = arena[:, A_G + n * 4 + ci, :]
                                gb = AB[A_G + n * 4 + ci]
                                if n == 0:
                                    tt("dve", m_[:, 0:512], pst[:, :], gsl, ALU.mult, [psb, gb], mbs)
                                else:
                                    t1, t1b = ftile()
                                    tt("dve", t1[:, 0:512], pst[:, :], gsl, ALU.mult, [psb, gb], [t1b])
                                    if n == 1:
                                        tt("pool", m_[:, 0:512], m_[:, 0:512], t1[:, 0:512], ALU.add, mbs + [t1b], mbs)
                                    else:
                                        d = dg * 4 + ci
                                        tt("pool", arena[:, A_MG + d, :], m_[:, 0:512], t1[:, 0:512], ALU.add, mbs + [t1b], [AB[A_MG + d]])
                            br_rhs = lambda k, n=n: (brT[:, n * 4 + k, :], BRB[n * 4 + k])
                            linear_fm("br", l, wb_br[l, n], dg * 512, 4, 4, br_rhs, up_consume)

                    def resid_consume(ci, pst, psb):
                        tt("dve", xt[:, ci, :], pst[:, :], xt[:, ci, :], ALU.add, [psb, XB[ci]], [XB[ci]])

                    mg_rhs = lambda k: (arena[:, A_MG + k, :], AB[A_MG + k])
                    if "merge" not in DISABLE:
                        linear_fm("o", l, wb_o[l], 0, 8, 8, mg_rhs, resid_consume)

                    if 'xattn' not in DISABLE:
                        rmsnorm(xt, XB, TG, pv_off(l, "nxa"), ht, HB)

                        def q2_consume(ci, pst, psb):
                            act(arena[:, A_Q2 + ci, :], pst[:, :], AF.Copy, [psb], [AB[A_Q2 + ci]], scale=1.0 / 16.0)

                        linear_fm("xq", l, wb_xq[l], 0, 8, 8, h_rhs, q2_consume)
                        for hx in range(4):
                            pts = []
                            for mb in range(2):
                                SC, SCb = rbank()
                                for kk in range(2):
                                    mm(SC[:, :], kmem[:, 2 * hx + kk, mb * 128:(mb + 1) * 128], arena[:, A_Q2 + 2 * hx + kk, :], kk == 0, kk == 1,
                                       [KMB, AB[A_Q2 + 2 * hx + kk]], [SCb])
                                p_, pb_ = btile()
                                act(p_[:, :], SC[:, :], AF.Exp, [SCb], [pb_])
                                pts.append((p_, pb_))
                            DEN, DENb = rbank()
                            for mb in range(2):
                                mm(DEN[:, :], ones_bf, pts[mb][0][:, :], mb == 0, mb == 1, [CBB, pts[mb][1]], [DENb])
                            rd_, rdb = ftile()
                            P.op("dve", lambda e, rd_=rd_, DEN=DEN: e.reciprocal(out=rd_[:, 0:512], in_=DEN[:, :]), reads=[DENb], writes=[rdb])
                            for kk in range(2):
                                OT, OTb = rbank()
                                dch = 2 * hx + kk
                                for mb in range(2):
                                    mm(OT[:, :], vmem[:, mb, dch * 128:(dch + 1) * 128], pts[mb][0][:, :], mb == 0, mb == 1, [VMB, pts[mb][1]], [OTb])
                                tt("dve", arena[:, A_OT + dch, :], OT[:, :], rd_[:, 0:512], ALU.mult, [OTb, rdb], [AB[A_OT + dch]])
                        ot_rhs = lambda k: (arena[:, A_OT + k, :], AB[A_OT + k])
                        linear_fm("xo", l, wb_xo[l], 0, 8, 8, ot_rhs, resid_consume)

                    if 'mlp' not in DISABLE:
                        rmsnorm(xt, XB, TG, pv_off(l, "nmlp"), ht, HB)

                        def up_mlp_consume(ci, pst, psb):
                            t1, t1b = ftile()
                            act(t1[:, 0:512], pst[:, :], AF.Relu, [psb], [t1b])
                            tt("pool", arena[:, ci, :], t1[:, 0:512], t1[:, 0:512], ALU.mult, [t1b], [AB[ci]])

                        linear_fm("up", l, wb_up[l], 0, 32, 8, h_rhs, up_mlp_consume)
                        for d in range(8):
                            slot, wsb = wslot()
                            view = slot[:, :].rearrange("p (c n) -> p c n", n=128)
                            load("sp", slot[:, :], wb_dn[l, d], wsb, R=[WB[("dn", l)]])
                            pst, psb = rbank()
                            for k in range(32):
                                mm(pst[:, :], view[:, k, :], arena[:, k, :], k == 0, k == 31, [wsb, AB[k]], [psb])
                            resid_consume(d, pst, psb)

                    if l == L - 1:
                        rmsnorm_final(P, xt, XB, sq, SQB, rstd, RSB, pv, PVB, L * PV_L, rbank, mm, act, stt, tt, ones_bf, CBB,
                                      ftile, out_d, t0, load)
                    else:
                        P.dma("pool", lambda e, xt=xt, t0=t0: [e.dma_start(out=xs_d.rearrange("(c p) n -> p c n", p=128)[:, :, t0:t0 + TG], in_=xt[:])],
                              XB[0], reads=XB, writes=[XSD[g]])


        except StopBuild:
            pass
        P.op("sp", lambda e: None, reads=OUTD)
        P.generate(block)
        build.stats = (P.stats, P.sbytes)
    return nc


OUTD = []


def rmsnorm_final(P, xt, XB, sq, SQB, rstd, RSB, pv, PVB, gcol, rbank, mm, act, stt, tt, ones_bf, CBB, ftile, out_d, t0, load):
    for c in range(8):
        tt("pool", sq[:, c, :], xt[:, c, :], xt[:, c, :], ALU.mult, [XB[c]], [SQB[c]])
    pst, psb = rbank()
    for c in range(8):
        mm(pst[:, :], ones_bf, sq[:, c, :], c == 0, c == 7, [SQB[c], CBB], [psb])
    act(rstd[:, :], pst[:, :], AF.Ln, [psb], [RSB], scale=1.0 / 1024.0, bias=EPS)
    act(rstd[:, :], rstd[:, :], AF.Exp, [RSB], [RSB], scale=-0.5)
    for c in range(8):
        o, ob = ftile()
        stt(o[:, 0:512], xt[:, c, :], pv[:, gcol + c:gcol + c + 1], rstd[:, :], ALU.mult, ALU.mult, [XB[c], PVB, RSB], [ob])
        od = P.buf("outd")
        P.dma("pool", lambda e, o=o, c=c: [e.dma_start(out=out_d[c * 128:(c + 1) * 128, t0:t0 + 512], in_=o[:, 0:512])],
              ob, reads=[ob], writes=[od])
        OUTD.append(od)


_CACHE = {}


def run(inputs, T, L=2, n_cores=8, active=ACTIVE_CORES):
    f32 = np.float32
    x = np.asarray(inputs["x"], f32)
    mem = np.asarray(inputs["mem"], f32)
    B = x.shape[0]
    cf, cb, rot, cd = host_constants(T)
    key = (T, L)
    OUTD.clear()
    nc = build(T, L, cd)
    pvec = host_pvec(inputs, L)
    rgain = np.ascontiguousarray(np.broadcast_to(np.asarray(inputs["ret_norm_g"], f32)[:, None, :], (L, 128, 512)))
    common = {
        "pv": pvec, "cf": cf, "cb": cb, "rot": rot, "rgain": rgain,
        "w_in": np.asarray(inputs["w_in"], f32), "w_branch": np.asarray(inputs["w_branch"], f32),
        "w_o": np.asarray(inputs["w_o"], f32), "w_xq": np.asarray(inputs["w_xq"], f32),
        "w_xkv": np.asarray(inputs["w_xkv"], f32), "w_xo": np.asarray(inputs["w_xo"], f32),
        "w_up": np.asarray(inputs["w_up"], f32), "w_down": np.asarray(inputs["w_down"], f32),
    }
    in_maps = []
    zx = np.zeros((1024, T), f32)
    zm = np.zeros((1024, MEM_LEN), f32)
    slot = {c: i for i, c in enumerate(active)}
    for c in range(n_cores):
        m = dict(common)
        if c in slot and slot[c] < B:
            b = slot[c]
            m["xT"] = np.ascontiguousarray(x[b].T)
            m["memT"] = np.ascontiguousarray(mem[b].T)
        else:
            m["xT"] = zx
            m["memT"] = zm
        in_maps.append(m)
    res = run_bass_kernel_spmd(nc, in_maps, core_ids=list(range(n_cores)))
    out = np.empty((B, T, 1024), f32)
    for c, b in slot.items():
        if b < B:
            out[b] = np.asarray(res.results[c]["outT"], f32).T
    return out


def kernel(**inputs):
    T = inputs["x"].shape[1]
    return run(inputs, T)
```

```python
import numpy as np
import ml_dtypes
import concourse.bass as bass
import concourse.mybir as mybir
from concourse.bass_utils import run_bass_kernel_spmd
from contextlib import ExitStack

F32 = mybir.dt.float32
BF16 = mybir.dt.bfloat16
ALU = mybir.AluOpType
AF = mybir.ActivationFunctionType

D_MODEL = 1024
MEM_LEN = 256
IN_COLS = 8192
D_FF = 4096
EPS = 1e-6
TG = 512
ACTIVE_CORES = (0, 1, 4, 5)
DISABLE = set()
CUT = 0


class StopBuild(Exception):
    pass


def cut(n):
    if CUT == n:
        raise StopBuild()


class Buf:
    __slots__ = ("name", "lw", "rd", "sem", "dcount", "last_dma")

    def __init__(self, name):
        self.name = name
        self.lw = None
        self.rd = []
        self.sem = None
        self.dcount = 0
        self.last_dma = None


class Inst:
    __slots__ = ("eng", "fn", "hard", "soft", "sig", "is_dma", "owner", "dval", "semval")

    def __init__(self, eng, fn, is_dma=False):
        self.eng = eng
        self.fn = fn
        self.hard = []
        self.soft = []
        self.sig = False
        self.is_dma = is_dma
        self.owner = None
        self.dval = 0
        self.semval = 0


class Prog:
    ENGS = ("pe", "act", "dve", "pool", "sp")

    def __init__(self, nc, es):
        self.nc = nc
        self.es = es
        self.q = {e: [] for e in self.ENGS}
        self.uid = 0
        self.sbytes = 0

    def sb(self, shape, dtype, name=None):
        self.uid += 1
        n = 1
        for s in shape[1:]:
            n *= s
        self.sbytes += n * (2 if dtype == BF16 else 4)
        return self.es.enter_context(self.nc.sbuf_tensor(name or f"sb{self.uid}", list(shape), dtype))

    def ps(self, shape, dtype=F32, name=None):
        self.uid += 1
        return self.es.enter_context(self.nc.psum_tensor(name or f"ps{self.uid}", list(shape), dtype))

    def buf(self, name=None):
        self.uid += 1
        return Buf(name or f"b{self.uid}")

    def new_sem(self, name):
        return self.es.enter_context(self.nc.semaphore(name))

    def _track(self, inst, reads, writes):
        wset = set(id(b) for b in writes)
        for b in reads:
            if b.lw is not None:
                inst.hard.append(b.lw)
        for b in writes:
            if b.lw is not None:
                inst.hard.append(b.lw)
            inst.soft.extend(b.rd)
        for b in reads:
            if id(b) not in wset:
                b.rd.append(inst)
        for b in writes:
            b.lw = inst
            b.rd = []

    def op(self, eng, fn, reads=(), writes=()):
        inst = Inst(eng, fn)
        self._track(inst, reads, writes)
        self.q[eng].append(inst)
        return inst

    def dma(self, queue, fn, owner, reads=(), writes=(), npieces=1):
        inst = Inst(queue, fn, is_dma=True)
        self._track(inst, reads, writes)
        if owner.last_dma is not None:
            inst.hard.append(owner.last_dma)
        owner.last_dma = inst
        if owner.sem is None:
            self.uid += 1
            owner.sem = self.new_sem(f"dsem{self.uid}")
        owner.dcount += 16 * npieces
        inst.owner = owner
        inst.dval = owner.dcount
        self.q[queue].append(inst)
        return inst

    def generate(self, block):
        esem = {e: self.new_sem(f"esem_{e}") for e in ("pe", "act", "dve", "pool")}
        for e in self.ENGS:
            for inst in self.q[e]:
                deps = []
                for p in inst.hard:
                    if p is inst:
                        continue
                    if p.is_dma or inst.is_dma or p.eng != inst.eng or inst.eng != "pe":
                        deps.append(p)
                for p in inst.soft:
                    if p is inst:
                        continue
                    if p.is_dma or inst.is_dma or p.eng != inst.eng:
                        deps.append(p)
                inst.hard = deps
                inst.soft = None
                for p in deps:
                    if not p.is_dma:
                        p.sig = True
        for e in ("pe", "act", "dve", "pool"):
            c = 0
            for inst in self.q[e]:
                if inst.is_dma:
                    continue
                if inst.sig:
                    c += 1
                    inst.semval = c
        self.stats = {}

        def gen(e, eng):
            waited = {}
            nw = 0
            for inst in self.q[e]:
                need = {}
                for p in inst.hard:
                    if p.is_dma:
                        s, v = p.owner.sem, p.dval
                    else:
                        s, v = esem[p.eng], p.semval
                    k = id(s)
                    if waited.get(k, 0) >= v:
                        continue
                    if k not in need or need[k][1] < v:
                        need[k] = (s, v)
                for k, (s, v) in need.items():
                    eng.wait_ge(s, v)
                    waited[k] = v
                    nw += 1
                r = inst.fn(eng)
                if inst.is_dma:
                    for h in r:
                        h.then_inc(inst.owner.sem, 16)
                elif inst.sig:
                    r.then_inc(esem[e], 1)
            self.stats[e] = (len(self.q[e]), nw)

        @block.sync
        def _(eng):
            gen("sp", eng)

        @block.scalar
        def _(eng):
            gen("act", eng)

        @block.vector
        def _(eng):
            gen("dve", eng)

        @block.gpsimd
        def _(eng):
            gen("pool", eng)

        @block.tensor
        def _(eng):
            gen("pe", eng)


PV_L = 8 * 4 + 24 + 12 + 4


def pv_off(l, name):
    base = l * PV_L
    offs = {"nmix": 0, "nxa": 8, "nmem": 16, "nmlp": 24, "bgate": 32, "convw": 56, "convb": 68}
    return base + offs[name]


def host_constants(T):
    f32 = np.float32
    idx = np.arange(128, dtype=f32)
    log_gamma = np.log1p(-np.exp2(-5.0 - np.arange(4, dtype=f32))).astype(f32)
    j = idx[:, None]
    i = idx[None, :]
    decayT = np.zeros((128, 4, 128), f32)
    for h in range(4):
        rel = i - j
        decayT[:, h, :] = np.where(rel >= 0, np.exp(np.maximum(rel, 0.0) * log_gamma[h]), 0.0)
    ks = f32(128.0 ** -0.5)
    decayT = (decayT * ks).astype(f32)
    qdec = np.zeros((128, 4, 128), f32)
    for h in range(4):
        qdec[:, h, :] = np.exp((idx + 1) * log_gamma[h])[None, :]
    mask = (j < i).astype(f32)
    kdecay = (np.exp((127 - idx)[:, None] * log_gamma[None, :]) * ks).astype(f32)
    cf = np.concatenate([decayT.reshape(128, 512), qdec.reshape(128, 512), mask, kdecay], axis=1).astype(f32)
    cd = [float(np.exp(f32(128.0) * log_gamma[h])) for h in range(4)]
    ident = np.eye(128, dtype=f32)
    ones = np.ones((128, 128), f32)
    lincl = (j >= i).astype(f32)
    lc = 1.0 - lincl
    cb = np.concatenate([ident, ones, lincl, lc, mask], axis=1).astype(ml_dtypes.bfloat16)
    pos = np.arange(T, dtype=f32)
    inv_freq = (f32(10000.0) ** (-np.arange(0, 128, 2, dtype=f32) / f32(128))).astype(f32)
    ang = (pos[:, None] * inv_freq[None, :]).astype(f32)
    cos = np.cos(ang).astype(f32).T
    sin = np.sin(ang).astype(f32).T
    cosf = np.concatenate([cos, cos], 0)
    sinf = np.concatenate([sin, -sin], 0)
    rot = np.stack([cosf, sinf], 0).astype(f32)
    return cf, cb, rot, cd


CF_DECAY, CF_QDEC, CF_MASK, CF_KDEC, CF_N = 0, 512, 1024, 1024 + 128, 1024 + 128 + 4
CB_ID, CB_ONES, CB_LINCL, CB_LC, CB_MASK, CB_N = 0, 128, 256, 384, 512, 640


def col8(v):
    return np.ascontiguousarray(v.reshape(-1, 128).T)


def host_pvec(inp, L):
    cols = []
    for l in range(L):
        cols += [col8(inp["norm_mix_g"][l]), col8(inp["norm_xa_g"][l]), col8(inp["norm_mem_g"][l]),
                 col8(inp["norm_mlp_g"][l])]
        cols.append(np.concatenate([col8(inp["b_gate"][l][n]) for n in range(3)], axis=1))
        cols.append(np.concatenate([col8(inp["conv_w"][l][jj]) for jj in range(3)], axis=1))
        cols.append(col8(inp["conv_b"][l]))
    cols.append(col8(inp["final_g"]))
    return np.ascontiguousarray(np.concatenate(cols, axis=1).astype(np.float32))


def build(T, L, cd, dbg=False):
    NG = T // TG
    NB = T // 128
    nc = bass.Bass("TRN2", target_bir_lowering=False)

    def din(name, shape, dt=F32):
        return nc.dram_tensor(name, list(shape), dt, kind="ExternalInput").ap()

    xT_d = din("xT", [1024, T])
    memT_d = din("memT", [1024, MEM_LEN])
    pv_d = din("pv", [128, L * PV_L + 8])
    cf_d = din("cf", [128, CF_N])
    cb_d = din("cb", [128, CB_N], BF16)
    rot_d = din("rot", [2, 128, T])
    rgain_d = din("rgain", [L, 128, 512])
    w_in_d = din("w_in", [L, 1024, IN_COLS])
    w_br_d = din("w_branch", [L, 3, 512, 1024])
    w_o_d = din("w_o", [L, 1024, 1024])
    w_xq_d = din("w_xq", [L, 1024, 1024])
    w_xkv_d = din("w_xkv", [L, 1024, 2048])
    w_xo_d = din("w_xo", [L, 1024, 1024])
    w_up_d = din("w_up", [L, 1024, D_FF])
    w_dn_d = din("w_down", [L, D_FF, 1024])
    out_d = nc.dram_tensor("outT", [1024, T], F32, kind="ExternalOutput").ap()

    def dint(name, shape, dt=BF16):
        return nc.dram_tensor(name, list(shape), dt).ap()

    wb_in = dint("wb_in", [L, 1024, IN_COLS])
    wb_br = dint("wb_br", [L, 3, 512, 1024])
    wb_o = dint("wb_o", [L, 1024, 1024])
    wb_xq = dint("wb_xq", [L, 1024, 1024])
    wb_xkv = dint("wb_xkv", [L, 1024, 2048])
    wb_xo = dint("wb_xo", [L, 1024, 1024])
    wb_up = dint("wb_up", [L, 1024, D_FF])
    wb_dn = dint("wb_dn", [L, 8, 128, 32 * 128])
    xs_d = dint("xs", [1024, T], F32)

    with ExitStack() as es:
        P = Prog(nc, es)
        block = es.enter_context(nc.Block())

        pv = P.sb([128, L * PV_L + 8], F32); PVB = P.buf("pv")
        cf = P.sb([128, CF_N], F32); CFB = P.buf("cf")
        cb = P.sb([128, CB_N], BF16); CBB = P.buf("cb")
        zeros = P.sb([128, 512], BF16); ZB = P.buf("zeros")
        rgain = P.sb([128, 512], F32); RGB = P.buf("rgain")
        NW = 3
        wslots = [(P.sb([128, 4096], BF16), P.buf(f"ws{i}")) for i in range(NW)]
        xts = [(P.sb([128, 8, 512], F32), [P.buf(f"x{i}_{c}") for c in range(8)]) for i in range(1)]
        ht = P.sb([128, 8, 512], BF16); HB = [P.buf(f"h{c}") for c in range(8)]
        NA = 32
        arena = P.sb([128, NA, 512], BF16); AB = [P.buf(f"ar{i}") for i in range(NA)]
        sq = arena[:, 0:8, :]; SQB = AB[0:8]
        marena = arena[:, 20:28, :].rearrange("p a n -> p (a n)").bitcast(F32)
        brT = P.sb([128, 12, 512], BF16); BRB = [P.buf(f"br{i}") for i in range(12)]
        kc = P.sb([128, 4, T], BF16); KCB = [[P.buf(f"kc{c}_{g}") for g in range(NG)] for c in range(4)]
        vc = P.sb([128, NB, 512], BF16); VCB = [P.buf(f"vc{b}") for b in range(NB)]
        rot_t = P.sb([128, 2, 512], F32); ROTB = P.buf("rot")
        kmem = P.sb([128, 8, 256], BF16); KMB = P.buf("kmem")
        vmem = P.sb([128, 2, 1024], BF16); VMB = P.buf("vmem")
        state = P.sb([128, 4, 128], F32); STB = P.buf("state")
        state_bf = P.sb([128, 4, 128], BF16); STBB = P.buf("statebf")
        halo = P.sb([128, 4, 2], F32); HALB = [P.buf(f"halo{c}") for c in range(4)]
        NF = 8
        ftiles = [(P.sb([128, 514], F32), P.buf(f"ft{i}")) for i in range(NF)]
        NBT = 10
        btiles = [(P.sb([128, 512], BF16), P.buf(f"bt{i}")) for i in range(NBT)]
        small = P.sb([128, 64], F32); SMB = P.buf("small")

        PB = [(P.ps([128, 512], F32), P.buf(f"pb{i}")) for i in range(7)]
        tb_ps = P.ps([128, 1024], BF16); _tb = P.buf("tb"); TBB = [_tb, _tb]
        ROT_BANKS = [0, 1, 2]
        B_BANKS = [3, 4]
        OUT_BANKS = [5, 6]
        st = {"rb": 0, "ws": 0, "ft": 0, "bt": 0, "tb": 0}

        def rbank():
            i = ROT_BANKS[st["rb"] % len(ROT_BANKS)]
            st["rb"] += 1
            return PB[i]

        def wslot():
            s = wslots[st["ws"] % NW]
            st["ws"] += 1
            return s

        def ftile():
            s = ftiles[st["ft"] % NF]
            st["ft"] += 1
            return s

        def btile():
            s = btiles[st["bt"] % NBT]
            st["bt"] += 1
            return s

        def tbank():
            i = st["tb"] % 2
            st["tb"] += 1
            return tb_ps[:, i * 512:(i + 1) * 512], TBB[i]

        ident = cb[:, CB_ID:CB_ID + 128]
        ones_bf = cb[:, CB_ONES:CB_ONES + 128]
        lincl = cb[:, CB_LINCL:CB_LINCL + 128]
        lcomp = cb[:, CB_LC:CB_LC + 128]
        mask_bf = cb[:, CB_MASK:CB_MASK + 128]

        def mm(out, lhsT, rhs, start, stop, R, Wb):
            P.op("pe", lambda e: e.matmul(out, lhsT=lhsT, rhs=rhs, start=start, stop=stop), reads=R, writes=Wb)

        def tr(out, in_, R, Wb):
            P.op("pe", lambda e: e.transpose(out, in_, ident), reads=R + [CBB], writes=Wb)

        def act(out, in_, func, R, Wb, scale=None, bias=None):
            kw = {}
            if scale is not None:
                kw["scale"] = scale
            if bias is not None:
                kw["bias"] = bias
            P.op("act", lambda e: e.activation(out=out, in_=in_, func=func, **kw), reads=R, writes=Wb)

        def tt(eng, out, in0, in1, op, R, Wb):
            P.op(eng, lambda e: e.tensor_tensor(out=out, in0=in0, in1=in1, op=op), reads=R, writes=Wb)

        def ts(eng, out, in0, s1, s2, op0, op1, R, Wb):
            if op1 is None:
                P.op(eng, lambda e: e.tensor_scalar(out=out, in0=in0, scalar1=s1, scalar2=None, op0=op0), reads=R, writes=Wb)
            else:
                P.op(eng, lambda e: e.tensor_scalar(out=out, in0=in0, scalar1=s1, scalar2=s2, op0=op0, op1=op1), reads=R, writes=Wb)

        def stt(out, in0, scalar, in1, op0, op1, R, Wb):
            P.op("dve", lambda e: e.scalar_tensor_tensor(out=out, in0=in0, scalar=scalar, in1=in1, op0=op0, op1=op1), reads=R, writes=Wb)

        def cp(eng, out, in_, R, Wb):
            P.op(eng, lambda e: e.tensor_copy(out=out, in_=in_), reads=R, writes=Wb)

        def load(queue, out, in_, owner, R=(), Wb=None):
            P.dma(queue, lambda e: [e.dma_start(out=out, in_=in_)], owner, reads=list(R), writes=[owner] if Wb is None else Wb)

        load("sp", pv[:], pv_d, PVB)
        load("sp", cf[:], cf_d, CFB)
        load("sp", cb[:], cb_d, CBB)
        P.op("dve", lambda e: e.memset(zeros[:], 0.0), writes=[ZB])

        WB = {}

        PCB = P.buf("precast_chain")

        def precast(name, l, pieces):
            b = P.buf(f"wb_{name}{l}")
            for (o, i) in pieces:
                P.dma("pool", lambda e, o=o, i=i: [e.dma_start(out=o, in_=i)], PCB, writes=[b])
            WB[(name, l)] = b

        for l in range(L):
            precast("in", l, [(wb_in[l, r * 128:(r + 1) * 128, :], w_in_d[l, r * 128:(r + 1) * 128, :]) for r in range(8)])
            precast("xkv", l, [(wb_xkv[l, r * 256:(r + 1) * 256, :], w_xkv_d[l, r * 256:(r + 1) * 256, :]) for r in range(4)])
            precast("br", l, [(wb_br[l, n], w_br_d[l, n]) for n in range(3)])
            precast("o", l, [(wb_o[l, r * 512:(r + 1) * 512, :], w_o_d[l, r * 512:(r + 1) * 512, :]) for r in range(2)])
            precast("xq", l, [(wb_xq[l, r * 512:(r + 1) * 512, :], w_xq_d[l, r * 512:(r + 1) * 512, :]) for r in range(2)])
            precast("xo", l, [(wb_xo[l, r * 512:(r + 1) * 512, :], w_xo_d[l, r * 512:(r + 1) * 512, :]) for r in range(2)])
            precast("up", l, [(wb_up[l, r * 256:(r + 1) * 256, :], w_up_d[l, r * 256:(r + 1) * 256, :]) for r in range(4)])
            precast("dn", l, [(wb_dn[l, d].rearrange("p (c n) -> p c n", n=128),
                               w_dn_d[l, :, d * 128:(d + 1) * 128].rearrange("(c p) n -> p c n", p=128)) for d in range(8)])

        def wload_cols(name, l, src2d, c0, ncols, kchunks):
            slot, sb_ = wslot()
            view = slot[:, 0:kchunks * ncols].rearrange("p (c n) -> p c n", n=ncols)
            src = src2d.rearrange("(c p) n -> p c n", p=128)[:, :, c0:c0 + ncols]
            load("sp", view, src, sb_, R=[WB[(name, l)]])
            return view, sb_

        def rmsnorm(xt, XB, ncol, gcol, out_t, OB, out_is_f32=False):
            for c in range(8):
                tt("pool", sq[:, c, 0:ncol], xt[:, c, :], xt[:, c, :], ALU.mult, [XB[c]], [SQB[c]])
            pst, psb = rbank()
            rstd, RSB = ftile()
            for c in range(8):
                mm(pst[:, 0:ncol], ones_bf, sq[:, c, 0:ncol], c == 0, c == 7, [SQB[c], CBB], [psb])
            act(rstd[:, 0:ncol], pst[:, 0:ncol], AF.Ln, [psb], [RSB], scale=1.0 / 1024.0, bias=EPS)
            act(rstd[:, 0:ncol], rstd[:, 0:ncol], AF.Exp, [RSB], [RSB], scale=-0.5)
            for c in range(8):
                stt(out_t[:, c, :], xt[:, c, :], pv[:, gcol + c:gcol + c + 1], rstd[:, 0:ncol], ALU.mult, ALU.mult,
                    [XB[c], PVB, RSB], [OB[c]])

        def linear_fm(name, l, src2d, col0, nchunks, kchunks, rhs_fn, consume):
            done = 0
            while done < nchunks:
                n = min(4, nchunks - done)
                view, wsb = wload_cols(name, l, src2d, col0 + done * 128, n * 128, kchunks)
                for cc in range(n):
                    pst, psb = rbank()
                    for k in range(kchunks):
                        r_ap, r_b = rhs_fn(k)
                        mm(pst[:, :], view[:, k, cc * 128:(cc + 1) * 128], r_ap, k == 0, k == kchunks - 1, [wsb, r_b], [psb])
                    consume(done + cc, pst, psb)
                done += n

        def linear_tm(name, l, src2d, col0, consume):
            view, wsb = wload_cols(name, l, src2d, col0, 512, 8)
            for tb in range(4):
                pst, psb = rbank()
                for k in range(8):
                    mm(pst[:, :], ht[:, k, tb * 128:(tb + 1) * 128], view[:, k, :], k == 0, k == 7, [wsb, HB[k]], [psb])
                consume(tb, pst, psb)

        h_rhs = lambda k: (ht[:, k, :], HB[k])

        A_CB, A_CC, A_SQ, A_SQN, A_RQ, A_RK = 0, 4, 8, 12, 16, 20
        A_RV, A_GS = 24, 28
        A_G, A_MG = 0, 12
        A_Q2, A_OT = 16, 24
        BR_CONV, BR_SB, BR_RET = 0, 4, 8

        def arena_tm(a0):
            return arena[:, a0:a0 + 4, :], AB[a0:a0 + 4]

        XSD = [P.buf(f"xsd{i}") for i in range(NG)]

        try:
            cut(1)
            for l in range(L):
                load("sp", rgain[:], rgain_d[l], RGB)
                P.op("pool", lambda e: e.memset(state[:], 0.0), writes=[STB])
                P.op("pool", lambda e: e.memset(state_bf[:], 0.0), writes=[STBB])
                P.op("pool", lambda e: e.memset(halo[:], 0.0), writes=HALB)
                xt0, XB0 = xts[0]
                memx = xt0[:, :, 0:MEM_LEN]
                P.dma("sp", lambda e, memx=memx: [e.dma_start(out=memx, in_=memT_d.rearrange("(c p) n -> p c n", p=128))],
                      XB0[0], writes=XB0)
                rmsnorm(memx, XB0, MEM_LEN, pv_off(l, "nmem"), ht[:, :, 0:MEM_LEN], HB)
                def kmem_consume(ci, pst, psb):
                    cp("dve", kmem[:, ci, :], pst[:, 0:MEM_LEN], [psb], [KMB])
                done = 0
                for half in range(2):
                    view, wsb = wload_cols("xkv", l, wb_xkv[l], half * 512, 512, 8)
                    for cc in range(4):
                        pst, psb = rbank()
                        for k in range(8):
                            mm(pst[:, 0:MEM_LEN], view[:, k, cc * 128:(cc + 1) * 128], ht[:, k, 0:MEM_LEN], k == 0, k == 7, [wsb, HB[k]], [psb])
                        kmem_consume(half * 4 + cc, pst, psb)
                for half in range(2):
                    view, wsb = wload_cols("xkv", l, wb_xkv[l], 1024 + half * 512, 512, 8)
                    for mb in range(2):
                        pst, psb = rbank()
                        for k in range(8):
                            mm(pst[:, :], ht[:, k, mb * 128:(mb + 1) * 128], view[:, k, :], k == 0, k == 7, [wsb, HB[k]], [psb])
                        cp("dve", vmem[:, mb, half * 512:(half + 1) * 512], pst[:, :], [psb], [VMB])

                cut(2)
                for g in range(NG):
                    xt, XB = xts[0]
                    t0 = g * TG
                    src = xT_d if l == 0 else xs_d
                    rd = [] if l == 0 else [XSD[g]]
                    P.dma("sp", lambda e, xt=xt, src=src, t0=t0: [e.dma_start(out=xt[:], in_=src.rearrange("(c p) n -> p c n", p=128)[:, :, t0:t0 + TG])],
                          XB[0], reads=rd, writes=XB)
                    load("sp", rot_t[:], rot_d[:, :, t0:t0 + TG].rearrange("f p n -> p f n"), ROTB)

                    rmsnorm(xt, XB, TG, pv_off(l, "nmix"), ht, HB)
                    win = wb_in[l]
                    cut(3)

                    def ev_bf(a0, alt=False):
                        def f(ci, pst, psb):
                            if (ci % 2 == 0) != alt:
                                act(arena[:, a0 + ci, :], pst[:, :], AF.Copy, [psb], [AB[a0 + ci]])
                            else:
                                cp("dve", arena[:, a0 + ci, :], pst[:, :], [psb], [AB[a0 + ci]])
                        return f

                    linear_fm("in", l, win, 0, 4, 8, h_rhs, ev_bf(A_CB))
                    linear_fm("in", l, win, 512, 4, 8, h_rhs, ev_bf(A_CC, True))
                    cut(4)

                    def conv_consume(ci, pst, psb):
                        u, ub = ftile()
                        cp("pool", u[:, 0:2], halo[:, ci, :], [HALB[ci]], [ub])
                        tt("dve", u[:, 2:514], pst[:, :], arena[:, A_CC + ci, :], ALU.mult, [psb, AB[A_CC + ci]], [ub])
                        cp("pool", halo[:, ci, :], u[:, 512:514], [ub], [HALB[ci]])
                        a, ab_ = ftile()
                        cw = pv_off(l, "convw")
                        cbias = pv_off(l, "convb")
                        ts("pool", a[:, 0:512], u[:, 2:514], pv[:, cw + 8 + ci:cw + 9 + ci], pv[:, cbias + ci:cbias + ci + 1],
                           ALU.mult, ALU.add, [ub, PVB], [ab_])
                        t1, t1b = ftile()
                        ts("pool", t1[:, 0:512], u[:, 1:513], pv[:, cw + 4 + ci:cw + 5 + ci], None, ALU.mult, None, [ub, PVB], [t1b])
                        tt("pool", a[:, 0:512], a[:, 0:512], t1[:, 0:512], ALU.add, [ab_, t1b], [ab_])
                        t2, t2b = ftile()
                        ts("pool", t2[:, 0:512], u[:, 0:512], pv[:, cw + ci:cw + 1 + ci], None, ALU.mult, None, [ub, PVB], [t2b])
                        tt("pool", a[:, 0:512], a[:, 0:512], t2[:, 0:512], ALU.add, [ab_, t2b], [ab_])
                        tt("pool", brT[:, BR_CONV + ci, :], a[:, 0:512], arena[:, A_CB + ci, :], ALU.mult, [ab_, AB[A_CB + ci]], [BRB[BR_CONV + ci]])

                    linear_fm("in", l, win, 1024, 4, 8, h_rhs, conv_consume)
                    cut(5)

                    def sq_consume(ci, pst, psb):
                        act(arena[0:64, A_SQ + ci, :], pst[0:64, :], AF.Copy, [psb], [AB[A_SQ + ci]], scale=0.125)
                        P.op("pool", lambda e, ci=ci: e.memset(arena[64:128, A_SQ + ci, :], 0.0), writes=[AB[A_SQ + ci]])
                        act(arena[64:128, A_SQN + ci, :], pst[64:128, :], AF.Copy, [psb], [AB[A_SQN + ci]], scale=0.125)
                        P.op("pool", lambda e, ci=ci: e.memset(arena[0:64, A_SQN + ci, :], 0.0), writes=[AB[A_SQN + ci]])

                    linear_fm("in", l, win, 1536, 4, 8, h_rhs, sq_consume)
                    cut(51)

                    def sk_consume(ci, pst, psb):
                        if ci % 2 == 0:
                            act(kc[:, ci, t0:t0 + TG], pst[:, :], AF.Copy, [psb], [KCB[ci][g]])
                        else:
                            cp("dve", kc[:, ci, t0:t0 + TG], pst[:, :], [psb], [KCB[ci][g]])

                    linear_fm("in", l, win, 2048, 4, 8, h_rhs, sk_consume)
                    cut(52)

                    def sv_consume(tb, pst, psb):
                        if tb % 2 == 0:
                            act(vc[:, g * 4 + tb, :], pst[:, :], AF.Copy, [psb], [VCB[g * 4 + tb]])
                        else:
                            cp("dve", vc[:, g * 4 + tb, :], pst[:, :], [psb], [VCB[g * 4 + tb]])

                    linear_tm("in", l, win, 2560, sv_consume)
                    cut(6)

                    def rot_consume(a0, fc, fs):
                        def f(ci, pst, psb):
                            t1, t1b = ftile()
                            t2, t2b = ftile()
                            tt("dve", t1[:, 0:512], pst[:, :], rot_t[:, fc, :], ALU.mult, [psb, ROTB], [t1b])
                            tt("dve", t2[0:64, 0:512], pst[64:128, :], rot_t[64:128, fs, :], ALU.mult, [psb, ROTB], [t2b])
                            tt("dve", t2[64:128, 0:512], pst[0:64, :], rot_t[0:64, fs, :], ALU.mult, [psb, ROTB], [t2b])
                            tt("pool", arena[:, a0 + ci, :], t1[:, 0:512], t2[:, 0:512], ALU.add, [t1b, t2b], [AB[a0 + ci]])
                        return f

                    linear_fm("in", l, win, 3072, 4, 8, h_rhs, rot_consume(A_RQ, 0, 1))
                    linear_fm("in", l, win, 3584, 4, 8, h_rhs, rot_consume(A_RK, 0, 1))

                    rv_t, rv_b = arena_tm(A_RV)
                    gs_t, gs_b = arena_tm(A_GS)

                    def rv_consume(tb, pst, psb):
                        if tb % 2 == 0:
                            act(rv_t[:, tb, :], pst[:, :], AF.Copy, [psb], [rv_b[tb]])
                        else:
                            cp("dve", rv_t[:, tb, :], pst[:, :], [psb], [rv_b[tb]])

                    linear_tm("in", l, win, 4096, rv_consume)

                    def rg_consume(tb, pst, psb):
                        t1, t1b = ftile()
                        act(t1[:, 0:512], pst[:, :], AF.Silu, [psb], [t1b])
                        tt("pool", gs_t[:, tb, :], t1[:, 0:512], rgain[:], ALU.mult, [t1b, RGB], [gs_b[tb]])

                    linear_tm("in", l, win, 4608, rg_consume)
                    cut(7)

                    for c in (range(4) if 'ret' not in DISABLE else []):
                        cs = slice(c * 128, (c + 1) * 128)
                        S, Sb = rbank()
                        for hr in range(4):
                            mm(S[:, hr * 128:(hr + 1) * 128], arena[:, A_RK + hr, cs], arena[:, A_RQ + hr, cs], True, True,
                               [AB[A_RK + hr], AB[A_RQ + hr]], [Sb])
                        sm, smb = btile()
                        tt("dve", sm[:, :], S[:, :], cf[:, CF_DECAY:CF_DECAY + 512], ALU.mult, [Sb, CFB], [smb])
                        qd, qdb = btile()
                        for hr in range(4):
                            hs = slice(hr * 128, (hr + 1) * 128)
                            tt("pool", qd[:, hs], arena[:, A_RQ + hr, cs], cf[:, CF_QDEC + hr * 128:CF_QDEC + (hr + 1) * 128],
                               ALU.mult, [AB[A_RQ + hr], CFB], [qdb])
                        O, Ob = rbank()
                        for hr in range(4):
                            hs = slice(hr * 128, (hr + 1) * 128)
                            mm(O[:, hs], sm[:, hs], rv_t[:, c, hs], True, False, [smb, rv_b[c]], [Ob])
                            mm(O[:, hs], qd[:, hs], state_bf[:, hr, :], False, True, [qdb, STBB], [Ob])
                        for hr in range(4):
                            hs = slice(hr * 128, (hr + 1) * 128)
                            P.op("dve", lambda e, hr=hr, hs=hs, O=O: e.bn_stats(out=small[:, hr * 6:hr * 6 + 6], in_=O[:, hs]), reads=[Ob], writes=[SMB])
                        for hr in range(4):
                            P.op("dve", lambda e, hr=hr: e.bn_aggr(out=small[:, 24 + hr * 2:26 + hr * 2], in_=small[:, hr * 6:hr * 6 + 6]), reads=[SMB], writes=[SMB])
                        for hr in range(4):
                            act(small[:, 32 + hr:33 + hr], small[:, 25 + hr * 2:26 + hr * 2], AF.Ln, [SMB], [SMB], bias=EPS)
                        act(small[:, 32:36], small[:, 32:36], AF.Exp, [SMB], [SMB], scale=-0.5)
                        on, onb = ftile()
                        for hr in range(4):
                            hs = slice(hr * 128, (hr + 1) * 128)
                            ts("dve", on[:, hs], O[:, hs], small[:, 24 + hr * 2:25 + hr * 2], small[:, 32 + hr:33 + hr],
                               ALU.subtract, ALU.mult, [Ob, SMB], [onb])
                        rt_, rtb = btile()
                        tt("pool", rt_[:, :], on[:, 0:512], gs_t[:, c, :], ALU.mult, [onb, gs_b[c]], [rtb])
                        tp, tpb = tbank()
                        for hr in range(4):
                            hs = slice(hr * 128, (hr + 1) * 128)
                            tr(tp[:, hs], rt_[:, hs], [rtb], [tpb])
                        P.op("dve", lambda e, tp=tp, cs=cs: e.tensor_copy(out=brT[:, BR_RET:BR_RET + 4, cs],
                                                                       in_=tp.rearrange("p (h n) -> p h n", n=128)),
                             reads=[tpb], writes=BRB[BR_RET:BR_RET + 4])
                        tp2, tp2b = tbank()
                        for hr in range(4):
                            hs = slice(hr * 128, (hr + 1) * 128)
                            tr(tp2[:, hs], arena[:, A_RK + hr, cs], [AB[A_RK + hr]], [tp2b])
                        kdk, kdkb = btile()
                        for hr in range(4):
                            hs = slice(hr * 128, (hr + 1) * 128)
                            ts("dve", kdk[:, hs], tp2[:, hs], cf[:, CF_KDEC + hr:CF_KDEC + hr + 1], None, ALU.mult, None, [tp2b, CFB], [kdkb])
                        KV, KVb = rbank()
                        for hr in range(4):
                            hs = slice(hr * 128, (hr + 1) * 128)
                            mm(KV[:, hs], kdk[:, hs], rv_t[:, c, hs], True, True, [kdkb, rv_b[c]], [KVb])
                        for hr in range(4):
                            hs = slice(hr * 128, (hr + 1) * 128)
                            stt(state[:, hr, :], state[:, hr, :], cd[hr], KV[:, hs], ALU.mult, ALU.add, [STB, KVb], [STB])
                        cp("pool", state_bf[:], state[:], [STB], [STBB])

                    nblk = 4 * g + 4
                    for hp in (range(4) if 'sb' not in DISABLE else []):
                        Bt = [PB[B_BANKS[0]], PB[B_BANKS[1]]]
                        Ot = [PB[OUT_BANKS[0]], PB[OUT_BANKS[1]]]
                        for hh in range(2):
                            mm(Bt[hh][0][:, :], ones_bf, zeros[:], True, False, [CBB, ZB], [Bt[hh][1]])
                            mm(Ot[hh][0][:, :], ones_bf, zeros[:], True, False, [CBB, ZB], [Ot[hh][1]])
                        items = list(reversed(range(nblk)))
                        n_it = len(items)
                        ctx = [None] * n_it

                        def S1(s):
                            j = items[s]
                            jj = j - 4 * g
                            c0 = max(jj, 0) * 128
                            kT = kc[:, hp, j * 128:(j + 1) * 128]
                            kb = KCB[hp][j // 4]
                            cs_ = []
                            for hh in range(2):
                                qa = (A_SQ if hh == 0 else A_SQN) + hp
                                Z, Zb = rbank()
                                mm(Z[:, c0:512], kT, arena[:, qa, c0:512], True, True, [kb, AB[qa]], [Zb])
                                cs_.append(dict(j=j, hh=hh, c0=c0, Z=Z, Zb=Zb))
                            for c in cs_:
                                e_, eb = ftile()
                                act(e_[:, c0:512], c["Z"][:, c0:512], AF.Exp, [c["Zb"]], [eb])
                                if jj >= 0:
                                    tt("pool", e_[:, c0:c0 + 128], e_[:, c0:c0 + 128], cf[:, CF_MASK:CF_MASK + 128], ALU.mult, [eb, CFB], [eb])
                                c["e"] = e_
                                c["eb"] = eb
                            ctx[s] = cs_

                        def S1b(s):
                            for c in ctx[s]:
                                c0 = c["c0"]
                                l_, lb = btile()
                                act(l_[:, c0:512], c["e"][:, c0:512], AF.Ln, [c["eb"]], [lb], bias=1.0)
                                c["l"] = l_
                                c["lb"] = lb

                        def S2(s):
                            for c in ctx[s]:
                                B_, Bb = Bt[c["hh"]]
                                c0 = c["c0"]
                                mm(B_[:, c0:512], lincl, c["l"][:, c0:512], False, False, [CBB, c["lb"]], [Bb])
                            for c in ctx[s]:
                                B_, Bb = Bt[c["hh"]]
                                c0 = c["c0"]
                                en, enb = ftile()
                                act(en[:, c0:512], B_[:, c0:512], AF.Exp, [Bb], [enb], scale=-1.0)
                                a_, ab_ = btile()
                                tt("dve", a_[:, c0:512], c["e"][:, c0:512], en[:, c0:512], ALU.mult, [c["eb"], enb], [ab_])
                                c["a"] = a_
                                c["ab"] = ab_

                        def S3(s):
                            for c in ctx[s]:
                                B_, Bb = Bt[c["hh"]]
                                O_, Ob_ = Ot[c["hh"]]
                                c0 = c["c0"]
                                j = c["j"]
                                mm(B_[:, c0:512], lcomp, c["l"][:, c0:512], False, False, [CBB, c["lb"]], [Bb])
                                mm(O_[:, c0:512], vc[:, j, hp * 128:(hp + 1) * 128], c["a"][:, c0:512], False, j == 0,
                                   [VCB[j], c["ab"]], [Ob_])
                            ctx[s] = None

                        for s_ in range(n_it + 2):
                            if s_ < n_it:
                                S1(s_)
                            if 0 <= s_ - 2 < n_it:
                                S3(s_ - 2)
                            if 0 <= s_ - 1 < n_it:
                                S2(s_ - 1)
                            if s_ < n_it:
                                S1b(s_)
                        act(brT[0:64, BR_SB + hp, :], Ot[0][0][0:64, :], AF.Copy, [Ot[0][1]], [BRB[BR_SB + hp]])
                        act(brT[64:128, BR_SB + hp, :], Ot[1][0][64:128, :], AF.Copy, [Ot[1][1]], [BRB[BR_SB + hp]])

                    for dg in (range(2) if 'merge' not in DISABLE else []):
                        for n in range(3):
                            def gate_consume(ci, pst, psb, n=n):
                                bc = pv_off(l, "bgate") + n * 8 + dg * 4 + ci
                                act(arena[:, A_G + n * 4 + ci, :], pst[:, :], AF.Sigmoid, [psb, PVB], [AB[A_G + n * 4 + ci]],
                                    bias=pv[:, bc:bc + 1])
                            linear_fm("in", l, win, 5120 + n * 1024 + dg * 512, 4, 8, h_rhs, gate_consume)
                        mt = [(marena[:, i * 512:(i + 1) * 512], None) for i in range(4)]
                        for n in range(3):
                            def up_consume(ci, pst, psb, n=n):
                                m_ = mt[ci][0]
                                mbs = [AB[20 + 2 * ci], AB[21 + 2 * ci]]
                                gsl = arena[:, A_G + n * 4 + ci, :]
                                gb = AB[A_G + n * 4 + ci]
                                if n == 0:
                                    tt("dve", m_[:, 0:512], pst[:, :], gsl, ALU.mult, [psb, gb], mbs)
                                else:
                                    t1, t1b = ftile()
                                    tt("dve", t1[:, 0:512], pst[:, :], gsl, ALU.mult, [psb, gb], [t1b])
                                    if n == 1:
                                        tt("pool", m_[:, 0:512], m_[:, 0:512], t1[:, 0:512], ALU.add, mbs + [t1b], mbs)
                                    else:
                                        d = dg * 4 + ci
                                        tt("pool", arena[:, A_MG + d, :], m_[:, 0:512], t1[:, 0:512], ALU.add, mbs + [t1b], [AB[A_MG + d]])
                            br_rhs = lambda k, n=n: (brT[:, n * 4 + k, :], BRB[n * 4 + k])
                            linear_fm("br", l, wb_br[l, n], dg * 512, 4, 4, br_rhs, up_consume)

                    def resid_consume(ci, pst, psb):
                        tt("dve", xt[:, ci, :], pst[:, :], xt[:, ci, :], ALU.add, [psb, XB[ci]], [XB[ci]])

                    mg_rhs = lambda k: (arena[:, A_MG + k, :], AB[A_MG + k])
                    if "merge" not in DISABLE:
                        linear_fm("o", l, wb_o[l], 0, 8, 8, mg_rhs, resid_consume)

                    if 'xattn' not in DISABLE:
                        rmsnorm(xt, XB, TG, pv_off(l, "nxa"), ht, HB)

                        def q2_consume(ci, pst, psb):
                            act(arena[:, A_Q2 + ci, :], pst[:, :], AF.Copy, [psb], [AB[A_Q2 + ci]], scale=1.0 / 16.0)

                        linear_fm("xq", l, wb_xq[l], 0, 8, 8, h_rhs, q2_consume)
                        for hx in range(4):
                            pts = []
                            for mb in range(2):
                                SC, SCb = rbank()
                                for kk in range(2):
                                    mm(SC[:, :], kmem[:, 2 * hx + kk, mb * 128:(mb + 1) * 128], arena[:, A_Q2 + 2 * hx + kk, :], kk == 0, kk == 1,
                                       [KMB, AB[A_Q2 + 2 * hx + kk]], [SCb])
                                p_, pb_ = btile()
                                act(p_[:, :], SC[:, :], AF.Exp, [SCb], [pb_])
                                pts.append((p_, pb_))
                            DEN, DENb = rbank()
                            for mb in range(2):
                                mm(DEN[:, :], ones_bf, pts[mb][0][:, :], mb == 0, mb == 1, [CBB, pts[mb][1]], [DENb])
                            rd_, rdb = ftile()
                            P.op("dve", lambda e, rd_=rd_, DEN=DEN: e.reciprocal(out=rd_[:, 0:512], in_=DEN[:, :]), reads=[DENb], writes=[rdb])
                            for kk in range(2):
                                OT, OTb = rbank()
                                dch = 2 * hx + kk
                                for mb in range(2):
                                    mm(OT[:, :], vmem[:, mb, dch * 128:(dch + 1) * 128], pts[mb][0][:, :], mb == 0, mb == 1, [VMB, pts[mb][1]], [OTb])
                                tt("dve", arena[:, A_OT + dch, :], OT[:, :], rd_[:, 0:512], ALU.mult, [OTb, rdb], [AB[A_OT + dch]])
                        ot_rhs = lambda k: (arena[:, A_OT + k, :], AB[A_OT + k])
                        linear_fm("xo", l, wb_xo[l], 0, 8, 8, ot_rhs, resid_consume)

                    if 'mlp' not in DISABLE:
                        rmsnorm(xt, XB, TG, pv_off(l, "nmlp"), ht, HB)

                        def up_mlp_consume(ci, pst, psb):
                            t1, t1b = ftile()
                            act(t1[:, 0:512], pst[:, :], AF.Relu, [psb], [t1b])
                            tt("pool", arena[:, ci, :], t1[:, 0:512], t1[:, 0:512], ALU.mult, [t1b], [AB[ci]])

                        linear_fm("up", l, wb_up[l], 0, 32, 8, h_rhs, up_mlp_consume)
                        for d in range(8):
                            slot, wsb = wslot()
                            view = slot[:, :].rearrange("p (c n) -> p c n", n=128)
                            load("sp", slot[:, :], wb_dn[l, d], wsb, R=[WB[("dn", l)]])
                            pst, psb = rbank()
                            for k in range(32):
                                mm(pst[:, :], view[:, k, :], arena[:, k, :], k == 0, k == 31, [wsb, AB[k]], [psb])
                            resid_consume(d, pst, psb)

                    if l == L - 1:
                        rmsnorm_final(P, xt, XB, sq, SQB, None, None, pv, PVB, L * PV_L, rbank, mm, act, stt, tt, ones_bf, CBB,
                                      ftile, out_d, t0, load)
                    else:
                        P.dma("pool", lambda e, xt=xt, t0=t0: [e.dma_start(out=xs_d.rearrange("(c p) n -> p c n", p=128)[:, :, t0:t0 + TG], in_=xt[:])],
                              XB[0], reads=XB, writes=[XSD[g]])


        except StopBuild:
            pass
        P.op("sp", lambda e: None, reads=OUTD)
        P.generate(block)
        build.stats = (P.stats, P.sbytes)
    return nc


OUTD = []


def rmsnorm_final(P, xt, XB, sq, SQB, _u1, _u2, pv, PVB, gcol, rbank, mm, act, stt, tt, ones_bf, CBB, ftile, out_d, t0, load):
    for c in range(8):
        tt("pool", sq[:, c, :], xt[:, c, :], xt[:, c, :], ALU.mult, [XB[c]], [SQB[c]])
    pst, psb = rbank()
    rstd_f, RSB = ftile()
    rstd = rstd_f[:, 0:512]
    for c in range(8):
        mm(pst[:, :], ones_bf, sq[:, c, :], c == 0, c == 7, [SQB[c], CBB], [psb])
    act(rstd, pst[:, :], AF.Ln, [psb], [RSB], scale=1.0 / 1024.0, bias=EPS)
    act(rstd, rstd, AF.Exp, [RSB], [RSB], scale=-0.5)
    for c in range(8):
        o, ob = ftile()
        if ob is RSB:
            o, ob = ftile()
        stt(o[:, 0:512], xt[:, c, :], pv[:, gcol + c:gcol + c + 1], rstd, ALU.mult, ALU.mult, [XB[c], PVB, RSB], [ob])
        od = P.buf("outd")
        P.dma("pool", lambda e, o=o, c=c: [e.dma_start(out=out_d[c * 128:(c + 1) * 128, t0:t0 + 512], in_=o[:, 0:512])],
              ob, reads=[ob], writes=[od])
        OUTD.append(od)


_CACHE = {}


def run(inputs, T, L=2, n_cores=8, active=ACTIVE_CORES):
    f32 = np.float32
    x = np.asarray(inputs["x"], f32)
    mem = np.asarray(inputs["mem"], f32)
    B = x.shape[0]
    cf, cb, rot, cd = host_constants(T)
    key = (T, L)
    OUTD.clear()
    nc = build(T, L, cd)
    pvec = host_pvec(inputs, L)
    rgain = np.ascontiguousarray(np.broadcast_to(np.asarray(inputs["ret_norm_g"], f32)[:, None, :], (L, 128, 512)))
    common = {
        "pv": pvec, "cf": cf, "cb": cb, "rot": rot, "rgain": rgain,
        "w_in": np.asarray(inputs["w_in"], f32), "w_branch": np.asarray(inputs["w_branch"], f32),
        "w_o": np.asarray(inputs["w_o"], f32), "w_xq": np.asarray(inputs["w_xq"], f32),
        "w_xkv": np.asarray(inputs["w_xkv"], f32), "w_xo": np.asarray(inputs["w_xo"], f32),
        "w_up": np.asarray(inputs["w_up"], f32), "w_down": np.asarray(inputs["w_down"], f32),
    }
    in_maps = []
    zx = np.zeros((1024, T), f32)
    zm = np.zeros((1024, MEM_LEN), f32)
    slot = {c: i for i, c in enumerate(active)}
    for c in range(n_cores):
        m = dict(common)
        if c in slot and slot[c] < B:
            b = slot[c]
            m["xT"] = np.ascontiguousarray(x[b].T)
            m["memT"] = np.ascontiguousarray(mem[b].T)
        else:
            m["xT"] = zx
            m["memT"] = zm
        in_maps.append(m)
    res = run_bass_kernel_spmd(nc, in_maps, core_ids=list(range(n_cores)))
    out = np.empty((B, T, 1024), f32)
    for c, b in slot.items():
        if b < B:
            out[b] = np.asarray(res.results[c]["outT"], f32).T
    return out


def kernel(**inputs):
    T = inputs["x"].shape[1]
    return run(inputs, T)
```

```python
import numpy as np
import ml_dtypes
import concourse.bass as bass
import concourse.mybir as mybir
from concourse.bass_utils import run_bass_kernel_spmd
from contextlib import ExitStack

F32 = mybir.dt.float32
BF16 = mybir.dt.bfloat16
ALU = mybir.AluOpType
AF = mybir.ActivationFunctionType

D_MODEL = 1024
MEM_LEN = 256
IN_COLS = 8192
D_FF = 4096
EPS = 1e-6
TG = 512
ACTIVE_CORES = (0, 1, 4, 5)
DISABLE = set()
CUT = 0


class StopBuild(Exception):
    pass


def cut(n):
    if CUT == n:
        raise StopBuild()


class Buf:
    __slots__ = ("name", "lw", "rd", "sem", "dcount", "last_dma")

    def __init__(self, name):
        self.name = name
        self.lw = None
        self.rd = []
        self.sem = None
        self.dcount = 0
        self.last_dma = None


class Inst:
    __slots__ = ("eng", "fn", "hard", "soft", "sig", "is_dma", "owner", "dval", "semval")

    def __init__(self, eng, fn, is_dma=False):
        self.eng = eng
        self.fn = fn
        self.hard = []
        self.soft = []
        self.sig = False
        self.is_dma = is_dma
        self.owner = None
        self.dval = 0
        self.semval = 0


class Prog:
    ENGS = ("pe", "act", "dve", "pool", "sp")

    def __init__(self, nc, es):
        self.nc = nc
        self.es = es
        self.q = {e: [] for e in self.ENGS}
        self.uid = 0
        self.sbytes = 0

    def sb(self, shape, dtype, name=None):
        self.uid += 1
        n = 1
        for s in shape[1:]:
            n *= s
        self.sbytes += n * (2 if dtype == BF16 else 4)
        return self.es.enter_context(self.nc.sbuf_tensor(name or f"sb{self.uid}", list(shape), dtype))

    def ps(self, shape, dtype=F32, name=None):
        self.uid += 1
        return self.es.enter_context(self.nc.psum_tensor(name or f"ps{self.uid}", list(shape), dtype))

    def buf(self, name=None):
        self.uid += 1
        return Buf(name or f"b{self.uid}")

    def new_sem(self, name):
        return self.es.enter_context(self.nc.semaphore(name))

    def _track(self, inst, reads, writes):
        wset = set(id(b) for b in writes)
        for b in reads:
            if b.lw is not None:
                inst.hard.append(b.lw)
        for b in writes:
            if b.lw is not None:
                inst.hard.append(b.lw)
            inst.soft.extend(b.rd)
        for b in reads:
            if id(b) not in wset:
                b.rd.append(inst)
        for b in writes:
            b.lw = inst
            b.rd = []

    def op(self, eng, fn, reads=(), writes=()):
        inst = Inst(eng, fn)
        self._track(inst, reads, writes)
        self.q[eng].append(inst)
        return inst

    def dma(self, queue, fn, owner, reads=(), writes=(), npieces=1):
        inst = Inst(queue, fn, is_dma=True)
        self._track(inst, reads, writes)
        if owner.last_dma is not None:
            inst.hard.append(owner.last_dma)
        owner.last_dma = inst
        if owner.sem is None:
            self.uid += 1
            owner.sem = self.new_sem(f"dsem{self.uid}")
        owner.dcount += 16 * npieces
        inst.owner = owner
        inst.dval = owner.dcount
        self.q[queue].append(inst)
        return inst

    def generate(self, block):
        esem = {e: self.new_sem(f"esem_{e}") for e in ("pe", "act", "dve", "pool")}
        for e in self.ENGS:
            for inst in self.q[e]:
                deps = []
                for p in inst.hard:
                    if p is inst:
                        continue
                    if p.is_dma or inst.is_dma or p.eng != inst.eng or inst.eng != "pe":
                        deps.append(p)
                for p in inst.soft:
                    if p is inst:
                        continue
                    if p.is_dma or inst.is_dma or p.eng != inst.eng:
                        deps.append(p)
                inst.hard = deps
                inst.soft = None
                for p in deps:
                    if not p.is_dma:
                        p.sig = True
        for e in ("pe", "act", "dve", "pool"):
            c = 0
            for inst in self.q[e]:
                if inst.is_dma:
                    continue
                if inst.sig:
                    c += 1
                    inst.semval = c
        self.stats = {}

        def gen(e, eng):
            waited = {}
            nw = 0
            for inst in self.q[e]:
                need = {}
                for p in inst.hard:
                    if p.is_dma:
                        s, v = p.owner.sem, p.dval
                    else:
                        s, v = esem[p.eng], p.semval
                    k = id(s)
                    if waited.get(k, 0) >= v:
                        continue
                    if k not in need or need[k][1] < v:
                        need[k] = (s, v)
                for k, (s, v) in need.items():
                    eng.wait_ge(s, v)
                    waited[k] = v
                    nw += 1
                r = inst.fn(eng)
                if inst.is_dma:
                    for h in r:
                        h.then_inc(inst.owner.sem, 16)
                elif inst.sig:
                    r.then_inc(esem[e], 1)
            self.stats[e] = (len(self.q[e]), nw)

        @block.sync
        def _(eng):
            gen("sp", eng)

        @block.scalar
        def _(eng):
            gen("act", eng)

        @block.vector
        def _(eng):
            gen("dve", eng)

        @block.gpsimd
        def _(eng):
            gen("pool", eng)

        @block.tensor
        def _(eng):
            gen("pe", eng)


PV_L = 8 * 4 + 24 + 12 + 4


def pv_off(l, name):
    base = l * PV_L
    offs = {"nmix": 0, "nxa": 8, "nmem": 16, "nmlp": 24, "bgate": 32, "convw": 56, "convb": 68}
    return base + offs[name]


def host_constants(T):
    f32 = np.float32
    idx = np.arange(128, dtype=f32)
    log_gamma = np.log1p(-np.exp2(-5.0 - np.arange(4, dtype=f32))).astype(f32)
    j = idx[:, None]
    i = idx[None, :]
    decayT = np.zeros((128, 4, 128), f32)
    for h in range(4):
        rel = i - j
        decayT[:, h, :] = np.where(rel >= 0, np.exp(np.maximum(rel, 0.0) * log_gamma[h]), 0.0)
    ks = f32(128.0 ** -0.5)
    decayT = (decayT * ks).astype(f32)
    qdec = np.zeros((128, 4, 128), f32)
    for h in range(4):
        qdec[:, h, :] = np.exp((idx + 1) * log_gamma[h])[None, :]
    mask = (j < i).astype(f32)
    kdecay = (np.exp((127 - idx)[:, None] * log_gamma[None, :]) * ks).astype(f32)
    cf = np.concatenate([decayT.reshape(128, 512), qdec.reshape(128, 512), mask, kdecay], axis=1).astype(f32)
    cd = [float(np.exp(f32(128.0) * log_gamma[h])) for h in range(4)]
    ident = np.eye(128, dtype=f32)
    ones = np.ones((128, 128), f32)
    lincl = (j >= i).astype(f32)
    lc = 1.0 - lincl
    cb = np.concatenate([ident, ones, lincl, lc, mask], axis=1).astype(ml_dtypes.bfloat16)
    pos = np.arange(T, dtype=f32)
    inv_freq = (f32(10000.0) ** (-np.arange(0, 128, 2, dtype=f32) / f32(128))).astype(f32)
    ang = (pos[:, None] * inv_freq[None, :]).astype(f32)
    cos = np.cos(ang).astype(f32).T
    sin = np.sin(ang).astype(f32).T
    cosf = np.concatenate([cos, cos], 0)
    sinf = np.concatenate([sin, -sin], 0)
    rot = np.stack([cosf, sinf], 0).astype(f32)
    return cf, cb, rot, cd


CF_DECAY, CF_QDEC, CF_MASK, CF_KDEC, CF_N = 0, 512, 1024, 1024 + 128, 1024 + 128 + 4
CB_ID, CB_ONES, CB_LINCL, CB_LC, CB_MASK, CB_N = 0, 128, 256, 384, 512, 640


def col8(v):
    return np.ascontiguousarray(v.reshape(-1, 128).T)


def host_pvec(inp, L):
    cols = []
    for l in range(L):
        cols += [col8(inp["norm_mix_g"][l]), col8(inp["norm_xa_g"][l]), col8(inp["norm_mem_g"][l]),
                 col8(inp["norm_mlp_g"][l])]
        cols.append(np.concatenate([col8(inp["b_gate"][l][n]) for n in range(3)], axis=1))
        cols.append(np.concatenate([col8(inp["conv_w"][l][jj]) for jj in range(3)], axis=1))
        cols.append(col8(inp["conv_b"][l]))
    cols.append(col8(inp["final_g"]))
    return np.ascontiguousarray(np.concatenate(cols, axis=1).astype(np.float32))


def build(T, L, cd, dbg=False):
    NG = T // TG
    NB = T // 128
    nc = bass.Bass("TRN2", target_bir_lowering=False)

    def din(name, shape, dt=F32):
        return nc.dram_tensor(name, list(shape), dt, kind="ExternalInput").ap()

    xT_d = din("xT", [1024, T])
    memT_d = din("memT", [1024, MEM_LEN])
    pv_d = din("pv", [128, L * PV_L + 8])
    cf_d = din("cf", [128, CF_N])
    cb_d = din("cb", [128, CB_N], BF16)
    rot_d = din("rot", [2, 128, T])
    rgain_d = din("rgain", [L, 128, 512])
    w_in_d = din("w_in", [L, 1024, IN_COLS])
    w_br_d = din("w_branch", [L, 3, 512, 1024])
    w_o_d = din("w_o", [L, 1024, 1024])
    w_xq_d = din("w_xq", [L, 1024, 1024])
    w_xkv_d = din("w_xkv", [L, 1024, 2048])
    w_xo_d = din("w_xo", [L, 1024, 1024])
    w_up_d = din("w_up", [L, 1024, D_FF])
    w_dn_d = din("w_down", [L, D_FF, 1024])
    out_d = nc.dram_tensor("outT", [1024, T], F32, kind="ExternalOutput").ap()

    def dint(name, shape, dt=BF16):
        return nc.dram_tensor(name, list(shape), dt).ap()

    wb_in = dint("wb_in", [L, 1024, IN_COLS])
    wb_br = dint("wb_br", [L, 3, 512, 1024])
    wb_o = dint("wb_o", [L, 1024, 1024])
    wb_xq = dint("wb_xq", [L, 1024, 1024])
    wb_xkv = dint("wb_xkv", [L, 1024, 2048])
    wb_xo = dint("wb_xo", [L, 1024, 1024])
    wb_up = dint("wb_up", [L, 1024, D_FF])
    wb_dn = dint("wb_dn", [L, 8, 128, 32 * 128])
    xs_d = dint("xs", [1024, T], F32)

    with ExitStack() as es:
        P = Prog(nc, es)
        block = es.enter_context(nc.Block())

        pv = P.sb([128, L * PV_L + 8], F32); PVB = P.buf("pv")
        cf = P.sb([128, CF_N], F32); CFB = P.buf("cf")
        cb = P.sb([128, CB_N], BF16); CBB = P.buf("cb")
        zeros = P.sb([128, 512], BF16); ZB = P.buf("zeros")
        rgain = P.sb([128, 512], F32); RGB = P.buf("rgain")
        NW = 3
        wslots = [(P.sb([128, 4096], BF16), P.buf(f"ws{i}")) for i in range(NW)]
        xts = [(P.sb([128, 8, 512], F32), [P.buf(f"x{i}_{c}") for c in range(8)]) for i in range(1)]
        ht = P.sb([128, 8, 512], BF16); HB = [P.buf(f"h{c}") for c in range(8)]
        NA = 32
        arena = P.sb([128, NA, 512], BF16); AB = [P.buf(f"ar{i}") for i in range(NA)]
        sq = arena[:, 0:8, :]; SQB = AB[0:8]
        marena = arena[:, 20:28, :].rearrange("p a n -> p (a n)").bitcast(F32)
        brT = P.sb([128, 12, 512], BF16); BRB = [P.buf(f"br{i}") for i in range(12)]
        kc = P.sb([128, 4, T], BF16); KCB = [[P.buf(f"kc{c}_{g}") for g in range(NG)] for c in range(4)]
        vc = P.sb([128, NB, 512], BF16); VCB = [P.buf(f"vc{b}") for b in range(NB)]
        rot_t = P.sb([128, 2, 512], F32); ROTB = P.buf("rot")
        kmem = P.sb([128, 8, 256], BF16); KMB = P.buf("kmem")
        vmem = P.sb([128, 2, 1024], BF16); VMB = P.buf("vmem")
        state = P.sb([128, 4, 128], F32); STB = P.buf("state")
        state_bf = P.sb([128, 4, 128], BF16); STBB = P.buf("statebf")
        halo = P.sb([128, 4, 2], F32); HALB = [P.buf(f"halo{c}") for c in range(4)]
        NF = 8
        ftiles = [(P.sb([128, 514], F32), P.buf(f"ft{i}")) for i in range(NF)]
        NBT = 10
        btiles = [(P.sb([128, 512], BF16), P.buf(f"bt{i}")) for i in range(NBT)]
        small = P.sb([128, 64], F32); SMB = P.buf("small")

        PB = [(P.ps([128, 512], F32), P.buf(f"pb{i}")) for i in range(7)]
        tb_ps = P.ps([128, 1024], BF16); _tb = P.buf("tb"); TBB = [_tb, _tb]
        ROT_BANKS = [0, 1, 2]
        B_BANKS = [3, 4]
        OUT_BANKS = [5, 6]
        st = {"rb": 0, "ws": 0, "ft": 0, "bt": 0, "tb": 0}

        def rbank():
            i = ROT_BANKS[st["rb"] % len(ROT_BANKS)]
            st["rb"] += 1
            return PB[i]

        def wslot():
            s = wslots[st["ws"] % NW]
            st["ws"] += 1
            return s

        def ftile():
            s = ftiles[st["ft"] % NF]
            st["ft"] += 1
            return s

        def btile():
            s = btiles[st["bt"] % NBT]
            st["bt"] += 1
            return s

        def tbank():
            i = st["tb"] % 2
            st["tb"] += 1
            return tb_ps[:, i * 512:(i + 1) * 512], TBB[i]

        ident = cb[:, CB_ID:CB_ID + 128]
        ones_bf = cb[:, CB_ONES:CB_ONES + 128]
        lincl = cb[:, CB_LINCL:CB_LINCL + 128]
        lcomp = cb[:, CB_LC:CB_LC + 128]
        mask_bf = cb[:, CB_MASK:CB_MASK + 128]

        def mm(out, lhsT, rhs, start, stop, R, Wb):
            P.op("pe", lambda e: e.matmul(out, lhsT=lhsT, rhs=rhs, start=start, stop=stop), reads=R, writes=Wb)

        def tr(out, in_, R, Wb):
            P.op("pe", lambda e: e.transpose(out, in_, ident), reads=R + [CBB], writes=Wb)

        def act(out, in_, func, R, Wb, scale=None, bias=None):
            kw = {}
            if scale is not None:
                kw["scale"] = scale
            if bias is not None:
                kw["bias"] = bias
            P.op("act", lambda e: e.activation(out=out, in_=in_, func=func, **kw), reads=R, writes=Wb)

        def tt(eng, out, in0, in1, op, R, Wb):
            P.op(eng, lambda e: e.tensor_tensor(out=out, in0=in0, in1=in1, op=op), reads=R, writes=Wb)

        def ts(eng, out, in0, s1, s2, op0, op1, R, Wb):
            if op1 is None:
                P.op(eng, lambda e: e.tensor_scalar(out=out, in0=in0, scalar1=s1, scalar2=None, op0=op0), reads=R, writes=Wb)
            else:
                P.op(eng, lambda e: e.tensor_scalar(out=out, in0=in0, scalar1=s1, scalar2=s2, op0=op0, op1=op1), reads=R, writes=Wb)

        def stt(out, in0, scalar, in1, op0, op1, R, Wb):
            P.op("dve", lambda e: e.scalar_tensor_tensor(out=out, in0=in0, scalar=scalar, in1=in1, op0=op0, op1=op1), reads=R, writes=Wb)

        def cp(eng, out, in_, R, Wb):
            P.op(eng, lambda e: e.tensor_copy(out=out, in_=in_), reads=R, writes=Wb)

        def load(queue, out, in_, owner, R=(), Wb=None):
            P.dma(queue, lambda e: [e.dma_start(out=out, in_=in_)], owner, reads=list(R), writes=[owner] if Wb is None else Wb)

        load("sp", pv[:], pv_d, PVB)
        load("sp", cf[:], cf_d, CFB)
        load("sp", cb[:], cb_d, CBB)
        P.op("dve", lambda e: e.memset(zeros[:], 0.0), writes=[ZB])

        WB = {}

        PCB = P.buf("precast_chain")

        def precast(name, l, pieces):
            b = P.buf(f"wb_{name}{l}")
            for (o, i) in pieces:
                P.dma("pool", lambda e, o=o, i=i: [e.dma_start(out=o, in_=i)], PCB, writes=[b])
            WB[(name, l)] = b

        for l in range(L):
            precast("in", l, [(wb_in[l, r * 128:(r + 1) * 128, :], w_in_d[l, r * 128:(r + 1) * 128, :]) for r in range(8)])
            precast("xkv", l, [(wb_xkv[l, r * 256:(r + 1) * 256, :], w_xkv_d[l, r * 256:(r + 1) * 256, :]) for r in range(4)])
            precast("br", l, [(wb_br[l, n], w_br_d[l, n]) for n in range(3)])
            precast("o", l, [(wb_o[l, r * 512:(r + 1) * 512, :], w_o_d[l, r * 512:(r + 1) * 512, :]) for r in range(2)])
            precast("xq", l, [(wb_xq[l, r * 512:(r + 1) * 512, :], w_xq_d[l, r * 512:(r + 1) * 512, :]) for r in range(2)])
            precast("xo", l, [(wb_xo[l, r * 512:(r + 1) * 512, :], w_xo_d[l, r * 512:(r + 1) * 512, :]) for r in range(2)])
            precast("up", l, [(wb_up[l, r * 256:(r + 1) * 256, :], w_up_d[l, r * 256:(r + 1) * 256, :]) for r in range(4)])
            precast("dn", l, [(wb_dn[l, d].rearrange("p (c n) -> p c n", n=128),
                               w_dn_d[l, :, d * 128:(d + 1) * 128].rearrange("(c p) n -> p c n", p=128)) for d in range(8)])

        def wload_cols(name, l, src2d, c0, ncols, kchunks):
            slot, sb_ = wslot()
            view = slot[:, 0:kchunks * ncols].rearrange("p (c n) -> p c n", n=ncols)
            src = src2d.rearrange("(c p) n -> p c n", p=128)[:, :, c0:c0 + ncols]
            load("sp", view, src, sb_, R=[WB[(name, l)]])
            return view, sb_

        def rmsnorm(xt, XB, ncol, gcol, out_t, OB, out_is_f32=False):
            for c in range(8):
                tt("pool", sq[:, c, 0:ncol], xt[:, c, :], xt[:, c, :], ALU.mult, [XB[c]], [SQB[c]])
            pst, psb = rbank()
            rstd, RSB = ftile()
            for c in range(8):
                mm(pst[:, 0:ncol], ones_bf, sq[:, c, 0:ncol], c == 0, c == 7, [SQB[c], CBB], [psb])
            act(rstd[:, 0:ncol], pst[:, 0:ncol], AF.Ln, [psb], [RSB], scale=1.0 / 1024.0, bias=EPS)
            act(rstd[:, 0:ncol], rstd[:, 0:ncol], AF.Exp, [RSB], [RSB], scale=-0.5)
            for c in range(8):
                stt(out_t[:, c, :], xt[:, c, :], pv[:, gcol + c:gcol + c + 1], rstd[:, 0:ncol], ALU.mult, ALU.mult,
                    [XB[c], PVB, RSB], [OB[c]])

        def linear_fm(name, l, src2d, col0, nchunks, kchunks, rhs_fn, consume):
            done = 0
            while done < nchunks:
                n = min(4, nchunks - done)
                view, wsb = wload_cols(name, l, src2d, col0 + done * 128, n * 128, kchunks)
                for cc in range(n):
                    pst, psb = rbank()
                    for k in range(kchunks):
                        r_ap, r_b = rhs_fn(k)
                        mm(pst[:, :], view[:, k, cc * 128:(cc + 1) * 128], r_ap, k == 0, k == kchunks - 1, [wsb, r_b], [psb])
                    consume(done + cc, pst, psb)
                done += n

        def linear_tm(name, l, src2d, col0, consume):
            view, wsb = wload_cols(name, l, src2d, col0, 512, 8)
            for tb in range(4):
                pst, psb = rbank()
                for k in range(8):
                    mm(pst[:, :], ht[:, k, tb * 128:(tb + 1) * 128], view[:, k, :], k == 0, k == 7, [wsb, HB[k]], [psb])
                consume(tb, pst, psb)

        h_rhs = lambda k: (ht[:, k, :], HB[k])

        A_CB, A_CC, A_SQ, A_SQN, A_RQ, A_RK = 0, 4, 8, 12, 16, 20
        A_RV, A_GS = 24, 28
        A_G, A_MG = 0, 12
        A_Q2, A_OT = 16, 24
        BR_CONV, BR_SB, BR_RET = 0, 4, 8

        def arena_tm(a0):
            return arena[:, a0:a0 + 4, :], AB[a0:a0 + 4]

        XSD = [[P.buf(f"xsd{i}_{c}") for c in range(8)] for i in range(NG)]

        try:
            cut(1)
            for l in range(L):
                load("sp", rgain[:], rgain_d[l], RGB)
                P.op("pool", lambda e: e.memset(state[:], 0.0), writes=[STB])
                P.op("pool", lambda e: e.memset(state_bf[:], 0.0), writes=[STBB])
                P.op("pool", lambda e: e.memset(halo[:], 0.0), writes=HALB)
                xt0, XB0 = xts[0]
                memx = xt0[:, :, 0:MEM_LEN]
                P.dma("sp", lambda e, memx=memx: [e.dma_start(out=memx, in_=memT_d.rearrange("(c p) n -> p c n", p=128))],
                      XB0[0], writes=XB0)
                rmsnorm(memx, XB0, MEM_LEN, pv_off(l, "nmem"), ht[:, :, 0:MEM_LEN], HB)
                def kmem_consume(ci, pst, psb):
                    cp("dve", kmem[:, ci, :], pst[:, 0:MEM_LEN], [psb], [KMB])
                done = 0
                for half in range(2):
                    view, wsb = wload_cols("xkv", l, wb_xkv[l], half * 512, 512, 8)
                    for cc in range(4):
                        pst, psb = rbank()
                        for k in range(8):
                            mm(pst[:, 0:MEM_LEN], view[:, k, cc * 128:(cc + 1) * 128], ht[:, k, 0:MEM_LEN], k == 0, k == 7, [wsb, HB[k]], [psb])
                        kmem_consume(half * 4 + cc, pst, psb)
                for half in range(2):
                    view, wsb = wload_cols("xkv", l, wb_xkv[l], 1024 + half * 512, 512, 8)
                    for mb in range(2):
                        pst, psb = rbank()
                        for k in range(8):
                            mm(pst[:, :], ht[:, k, mb * 128:(mb + 1) * 128], view[:, k, :], k == 0, k == 7, [wsb, HB[k]], [psb])
                        cp("dve", vmem[:, mb, half * 512:(half + 1) * 512], pst[:, :], [psb], [VMB])

                cut(2)
                for g in range(NG):
                    xt, XB = xts[0]
                    t0 = g * TG
                    src = xT_d if l == 0 else xs_d
                    for c in range(8):
                        rd = [] if l == 0 else [XSD[g][c]]
                        P.dma("sp", lambda e, xt=xt, src=src, t0=t0, c=c: [e.dma_start(out=xt[:, c, :], in_=src[c * 128:(c + 1) * 128, t0:t0 + TG])],
                              XB[c], reads=rd, writes=[XB[c]])
                    load("sp", rot_t[:], rot_d[:, :, t0:t0 + TG].rearrange("f p n -> p f n"), ROTB)

                    rmsnorm(xt, XB, TG, pv_off(l, "nmix"), ht, HB)
                    win = wb_in[l]
                    cut(3)

                    def ev_bf(a0, alt=False):
                        def f(ci, pst, psb):
                            if (ci % 2 == 0) != alt:
                                act(arena[:, a0 + ci, :], pst[:, :], AF.Copy, [psb], [AB[a0 + ci]])
                            else:
                                cp("dve", arena[:, a0 + ci, :], pst[:, :], [psb], [AB[a0 + ci]])
                        return f

                    linear_fm("in", l, win, 0, 4, 8, h_rhs, ev_bf(A_CB))
                    linear_fm("in", l, win, 512, 4, 8, h_rhs, ev_bf(A_CC, True))
                    cut(4)

                    def conv_consume(ci, pst, psb):
                        u, ub = ftile()
                        cp("pool", u[:, 0:2], halo[:, ci, :], [HALB[ci]], [ub])
                        tt("dve", u[:, 2:514], pst[:, :], arena[:, A_CC + ci, :], ALU.mult, [psb, AB[A_CC + ci]], [ub])
                        cp("pool", halo[:, ci, :], u[:, 512:514], [ub], [HALB[ci]])
                        a, ab_ = ftile()
                        cw = pv_off(l, "convw")
                        cbias = pv_off(l, "convb")
                        ts("pool", a[:, 0:512], u[:, 2:514], pv[:, cw + 8 + ci:cw + 9 + ci], pv[:, cbias + ci:cbias + ci + 1],
                           ALU.mult, ALU.add, [ub, PVB], [ab_])
                        t1, t1b = ftile()
                        ts("pool", t1[:, 0:512], u[:, 1:513], pv[:, cw + 4 + ci:cw + 5 + ci], None, ALU.mult, None, [ub, PVB], [t1b])
                        tt("pool", a[:, 0:512], a[:, 0:512], t1[:, 0:512], ALU.add, [ab_, t1b], [ab_])
                        t2, t2b = ftile()
                        ts("pool", t2[:, 0:512], u[:, 0:512], pv[:, cw + ci:cw + 1 + ci], None, ALU.mult, None, [ub, PVB], [t2b])
                        tt("pool", a[:, 0:512], a[:, 0:512], t2[:, 0:512], ALU.add, [ab_, t2b], [ab_])
                        tt("pool", brT[:, BR_CONV + ci, :], a[:, 0:512], arena[:, A_CB + ci, :], ALU.mult, [ab_, AB[A_CB + ci]], [BRB[BR_CONV + ci]])

                    linear_fm("in", l, win, 1024, 4, 8, h_rhs, conv_consume)
                    cut(5)

                    def sq_consume(ci, pst, psb):
                        act(arena[0:64, A_SQ + ci, :], pst[0:64, :], AF.Copy, [psb], [AB[A_SQ + ci]], scale=0.125)
                        P.op("pool", lambda e, ci=ci: e.memset(arena[64:128, A_SQ + ci, :], 0.0), writes=[AB[A_SQ + ci]])
                        act(arena[64:128, A_SQN + ci, :], pst[64:128, :], AF.Copy, [psb], [AB[A_SQN + ci]], scale=0.125)
                        P.op("pool", lambda e, ci=ci: e.memset(arena[0:64, A_SQN + ci, :], 0.0), writes=[AB[A_SQN + ci]])

                    linear_fm("in", l, win, 1536, 4, 8, h_rhs, sq_consume)
                    cut(51)

                    def sk_consume(ci, pst, psb):
                        if ci % 2 == 0:
                            act(kc[:, ci, t0:t0 + TG], pst[:, :], AF.Copy, [psb], [KCB[ci][g]])
                        else:
                            cp("dve", kc[:, ci, t0:t0 + TG], pst[:, :], [psb], [KCB[ci][g]])

                    linear_fm("in", l, win, 2048, 4, 8, h_rhs, sk_consume)
                    cut(52)

                    def sv_consume(tb, pst, psb):
                        if tb % 2 == 0:
                            act(vc[:, g * 4 + tb, :], pst[:, :], AF.Copy, [psb], [VCB[g * 4 + tb]])
                        else:
                            cp("dve", vc[:, g * 4 + tb, :], pst[:, :], [psb], [VCB[g * 4 + tb]])

                    linear_tm("in", l, win, 2560, sv_consume)
                    cut(6)

                    def rot_consume(a0, fc, fs):
                        def f(ci, pst, psb):
                            t1, t1b = ftile()
                            t2, t2b = ftile()
                            tt("dve", t1[:, 0:512], pst[:, :], rot_t[:, fc, :], ALU.mult, [psb, ROTB], [t1b])
                            tt("dve", t2[0:64, 0:512], pst[64:128, :], rot_t[64:128, fs, :], ALU.mult, [psb, ROTB], [t2b])
                            tt("dve", t2[64:128, 0:512], pst[0:64, :], rot_t[0:64, fs, :], ALU.mult, [psb, ROTB], [t2b])
                            tt("pool", arena[:, a0 + ci, :], t1[:, 0:512], t2[:, 0:512], ALU.add, [t1b, t2b], [AB[a0 + ci]])
                        return f

                    linear_fm("in", l, win, 3072, 4, 8, h_rhs, rot_consume(A_RQ, 0, 1))
                    linear_fm("in", l, win, 3584, 4, 8, h_rhs, rot_consume(A_RK, 0, 1))

                    rv_t, rv_b = arena_tm(A_RV)
                    gs_t, gs_b = arena_tm(A_GS)

                    def rv_consume(tb, pst, psb):
                        if tb % 2 == 0:
                            act(rv_t[:, tb, :], pst[:, :], AF.Copy, [psb], [rv_b[tb]])
                        else:
                            cp("dve", rv_t[:, tb, :], pst[:, :], [psb], [rv_b[tb]])

                    linear_tm("in", l, win, 4096, rv_consume)

                    def rg_consume(tb, pst, psb):
                        t1, t1b = ftile()
                        act(t1[:, 0:512], pst[:, :], AF.Silu, [psb], [t1b])
                        tt("pool", gs_t[:, tb, :], t1[:, 0:512], rgain[:], ALU.mult, [t1b, RGB], [gs_b[tb]])

                    linear_tm("in", l, win, 4608, rg_consume)
                    cut(7)

                    for c in (range(4) if 'ret' not in DISABLE else []):
                        cs = slice(c * 128, (c + 1) * 128)
                        S, Sb = rbank()
                        for hr in range(4):
                            mm(S[:, hr * 128:(hr + 1) * 128], arena[:, A_RK + hr, cs], arena[:, A_RQ + hr, cs], True, True,
                               [AB[A_RK + hr], AB[A_RQ + hr]], [Sb])
                        sm, smb = btile()
                        tt("dve", sm[:, :], S[:, :], cf[:, CF_DECAY:CF_DECAY + 512], ALU.mult, [Sb, CFB], [smb])
                        qd, qdb = btile()
                        for hr in range(4):
                            hs = slice(hr * 128, (hr + 1) * 128)
                            tt("pool", qd[:, hs], arena[:, A_RQ + hr, cs], cf[:, CF_QDEC + hr * 128:CF_QDEC + (hr + 1) * 128],
                               ALU.mult, [AB[A_RQ + hr], CFB], [qdb])
                        O, Ob = rbank()
                        for hr in range(4):
                            hs = slice(hr * 128, (hr + 1) * 128)
                            mm(O[:, hs], sm[:, hs], rv_t[:, c, hs], True, False, [smb, rv_b[c]], [Ob])
                            mm(O[:, hs], qd[:, hs], state_bf[:, hr, :], False, True, [qdb, STBB], [Ob])
                        for hr in range(4):
                            hs = slice(hr * 128, (hr + 1) * 128)
                            P.op("dve", lambda e, hr=hr, hs=hs, O=O: e.bn_stats(out=small[:, hr * 6:hr * 6 + 6], in_=O[:, hs]), reads=[Ob], writes=[SMB])
                        for hr in range(4):
                            P.op("dve", lambda e, hr=hr: e.bn_aggr(out=small[:, 24 + hr * 2:26 + hr * 2], in_=small[:, hr * 6:hr * 6 + 6]), reads=[SMB], writes=[SMB])
                        for hr in range(4):
                            act(small[:, 32 + hr:33 + hr], small[:, 25 + hr * 2:26 + hr * 2], AF.Ln, [SMB], [SMB], bias=EPS)
                        act(small[:, 32:36], small[:, 32:36], AF.Exp, [SMB], [SMB], scale=-0.5)
                        on, onb = ftile()
                        for hr in range(4):
                            hs = slice(hr * 128, (hr + 1) * 128)
                            ts("dve", on[:, hs], O[:, hs], small[:, 24 + hr * 2:25 + hr * 2], small[:, 32 + hr:33 + hr],
                               ALU.subtract, ALU.mult, [Ob, SMB], [onb])
                        rt_, rtb = btile()
                        tt("pool", rt_[:, :], on[:, 0:512], gs_t[:, c, :], ALU.mult, [onb, gs_b[c]], [rtb])
                        tp, tpb = tbank()
                        for hr in range(4):
                            hs = slice(hr * 128, (hr + 1) * 128)
                            tr(tp[:, hs], rt_[:, hs], [rtb], [tpb])
                        P.op("dve", lambda e, tp=tp, cs=cs: e.tensor_copy(out=brT[:, BR_RET:BR_RET + 4, cs],
                                                                       in_=tp.rearrange("p (h n) -> p h n", n=128)),
                             reads=[tpb], writes=BRB[BR_RET:BR_RET + 4])
                        tp2, tp2b = tbank()
                        for hr in range(4):
                            hs = slice(hr * 128, (hr + 1) * 128)
                            tr(tp2[:, hs], arena[:, A_RK + hr, cs], [AB[A_RK + hr]], [tp2b])
                        kdk, kdkb = btile()
                        for hr in range(4):
                            hs = slice(hr * 128, (hr + 1) * 128)
                            ts("dve", kdk[:, hs], tp2[:, hs], cf[:, CF_KDEC + hr:CF_KDEC + hr + 1], None, ALU.mult, None, [tp2b, CFB], [kdkb])
                        KV, KVb = rbank()
                        for hr in range(4):
                            hs = slice(hr * 128, (hr + 1) * 128)
                            mm(KV[:, hs], kdk[:, hs], rv_t[:, c, hs], True, True, [kdkb, rv_b[c]], [KVb])
                        for hr in range(4):
                            hs = slice(hr * 128, (hr + 1) * 128)
                            stt(state[:, hr, :], state[:, hr, :], cd[hr], KV[:, hs], ALU.mult, ALU.add, [STB, KVb], [STB])
                        cp("pool", state_bf[:], state[:], [STB], [STBB])

                    nblk = 4 * g + 4
                    for hp in (range(4) if 'sb' not in DISABLE else []):
                        Bt = [PB[B_BANKS[0]], PB[B_BANKS[1]]]
                        Ot = [PB[OUT_BANKS[0]], PB[OUT_BANKS[1]]]
                        for hh in range(2):
                            mm(Bt[hh][0][:, :], ones_bf, zeros[:], True, False, [CBB, ZB], [Bt[hh][1]])
                            mm(Ot[hh][0][:, :], ones_bf, zeros[:], True, False, [CBB, ZB], [Ot[hh][1]])
                        items = list(reversed(range(nblk)))
                        n_it = len(items)
                        ctx = [None] * n_it

                        def S1(s):
                            j = items[s]
                            jj = j - 4 * g
                            c0 = max(jj, 0) * 128
                            kT = kc[:, hp, j * 128:(j + 1) * 128]
                            kb = KCB[hp][j // 4]
                            cs_ = []
                            for hh in range(2):
                                qa = (A_SQ if hh == 0 else A_SQN) + hp
                                Z, Zb = rbank()
                                mm(Z[:, c0:512], kT, arena[:, qa, c0:512], True, True, [kb, AB[qa]], [Zb])
                                cs_.append(dict(j=j, hh=hh, c0=c0, Z=Z, Zb=Zb))
                            for c in cs_:
                                e_, eb = ftile()
                                act(e_[:, c0:512], c["Z"][:, c0:512], AF.Exp, [c["Zb"]], [eb])
                                if jj >= 0:
                                    tt("pool", e_[:, c0:c0 + 128], e_[:, c0:c0 + 128], cf[:, CF_MASK:CF_MASK + 128], ALU.mult, [eb, CFB], [eb])
                                c["e"] = e_
                                c["eb"] = eb
                            ctx[s] = cs_

                        def S1b(s):
                            for c in ctx[s]:
                                c0 = c["c0"]
                                l_, lb = btile()
                                act(l_[:, c0:512], c["e"][:, c0:512], AF.Ln, [c["eb"]], [lb], bias=1.0)
                                c["l"] = l_
                                c["lb"] = lb

                        def S2(s):
                            for c in ctx[s]:
                                B_, Bb = Bt[c["hh"]]
                                c0 = c["c0"]
                                mm(B_[:, c0:512], lincl, c["l"][:, c0:512], False, False, [CBB, c["lb"]], [Bb])
                            for c in ctx[s]:
                                B_, Bb = Bt[c["hh"]]
                                c0 = c["c0"]
                                en, enb = ftile()
                                act(en[:, c0:512], B_[:, c0:512], AF.Exp, [Bb], [enb], scale=-1.0)
                                a_, ab_ = btile()
                                tt("dve", a_[:, c0:512], c["e"][:, c0:512], en[:, c0:512], ALU.mult, [c["eb"], enb], [ab_])
                                c["a"] = a_
                                c["ab"] = ab_

                        def S3(s):
                            for c in ctx[s]:
                                B_, Bb = Bt[c["hh"]]
                                O_, Ob_ = Ot[c["hh"]]
                                c0 = c["c0"]
                                j = c["j"]
                                mm(B_[:, c0:512], lcomp, c["l"][:, c0:512], False, False, [CBB, c["lb"]], [Bb])
                                mm(O_[:, c0:512], vc[:, j, hp * 128:(hp + 1) * 128], c["a"][:, c0:512], False, j == 0,
                                   [VCB[j], c["ab"]], [Ob_])
                            ctx[s] = None

                        for s_ in range(n_it + 2):
                            if s_ < n_it:
                                S1(s_)
                            if 0 <= s_ - 2 < n_it:
                                S3(s_ - 2)
                            if 0 <= s_ - 1 < n_it:
                                S2(s_ - 1)
                            if s_ < n_it:
                                S1b(s_)
                        act(brT[0:64, BR_SB + hp, :], Ot[0][0][0:64, :], AF.Copy, [Ot[0][1]], [BRB[BR_SB + hp]])
                        act(brT[64:128, BR_SB + hp, :], Ot[1][0][64:128, :], AF.Copy, [Ot[1][1]], [BRB[BR_SB + hp]])

                    for dg in (range(2) if 'merge' not in DISABLE else []):
                        for n in range(3):
                            def gate_consume(ci, pst, psb, n=n):
                                bc = pv_off(l, "bgate") + n * 8 + dg * 4 + ci
                                act(arena[:, A_G + n * 4 + ci, :], pst[:, :], AF.Sigmoid, [psb, PVB], [AB[A_G + n * 4 + ci]],
                                    bias=pv[:, bc:bc + 1])
                            linear_fm("in", l, win, 5120 + n * 1024 + dg * 512, 4, 8, h_rhs, gate_consume)
                        mt = [(marena[:, i * 512:(i + 1) * 512], None) for i in range(4)]
                        for n in range(3):
                            def up_consume(ci, pst, psb, n=n):
                                m_ = mt[ci][0]
                                mbs = [AB[20 + 2 * ci], AB[21 + 2 * ci]]
                                gsl = arena[:, A_G + n * 4 + ci, :]
                                gb = AB[A_G + n * 4 + ci]
                                if n == 0:
                                    tt("dve", m_[:, 0:512], pst[:, :], gsl, ALU.mult, [psb, gb], mbs)
                                else:
                                    t1, t1b = ftile()
                                    tt("dve", t1[:, 0:512], pst[:, :], gsl, ALU.mult, [psb, gb], [t1b])
                                    if n == 1:
                                        tt("pool", m_[:, 0:512], m_[:, 0:512], t1[:, 0:512], ALU.add, mbs + [t1b], mbs)
                                    else:
                                        d = dg * 4 + ci
                                        tt("pool", arena[:, A_MG + d, :], m_[:, 0:512], t1[:, 0:512], ALU.add, mbs + [t1b], [AB[A_MG + d]])
                            br_rhs = lambda k, n=n: (brT[:, n * 4 + k, :], BRB[n * 4 + k])
                            linear_fm("br", l, wb_br[l, n], dg * 512, 4, 4, br_rhs, up_consume)

                    def resid_consume(ci, pst, psb):
                        tt("dve", xt[:, ci, :], pst[:, :], xt[:, ci, :], ALU.add, [psb, XB[ci]], [XB[ci]])

                    mg_rhs = lambda k: (arena[:, A_MG + k, :], AB[A_MG + k])
                    if "merge" not in DISABLE:
                        linear_fm("o", l, wb_o[l], 0, 8, 8, mg_rhs, resid_consume)

                    if 'xattn' not in DISABLE:
                        rmsnorm(xt, XB, TG, pv_off(l, "nxa"), ht, HB)

                        def q2_consume(ci, pst, psb):
                            act(arena[:, A_Q2 + ci, :], pst[:, :], AF.Copy, [psb], [AB[A_Q2 + ci]], scale=1.0 / 16.0)

                        linear_fm("xq", l, wb_xq[l], 0, 8, 8, h_rhs, q2_consume)
                        for hx in range(4):
                            pts = []
                            for mb in range(2):
                                SC, SCb = rbank()
                                for kk in range(2):
                                    mm(SC[:, :], kmem[:, 2 * hx + kk, mb * 128:(mb + 1) * 128], arena[:, A_Q2 + 2 * hx + kk, :], kk == 0, kk == 1,
                                       [KMB, AB[A_Q2 + 2 * hx + kk]], [SCb])
                                p_, pb_ = btile()
                                act(p_[:, :], SC[:, :], AF.Exp, [SCb], [pb_])
                                pts.append((p_, pb_))
                            DEN, DENb = rbank()
                            for mb in range(2):
                                mm(DEN[:, :], ones_bf, pts[mb][0][:, :], mb == 0, mb == 1, [CBB, pts[mb][1]], [DENb])
                            rd_, rdb = ftile()
                            P.op("dve", lambda e, rd_=rd_, DEN=DEN: e.reciprocal(out=rd_[:, 0:512], in_=DEN[:, :]), reads=[DENb], writes=[rdb])
                            for kk in range(2):
                                OT, OTb = rbank()
                                dch = 2 * hx + kk
                                for mb in range(2):
                                    mm(OT[:, :], vmem[:, mb, dch * 128:(dch + 1) * 128], pts[mb][0][:, :], mb == 0, mb == 1, [VMB, pts[mb][1]], [OTb])
                                tt("dve", arena[:, A_OT + dch, :], OT[:, :], rd_[:, 0:512], ALU.mult, [OTb, rdb], [AB[A_OT + dch]])
                        ot_rhs = lambda k: (arena[:, A_OT + k, :], AB[A_OT + k])
                        linear_fm("xo", l, wb_xo[l], 0, 8, 8, ot_rhs, resid_consume)

                    if 'mlp' not in DISABLE:
                        rmsnorm(xt, XB, TG, pv_off(l, "nmlp"), ht, HB)

                        def up_mlp_consume(ci, pst, psb):
                            t1, t1b = ftile()
                            act(t1[:, 0:512], pst[:, :], AF.Relu, [psb], [t1b])
                            tt("pool", arena[:, ci, :], t1[:, 0:512], t1[:, 0:512], ALU.mult, [t1b], [AB[ci]])

                        linear_fm("up", l, wb_up[l], 0, 32, 8, h_rhs, up_mlp_consume)
                        for d in range(8):
                            slot, wsb = wslot()
                            view = slot[:, :].rearrange("p (c n) -> p c n", n=128)
                            load("sp", slot[:, :], wb_dn[l, d], wsb, R=[WB[("dn", l)]])
                            pst, psb = rbank()
                            for k in range(32):
                                mm(pst[:, :], view[:, k, :], arena[:, k, :], k == 0, k == 31, [wsb, AB[k]], [psb])
                            resid_consume(d, pst, psb)
                            if l < L - 1:
                                P.dma("pool", lambda e, xt=xt, t0=t0, d=d: [e.dma_start(out=xs_d[d * 128:(d + 1) * 128, t0:t0 + TG], in_=xt[:, d, :])],
                                      XB[d], reads=[XB[d]], writes=[XSD[g][d]])

                    if l == L - 1:
                        rmsnorm_final(P, xt, XB, sq, SQB, None, None, pv, PVB, L * PV_L, rbank, mm, act, stt, tt, ones_bf, CBB,
                                      ftile, out_d, t0, load)
                    else:
                        pass


        except StopBuild:
            pass
        P.op("sp", lambda e: None, reads=OUTD)
        P.generate(block)
        build.stats = (P.stats, P.sbytes)
    return nc


OUTD = []


def rmsnorm_final(P, xt, XB, sq, SQB, _u1, _u2, pv, PVB, gcol, rbank, mm, act, stt, tt, ones_bf, CBB, ftile, out_d, t0, load):
    for c in range(8):
        tt("pool", sq[:, c, :], xt[:, c, :], xt[:, c, :], ALU.mult, [XB[c]], [SQB[c]])
    pst, psb = rbank()
    rstd_f, RSB = ftile()
    rstd = rstd_f[:, 0:512]
    for c in range(8):
        mm(pst[:, :], ones_bf, sq[:, c, :], c == 0, c == 7, [SQB[c], CBB], [psb])
    act(rstd, pst[:, :], AF.Ln, [psb], [RSB], scale=1.0 / 1024.0, bias=EPS)
    act(rstd, rstd, AF.Exp, [RSB], [RSB], scale=-0.5)
    for c in range(8):
        o, ob = ftile()
        if ob is RSB:
            o, ob = ftile()
        stt(o[:, 0:512], xt[:, c, :], pv[:, gcol + c:gcol + c + 1], rstd, ALU.mult, ALU.mult, [XB[c], PVB, RSB], [ob])
        od = P.buf("outd")
        P.dma("pool", lambda e, o=o, c=c: [e.dma_start(out=out_d[c * 128:(c + 1) * 128, t0:t0 + 512], in_=o[:, 0:512])],
              ob, reads=[ob], writes=[od])
        OUTD.append(od)


_CACHE = {}


def run(inputs, T, L=2, n_cores=8, active=ACTIVE_CORES):
    f32 = np.float32
    x = np.asarray(inputs["x"], f32)
    mem = np.asarray(inputs["mem"], f32)
    B = x.shape[0]
    cf, cb, rot, cd = host_constants(T)
    key = (T, L)
    OUTD.clear()
    nc = build(T, L, cd)
    pvec = host_pvec(inputs, L)
    rgain = np.ascontiguousarray(np.broadcast_to(np.asarray(inputs["ret_norm_g"], f32)[:, None, :], (L, 128, 512)))
    common = {
        "pv": pvec, "cf": cf, "cb": cb, "rot": rot, "rgain": rgain,
        "w_in": np.asarray(inputs["w_in"], f32), "w_branch": np.asarray(inputs["w_branch"], f32),
        "w_o": np.asarray(inputs["w_o"], f32), "w_xq": np.asarray(inputs["w_xq"], f32),
        "w_xkv": np.asarray(inputs["w_xkv"], f32), "w_xo": np.asarray(inputs["w_xo"], f32),
        "w_up": np.asarray(inputs["w_up"], f32), "w_down": np.asarray(inputs["w_down"], f32),
    }
    in_maps = []
    zx = np.zeros((1024, T), f32)
    zm = np.zeros((1024, MEM_LEN), f32)
    slot = {c: i for i, c in enumerate(active)}
    for c in range(n_cores):
        m = dict(common)
        if c in slot and slot[c] < B:
            b = slot[c]
            m["xT"] = np.ascontiguousarray(x[b].T)
            m["memT"] = np.ascontiguousarray(mem[b].T)
        else:
            m["xT"] = zx
            m["memT"] = zm
        in_maps.append(m)
    res = run_bass_kernel_spmd(nc, in_maps, core_ids=list(range(n_cores)))
    out = np.empty((B, T, 1024), f32)
    for c, b in slot.items():
        if b < B:
            out[b] = np.asarray(res.results[c]["outT"], f32).T
    return out


def kernel(**inputs):
    T = inputs["x"].shape[1]
    return run(inputs, T)
```

```python
import numpy as np
import ml_dtypes
import concourse.bass as bass
import concourse.mybir as mybir
from concourse.bass_utils import run_bass_kernel_spmd
from contextlib import ExitStack

F32 = mybir.dt.float32
BF16 = mybir.dt.bfloat16
ALU = mybir.AluOpType
AF = mybir.ActivationFunctionType

D_MODEL = 1024
MEM_LEN = 256
IN_COLS = 8192
D_FF = 4096
EPS = 1e-6
TG = 512
ACTIVE_CORES = (0, 1, 4, 5)
DISABLE = set()
CUT = 0


class StopBuild(Exception):
    pass


def cut(n):
    if CUT == n:
        raise StopBuild()


class Buf:
    __slots__ = ("name", "lw", "rd", "sem", "dcount", "last_dma")

    def __init__(self, name):
        self.name = name
        self.lw = None
        self.rd = []
        self.sem = None
        self.dcount = 0
        self.last_dma = None


class Inst:
    __slots__ = ("eng", "fn", "hard", "soft", "sig", "is_dma", "owner", "dval", "semval")

    def __init__(self, eng, fn, is_dma=False):
        self.eng = eng
        self.fn = fn
        self.hard = []
        self.soft = []
        self.sig = False
        self.is_dma = is_dma
        self.owner = None
        self.dval = 0
        self.semval = 0


class Prog:
    ENGS = ("pe", "act", "dve", "pool", "sp")

    def __init__(self, nc, es):
        self.nc = nc
        self.es = es
        self.q = {e: [] for e in self.ENGS}
        self.uid = 0
        self.sbytes = 0

    def sb(self, shape, dtype, name=None):
        self.uid += 1
        n = 1
        for s in shape[1:]:
            n *= s
        self.sbytes += n * (2 if dtype == BF16 else 4)
        return self.es.enter_context(self.nc.sbuf_tensor(name or f"sb{self.uid}", list(shape), dtype))

    def ps(self, shape, dtype=F32, name=None):
        self.uid += 1
        return self.es.enter_context(self.nc.psum_tensor(name or f"ps{self.uid}", list(shape), dtype))

    def buf(self, name=None):
        self.uid += 1
        return Buf(name or f"b{self.uid}")

    def new_sem(self, name):
        return self.es.enter_context(self.nc.semaphore(name))

    def _track(self, inst, reads, writes):
        wset = set(id(b) for b in writes)
        for b in reads:
            if b.lw is not None:
                inst.hard.append(b.lw)
        for b in writes:
            if b.lw is not None:
                inst.hard.append(b.lw)
            inst.soft.extend(b.rd)
        for b in reads:
            if id(b) not in wset:
                b.rd.append(inst)
        for b in writes:
            b.lw = inst
            b.rd = []

    def op(self, eng, fn, reads=(), writes=()):
        inst = Inst(eng, fn)
        self._track(inst, reads, writes)
        self.q[eng].append(inst)
        return inst

    def dma(self, queue, fn, owner, reads=(), writes=(), npieces=1):
        inst = Inst(queue, fn, is_dma=True)
        self._track(inst, reads, writes)
        if owner.last_dma is not None:
            inst.hard.append(owner.last_dma)
        owner.last_dma = inst
        if owner.sem is None:
            self.uid += 1
            owner.sem = self.new_sem(f"dsem{self.uid}")
        owner.dcount += 16 * npieces
        inst.owner = owner
        inst.dval = owner.dcount
        self.q[queue].append(inst)
        return inst

    def generate(self, block):
        esem = {e: self.new_sem(f"esem_{e}") for e in ("pe", "act", "dve", "pool")}
        for e in self.ENGS:
            for inst in self.q[e]:
                deps = []
                for p in inst.hard:
                    if p is inst:
                        continue
                    if p.is_dma or inst.is_dma or p.eng != inst.eng or inst.eng != "pe":
                        deps.append(p)
                for p in inst.soft:
                    if p is inst:
                        continue
                    if p.is_dma or inst.is_dma or p.eng != inst.eng:
                        deps.append(p)
                inst.hard = deps
                inst.soft = None
                for p in deps:
                    if not p.is_dma:
                        p.sig = True
        for e in ("pe", "act", "dve", "pool"):
            c = 0
            for inst in self.q[e]:
                if inst.is_dma:
                    continue
                if inst.sig:
                    c += 1
                    inst.semval = c
        self.stats = {}

        def gen(e, eng):
            waited = {}
            nw = 0
            for inst in self.q[e]:
                need = {}
                for p in inst.hard:
                    if p.is_dma:
                        s, v = p.owner.sem, p.dval
                    else:
                        s, v = esem[p.eng], p.semval
                    k = id(s)
                    if waited.get(k, 0) >= v:
                        continue
                    if k not in need or need[k][1] < v:
                        need[k] = (s, v)
                for k, (s, v) in need.items():
                    eng.wait_ge(s, v)
                    waited[k] = v
                    nw += 1
                r = inst.fn(eng)
                if inst.is_dma:
                    for h in r:
                        h.then_inc(inst.owner.sem, 16)
                elif inst.sig:
                    r.then_inc(esem[e], 1)
            self.stats[e] = (len(self.q[e]), nw)

        @block.sync
        def _(eng):
            gen("sp", eng)

        @block.scalar
        def _(eng):
            gen("act", eng)

        @block.vector
        def _(eng):
            gen("dve", eng)

        @block.gpsimd
        def _(eng):
            gen("pool", eng)

        @block.tensor
        def _(eng):
            gen("pe", eng)


PV_L = 8 * 4 + 24 + 12 + 4


def pv_off(l, name):
    base = l * PV_L
    offs = {"nmix": 0, "nxa": 8, "nmem": 16, "nmlp": 24, "bgate": 32, "convw": 56, "convb": 68}
    return base + offs[name]


def host_constants(T):
    f32 = np.float32
    idx = np.arange(128, dtype=f32)
    log_gamma = np.log1p(-np.exp2(-5.0 - np.arange(4, dtype=f32))).astype(f32)
    j = idx[:, None]
    i = idx[None, :]
    decayT = np.zeros((128, 4, 128), f32)
    for h in range(4):
        rel = i - j
        decayT[:, h, :] = np.where(rel >= 0, np.exp(np.maximum(rel, 0.0) * log_gamma[h]), 0.0)
    ks = f32(128.0 ** -0.5)
    decayT = (decayT * ks).astype(f32)
    qdec = np.zeros((128, 4, 128), f32)
    for h in range(4):
        qdec[:, h, :] = np.exp((idx + 1) * log_gamma[h])[None, :]
    mask = (j < i).astype(f32)
    kdecay = (np.exp((127 - idx)[:, None] * log_gamma[None, :]) * ks).astype(f32)
    cf = np.concatenate([decayT.reshape(128, 512), qdec.reshape(128, 512), mask, kdecay], axis=1).astype(f32)
    cd = [float(np.exp(f32(128.0) * log_gamma[h])) for h in range(4)]
    ident = np.eye(128, dtype=f32)
    ones = np.ones((128, 128), f32)
    lincl = (j >= i).astype(f32)
    lc = 1.0 - lincl
    cb = np.concatenate([ident, ones, lincl, lc, mask], axis=1).astype(ml_dtypes.bfloat16)
    pos = np.arange(T, dtype=f32)
    inv_freq = (f32(10000.0) ** (-np.arange(0, 128, 2, dtype=f32) / f32(128))).astype(f32)
    ang = (pos[:, None] * inv_freq[None, :]).astype(f32)
    cos = np.cos(ang).astype(f32).T
    sin = np.sin(ang).astype(f32).T
    cosf = np.concatenate([cos, cos], 0)
    sinf = np.concatenate([sin, -sin], 0)
    rot = np.stack([cosf, sinf], 0).astype(f32)
    return cf, cb, rot, cd


CF_DECAY, CF_QDEC, CF_MASK, CF_KDEC, CF_N = 0, 512, 1024, 1024 + 128, 1024 + 128 + 4
CB_ID, CB_ONES, CB_LINCL, CB_LC, CB_MASK, CB_N = 0, 128, 256, 384, 512, 640


def col8(v):
    return np.ascontiguousarray(v.reshape(-1, 128).T)


def host_pvec(inp, L):
    cols = []
    for l in range(L):
        cols += [col8(inp["norm_mix_g"][l]), col8(inp["norm_xa_g"][l]), col8(inp["norm_mem_g"][l]),
                 col8(inp["norm_mlp_g"][l])]
        cols.append(np.concatenate([col8(inp["b_gate"][l][n]) for n in range(3)], axis=1))
        cols.append(np.concatenate([col8(inp["conv_w"][l][jj]) for jj in range(3)], axis=1))
        cols.append(col8(inp["conv_b"][l]))
    cols.append(col8(inp["final_g"]))
    return np.ascontiguousarray(np.concatenate(cols, axis=1).astype(np.float32))


def build(T, L, cd, dbg=False):
    NG = T // TG
    NB = T // 128
    nc = bass.Bass("TRN2", target_bir_lowering=False)

    def din(name, shape, dt=F32):
        return nc.dram_tensor(name, list(shape), dt, kind="ExternalInput").ap()

    xT_d = din("xT", [1024, T])
    memT_d = din("memT", [1024, MEM_LEN])
    pv_d = din("pv", [128, L * PV_L + 8])
    cf_d = din("cf", [128, CF_N])
    cb_d = din("cb", [128, CB_N], BF16)
    rot_d = din("rot", [2, 128, T])
    rgain_d = din("rgain", [L, 128, 512])
    w_in_d = din("w_in", [L, 1024, IN_COLS])
    w_br_d = din("w_branch", [L, 3, 512, 1024])
    w_o_d = din("w_o", [L, 1024, 1024])
    w_xq_d = din("w_xq", [L, 1024, 1024])
    w_xkv_d = din("w_xkv", [L, 1024, 2048])
    w_xo_d = din("w_xo", [L, 1024, 1024])
    w_up_d = din("w_up", [L, 1024, D_FF])
    w_dn_d = din("w_down", [L, D_FF, 1024])
    out_d = nc.dram_tensor("outT", [1024, T], F32, kind="ExternalOutput").ap()

    def dint(name, shape, dt=BF16):
        return nc.dram_tensor(name, list(shape), dt).ap()

    wb_in = dint("wb_in", [L, 1024, IN_COLS])
    wb_br = dint("wb_br", [L, 3, 512, 1024])
    wb_o = dint("wb_o", [L, 1024, 1024])
    wb_xq = dint("wb_xq", [L, 1024, 1024])
    wb_xkv = dint("wb_xkv", [L, 1024, 2048])
    wb_xo = dint("wb_xo", [L, 1024, 1024])
    wb_up = dint("wb_up", [L, 1024, D_FF])
    wb_dn = dint("wb_dn", [L, 8, 128, 32 * 128])
    xs_d = dint("xs", [1024, T], F32)

    with ExitStack() as es:
        P = Prog(nc, es)
        block = es.enter_context(nc.Block())

        pv = P.sb([128, L * PV_L + 8], F32); PVB = P.buf("pv")
        cf = P.sb([128, CF_N], F32); CFB = P.buf("cf")
        cb = P.sb([128, CB_N], BF16); CBB = P.buf("cb")
        zeros = P.sb([128, 512], BF16); ZB = P.buf("zeros")
        rgain = P.sb([128, 512], F32); RGB = P.buf("rgain")
        NW = 3
        wslots = [(P.sb([128, 4096], BF16), P.buf(f"ws{i}")) for i in range(NW)]
        xts = [(P.sb([128, 8, 512], F32), [P.buf(f"x{i}_{c}") for c in range(8)]) for i in range(1)]
        ht = P.sb([128, 8, 512], BF16); HB = [P.buf(f"h{c}") for c in range(8)]
        NA = 32
        arena = P.sb([128, NA, 512], BF16); AB = [P.buf(f"ar{i}") for i in range(NA)]
        sq = arena[:, 0:8, :]; SQB = AB[0:8]
        marena = arena[:, 20:28, :].rearrange("p a n -> p (a n)").bitcast(F32)
        brT = P.sb([128, 12, 512], BF16); BRB = [P.buf(f"br{i}") for i in range(12)]
        kc = P.sb([128, 4, T], BF16); KCB = [[P.buf(f"kc{c}_{g}") for g in range(NG)] for c in range(4)]
        vc = P.sb([128, NB, 512], BF16); VCB = [P.buf(f"vc{b}") for b in range(NB)]
        rot_t = P.sb([128, 2, 512], F32); ROTB = P.buf("rot")
        kmem = P.sb([128, 8, 256], BF16); KMB = P.buf("kmem")
        vmem = P.sb([128, 2, 1024], BF16); VMB = P.buf("vmem")
        state = P.sb([128, 4, 128], F32); STB = P.buf("state")
        state_bf = P.sb([128, 4, 128], BF16); STBB = P.buf("statebf")
        halo = P.sb([128, 4, 2], F32); HALB = [P.buf(f"halo{c}") for c in range(4)]
        NF = 8
        ftiles = [(P.sb([128, 514], F32), P.buf(f"ft{i}")) for i in range(NF)]
        NBT = 10
        btiles = [(P.sb([128, 512], BF16), P.buf(f"bt{i}")) for i in range(NBT)]
        small = P.sb([128, 64], F32); SMB = P.buf("small")

        PB = [(P.ps([128, 512], F32), P.buf(f"pb{i}")) for i in range(7)]
        tb_ps = P.ps([128, 1024], BF16); _tb = P.buf("tb"); TBB = [_tb, _tb]
        ROT_BANKS = [0, 1, 2]
        B_BANKS = [3, 4]
        OUT_BANKS = [5, 6]
        st = {"rb": 0, "ws": 0, "ft": 0, "bt": 0, "tb": 0}

        def rbank():
            i = ROT_BANKS[st["rb"] % len(ROT_BANKS)]
            st["rb"] += 1
            return PB[i]

        def wslot():
            s = wslots[st["ws"] % NW]
            st["ws"] += 1
            return s

        def ftile():
            s = ftiles[st["ft"] % NF]
            st["ft"] += 1
            return s

        def btile():
            s = btiles[st["bt"] % NBT]
            st["bt"] += 1
            return s

        def tbank():
            i = st["tb"] % 2
            st["tb"] += 1
            return tb_ps[:, i * 512:(i + 1) * 512], TBB[i]

        ident = cb[:, CB_ID:CB_ID + 128]
        ones_bf = cb[:, CB_ONES:CB_ONES + 128]
        lincl = cb[:, CB_LINCL:CB_LINCL + 128]
        lcomp = cb[:, CB_LC:CB_LC + 128]
        mask_bf = cb[:, CB_MASK:CB_MASK + 128]

        def mm(out, lhsT, rhs, start, stop, R, Wb):
            P.op("pe", lambda e: e.matmul(out, lhsT=lhsT, rhs=rhs, start=start, stop=stop), reads=R, writes=Wb)

        def tr(out, in_, R, Wb):
            P.op("pe", lambda e: e.transpose(out, in_, ident), reads=R + [CBB], writes=Wb)

        def act(out, in_, func, R, Wb, scale=None, bias=None):
            kw = {}
            if scale is not None:
                kw["scale"] = scale
            if bias is not None:
                kw["bias"] = bias
            P.op("act", lambda e: e.activation(out=out, in_=in_, func=func, **kw), reads=R, writes=Wb)

        def tt(eng, out, in0, in1, op, R, Wb):
            P.op(eng, lambda e: e.tensor_tensor(out=out, in0=in0, in1=in1, op=op), reads=R, writes=Wb)

        def ts(eng, out, in0, s1, s2, op0, op1, R, Wb):
            if op1 is None:
                P.op(eng, lambda e: e.tensor_scalar(out=out, in0=in0, scalar1=s1, scalar2=None, op0=op0), reads=R, writes=Wb)
            else:
                P.op(eng, lambda e: e.tensor_scalar(out=out, in0=in0, scalar1=s1, scalar2=s2, op0=op0, op1=op1), reads=R, writes=Wb)

        def stt(out, in0, scalar, in1, op0, op1, R, Wb):
            P.op("dve", lambda e: e.scalar_tensor_tensor(out=out, in0=in0, scalar=scalar, in1=in1, op0=op0, op1=op1), reads=R, writes=Wb)

        def cp(eng, out, in_, R, Wb):
            P.op(eng, lambda e: e.tensor_copy(out=out, in_=in_), reads=R, writes=Wb)

        def load(queue, out, in_, owner, R=(), Wb=None):
            P.dma(queue, lambda e: [e.dma_start(out=out, in_=in_)], owner, reads=list(R), writes=[owner] if Wb is None else Wb)

        load("sp", pv[:], pv_d, PVB)
        load("sp", cf[:], cf_d, CFB)
        load("sp", cb[:], cb_d, CBB)
        P.op("dve", lambda e: e.memset(zeros[:], 0.0), writes=[ZB])

        WB = {}

        PCB = P.buf("precast_chain")

        def precast(name, l, pieces):
            b = P.buf(f"wb_{name}{l}")
            for (o, i) in pieces:
                P.dma("pool", lambda e, o=o, i=i: [e.dma_start(out=o, in_=i)], PCB, writes=[b])
            WB[(name, l)] = b

        for l in range(L):
            precast("in", l, [(wb_in[l, r * 128:(r + 1) * 128, :], w_in_d[l, r * 128:(r + 1) * 128, :]) for r in range(8)])
            precast("xkv", l, [(wb_xkv[l, r * 256:(r + 1) * 256, :], w_xkv_d[l, r * 256:(r + 1) * 256, :]) for r in range(4)])
            precast("br", l, [(wb_br[l, n], w_br_d[l, n]) for n in range(3)])
            precast("o", l, [(wb_o[l, r * 512:(r + 1) * 512, :], w_o_d[l, r * 512:(r + 1) * 512, :]) for r in range(2)])
            precast("xq", l, [(wb_xq[l, r * 512:(r + 1) * 512, :], w_xq_d[l, r * 512:(r + 1) * 512, :]) for r in range(2)])
            precast("xo", l, [(wb_xo[l, r * 512:(r + 1) * 512, :], w_xo_d[l, r * 512:(r + 1) * 512, :]) for r in range(2)])
            precast("up", l, [(wb_up[l, r * 256:(r + 1) * 256, :], w_up_d[l, r * 256:(r + 1) * 256, :]) for r in range(4)])
            precast("dn", l, [(wb_dn[l, d].rearrange("p (c n) -> p c n", n=128),
                               w_dn_d[l, :, d * 128:(d + 1) * 128].rearrange("(c p) n -> p c n", p=128)) for d in range(8)])

        def wload_cols(name, l, src2d, c0, ncols, kchunks):
            slot, sb_ = wslot()
            view = slot[:, 0:kchunks * ncols].rearrange("p (c n) -> p c n", n=ncols)
            src = src2d.rearrange("(c p) n -> p c n", p=128)[:, :, c0:c0 + ncols]
            load("sp", view, src, sb_, R=[WB[(name, l)]])
            return view, sb_

        def rmsnorm(xt, XB, ncol, gcol, out_t, OB, out_is_f32=False):
            for c in range(8):
                tt("dve", sq[:, c, 0:ncol], xt[:, c, :], xt[:, c, :], ALU.mult, [XB[c]], [SQB[c]])
            pst, psb = rbank()
            rstd, RSB = ftile()
            for c in range(8):
                mm(pst[:, 0:ncol], ones_bf, sq[:, c, 0:ncol], c == 0, c == 7, [SQB[c], CBB], [psb])
            act(rstd[:, 0:ncol], pst[:, 0:ncol], AF.Ln, [psb], [RSB], scale=1.0 / 1024.0, bias=EPS)
            act(rstd[:, 0:ncol], rstd[:, 0:ncol], AF.Exp, [RSB], [RSB], scale=-0.5)
            for c in range(8):
                stt(out_t[:, c, :], xt[:, c, :], pv[:, gcol + c:gcol + c + 1], rstd[:, 0:ncol], ALU.mult, ALU.mult,
                    [XB[c], PVB, RSB], [OB[c]])

        def linear_fm(name, l, src2d, col0, nchunks, kchunks, rhs_fn, consume):
            done = 0
            while done < nchunks:
                n = min(4, nchunks - done)
                view, wsb = wload_cols(name, l, src2d, col0 + done * 128, n * 128, kchunks)
                for cc in range(n):
                    pst, psb = rbank()
                    for k in range(kchunks):
                        r_ap, r_b = rhs_fn(k)
                        mm(pst[:, :], view[:, k, cc * 128:(cc + 1) * 128], r_ap, k == 0, k == kchunks - 1, [wsb, r_b], [psb])
                    consume(done + cc, pst, psb)
                done += n

        def linear_tm(name, l, src2d, col0, consume):
            view, wsb = wload_cols(name, l, src2d, col0, 512, 8)
            for tb in range(4):
                pst, psb = rbank()
                for k in range(8):
                    mm(pst[:, :], ht[:, k, tb * 128:(tb + 1) * 128], view[:, k, :], k == 0, k == 7, [wsb, HB[k]], [psb])
                consume(tb, pst, psb)

        h_rhs = lambda k: (ht[:, k, :], HB[k])

        A_CB, A_CC, A_SQ, A_SQN, A_RQ, A_RK = 0, 4, 8, 12, 16, 20
        A_RV, A_GS = 24, 28
        A_G, A_MG = 0, 12
        A_Q2, A_OT = 16, 24
        BR_CONV, BR_SB, BR_RET = 0, 4, 8

        def arena_tm(a0):
            return arena[:, a0:a0 + 4, :], AB[a0:a0 + 4]

        XSD = [[P.buf(f"xsd{i}_{c}") for c in range(8)] for i in range(NG)]

        try:
            cut(1)
            for l in range(L):
                load("sp", rgain[:], rgain_d[l], RGB)
                P.op("pool", lambda e: e.memset(state[:], 0.0), writes=[STB])
                P.op("pool", lambda e: e.memset(state_bf[:], 0.0), writes=[STBB])
                P.op("pool", lambda e: e.memset(halo[:], 0.0), writes=HALB)
                xt0, XB0 = xts[0]
                memx = xt0[:, :, 0:MEM_LEN]
                P.dma("sp", lambda e, memx=memx: [e.dma_start(out=memx, in_=memT_d.rearrange("(c p) n -> p c n", p=128))],
                      XB0[0], writes=XB0)
                rmsnorm(memx, XB0, MEM_LEN, pv_off(l, "nmem"), ht[:, :, 0:MEM_LEN], HB)
                def kmem_consume(ci, pst, psb):
                    cp("dve", kmem[:, ci, :], pst[:, 0:MEM_LEN], [psb], [KMB])
                done = 0
                for half in range(2):
                    view, wsb = wload_cols("xkv", l, wb_xkv[l], half * 512, 512, 8)
                    for cc in range(4):
                        pst, psb = rbank()
                        for k in range(8):
                            mm(pst[:, 0:MEM_LEN], view[:, k, cc * 128:(cc + 1) * 128], ht[:, k, 0:MEM_LEN], k == 0, k == 7, [wsb, HB[k]], [psb])
                        kmem_consume(half * 4 + cc, pst, psb)
                for half in range(2):
                    view, wsb = wload_cols("xkv", l, wb_xkv[l], 1024 + half * 512, 512, 8)
                    for mb in range(2):
                        pst, psb = rbank()
                        for k in range(8):
                            mm(pst[:, :], ht[:, k, mb * 128:(mb + 1) * 128], view[:, k, :], k == 0, k == 7, [wsb, HB[k]], [psb])
                        cp("dve", vmem[:, mb, half * 512:(half + 1) * 512], pst[:, :], [psb], [VMB])

                cut(2)
                for g in range(NG):
                    xt, XB = xts[0]
                    t0 = g * TG
                    src = xT_d if l == 0 else xs_d
                    for c in range(8):
                        rd = [] if l == 0 else [XSD[g][c]]
                        P.dma("sp", lambda e, xt=xt, src=src, t0=t0, c=c: [e.dma_start(out=xt[:, c, :], in_=src[c * 128:(c + 1) * 128, t0:t0 + TG])],
                              XB[c], reads=rd, writes=[XB[c]])
                    load("sp", rot_t[:], rot_d[:, :, t0:t0 + TG].rearrange("f p n -> p f n"), ROTB)

                    rmsnorm(xt, XB, TG, pv_off(l, "nmix"), ht, HB)
                    win = wb_in[l]
                    cut(3)

                    def ev_bf(a0, alt=False):
                        def f(ci, pst, psb):
                            if (ci % 2 == 0) != alt:
                                act(arena[:, a0 + ci, :], pst[:, :], AF.Copy, [psb], [AB[a0 + ci]])
                            else:
                                cp("dve", arena[:, a0 + ci, :], pst[:, :], [psb], [AB[a0 + ci]])
                        return f

                    linear_fm("in", l, win, 0, 4, 8, h_rhs, ev_bf(A_CB))
                    linear_fm("in", l, win, 512, 4, 8, h_rhs, ev_bf(A_CC, True))
                    cut(4)

                    def conv_consume(ci, pst, psb):
                        u, ub = ftile()
                        cp("pool", u[:, 0:2], halo[:, ci, :], [HALB[ci]], [ub])
                        tt("dve", u[:, 2:514], pst[:, :], arena[:, A_CC + ci, :], ALU.mult, [psb, AB[A_CC + ci]], [ub])
                        cp("pool", halo[:, ci, :], u[:, 512:514], [ub], [HALB[ci]])
                        a, ab_ = ftile()
                        cw = pv_off(l, "convw")
                        cbias = pv_off(l, "convb")
                        ts("dve", a[:, 0:512], u[:, 2:514], pv[:, cw + 8 + ci:cw + 9 + ci], pv[:, cbias + ci:cbias + ci + 1],
                           ALU.mult, ALU.add, [ub, PVB], [ab_])
                        stt(a[:, 0:512], u[:, 1:513], pv[:, cw + 4 + ci:cw + 5 + ci], a[:, 0:512], ALU.mult, ALU.add, [ub, PVB, ab_], [ab_])
                        stt(a[:, 0:512], u[:, 0:512], pv[:, cw + ci:cw + 1 + ci], a[:, 0:512], ALU.mult, ALU.add, [ub, PVB, ab_], [ab_])
                        tt("dve", brT[:, BR_CONV + ci, :], a[:, 0:512], arena[:, A_CB + ci, :], ALU.mult, [ab_, AB[A_CB + ci]], [BRB[BR_CONV + ci]])

                    linear_fm("in", l, win, 1024, 4, 8, h_rhs, conv_consume)
                    cut(5)

                    def sq_consume(ci, pst, psb):
                        act(arena[0:64, A_SQ + ci, :], pst[0:64, :], AF.Copy, [psb], [AB[A_SQ + ci]], scale=0.125)
                        P.op("pool", lambda e, ci=ci: e.memset(arena[64:128, A_SQ + ci, :], 0.0), writes=[AB[A_SQ + ci]])
                        act(arena[64:128, A_SQN + ci, :], pst[64:128, :], AF.Copy, [psb], [AB[A_SQN + ci]], scale=0.125)
                        P.op("pool", lambda e, ci=ci: e.memset(arena[0:64, A_SQN + ci, :], 0.0), writes=[AB[A_SQN + ci]])

                    linear_fm("in", l, win, 1536, 4, 8, h_rhs, sq_consume)
                    cut(51)

                    def sk_consume(ci, pst, psb):
                        if ci % 2 == 0:
                            act(kc[:, ci, t0:t0 + TG], pst[:, :], AF.Copy, [psb], [KCB[ci][g]])
                        else:
                            cp("dve", kc[:, ci, t0:t0 + TG], pst[:, :], [psb], [KCB[ci][g]])

                    linear_fm("in", l, win, 2048, 4, 8, h_rhs, sk_consume)
                    cut(52)

                    def sv_consume(tb, pst, psb):
                        if tb % 2 == 0:
                            act(vc[:, g * 4 + tb, :], pst[:, :], AF.Copy, [psb], [VCB[g * 4 + tb]])
                        else:
                            cp("dve", vc[:, g * 4 + tb, :], pst[:, :], [psb], [VCB[g * 4 + tb]])

                    linear_tm("in", l, win, 2560, sv_consume)
                    cut(6)

                    def rot_consume(a0, fc, fs):
                        def f(ci, pst, psb):
                            t1, t1b = ftile()
                            t2, t2b = ftile()
                            tt("dve", t1[:, 0:512], pst[:, :], rot_t[:, fc, :], ALU.mult, [psb, ROTB], [t1b])
                            tt("dve", t2[0:64, 0:512], pst[64:128, :], rot_t[64:128, fs, :], ALU.mult, [psb, ROTB], [t2b])
                            tt("dve", t2[64:128, 0:512], pst[0:64, :], rot_t[0:64, fs, :], ALU.mult, [psb, ROTB], [t2b])
                            tt("pool", arena[:, a0 + ci, :], t1[:, 0:512], t2[:, 0:512], ALU.add, [t1b, t2b], [AB[a0 + ci]])
                        return f

                    linear_fm("in", l, win, 3072, 4, 8, h_rhs, rot_consume(A_RQ, 0, 1))
                    linear_fm("in", l, win, 3584, 4, 8, h_rhs, rot_consume(A_RK, 0, 1))

                    rv_t, rv_b = arena_tm(A_RV)
                    gs_t, gs_b = arena_tm(A_GS)

                    def rv_consume(tb, pst, psb):
                        if tb % 2 == 0:
                            act(rv_t[:, tb, :], pst[:, :], AF.Copy, [psb], [rv_b[tb]])
                        else:
                            cp("dve", rv_t[:, tb, :], pst[:, :], [psb], [rv_b[tb]])

                    linear_tm("in", l, win, 4096, rv_consume)

                    def rg_consume(tb, pst, psb):
                        t1, t1b = ftile()
                        act(t1[:, 0:512], pst[:, :], AF.Silu, [psb], [t1b])
                        tt("pool", gs_t[:, tb, :], t1[:, 0:512], rgain[:], ALU.mult, [t1b, RGB], [gs_b[tb]])

                    linear_tm("in", l, win, 4608, rg_consume)
                    cut(7)

                    for c in (range(4) if 'ret' not in DISABLE else []):
                        cs = slice(c * 128, (c + 1) * 128)
                        S, Sb = rbank()
                        for hr in range(4):
                            mm(S[:, hr * 128:(hr + 1) * 128], arena[:, A_RK + hr, cs], arena[:, A_RQ + hr, cs], True, True,
                               [AB[A_RK + hr], AB[A_RQ + hr]], [Sb])
                        sm, smb = btile()
                        tt("dve", sm[:, :], S[:, :], cf[:, CF_DECAY:CF_DECAY + 512], ALU.mult, [Sb, CFB], [smb])
                        qd, qdb = btile()
                        for hr in range(4):
                            hs = slice(hr * 128, (hr + 1) * 128)
                            tt("pool", qd[:, hs], arena[:, A_RQ + hr, cs], cf[:, CF_QDEC + hr * 128:CF_QDEC + (hr + 1) * 128],
                               ALU.mult, [AB[A_RQ + hr], CFB], [qdb])
                        O, Ob = rbank()
                        for hr in range(4):
                            hs = slice(hr * 128, (hr + 1) * 128)
                            mm(O[:, hs], sm[:, hs], rv_t[:, c, hs], True, False, [smb, rv_b[c]], [Ob])
                            mm(O[:, hs], qd[:, hs], state_bf[:, hr, :], False, True, [qdb, STBB], [Ob])
                        for hr in range(4):
                            hs = slice(hr * 128, (hr + 1) * 128)
                            P.op("dve", lambda e, hr=hr, hs=hs, O=O: e.bn_stats(out=small[:, hr * 6:hr * 6 + 6], in_=O[:, hs]), reads=[Ob], writes=[SMB])
                        for hr in range(4):
                            P.op("dve", lambda e, hr=hr: e.bn_aggr(out=small[:, 24 + hr * 2:26 + hr * 2], in_=small[:, hr * 6:hr * 6 + 6]), reads=[SMB], writes=[SMB])
                        for hr in range(4):
                            act(small[:, 32 + hr:33 + hr], small[:, 25 + hr * 2:26 + hr * 2], AF.Ln, [SMB], [SMB], bias=EPS)
                        act(small[:, 32:36], small[:, 32:36], AF.Exp, [SMB], [SMB], scale=-0.5)
                        on, onb = ftile()
                        for hr in range(4):
                            hs = slice(hr * 128, (hr + 1) * 128)
                            ts("dve", on[:, hs], O[:, hs], small[:, 24 + hr * 2:25 + hr * 2], small[:, 32 + hr:33 + hr],
                               ALU.subtract, ALU.mult, [Ob, SMB], [onb])
                        rt_, rtb = btile()
                        tt("pool", rt_[:, :], on[:, 0:512], gs_t[:, c, :], ALU.mult, [onb, gs_b[c]], [rtb])
                        tp, tpb = tbank()
                        for hr in range(4):
                            hs = slice(hr * 128, (hr + 1) * 128)
                            tr(tp[:, hs], rt_[:, hs], [rtb], [tpb])
                        P.op("dve", lambda e, tp=tp, cs=cs: e.tensor_copy(out=brT[:, BR_RET:BR_RET + 4, cs],
                                                                       in_=tp.rearrange("p (h n) -> p h n", n=128)),
                             reads=[tpb], writes=BRB[BR_RET:BR_RET + 4])
                        tp2, tp2b = tbank()
                        for hr in range(4):
                            hs = slice(hr * 128, (hr + 1) * 128)
                            tr(tp2[:, hs], arena[:, A_RK + hr, cs], [AB[A_RK + hr]], [tp2b])
                        kdk, kdkb = btile()
                        for hr in range(4):
                            hs = slice(hr * 128, (hr + 1) * 128)
                            ts("dve", kdk[:, hs], tp2[:, hs], cf[:, CF_KDEC + hr:CF_KDEC + hr + 1], None, ALU.mult, None, [tp2b, CFB], [kdkb])
                        KV, KVb = rbank()
                        for hr in range(4):
                            hs = slice(hr * 128, (hr + 1) * 128)
                            mm(KV[:, hs], kdk[:, hs], rv_t[:, c, hs], True, True, [kdkb, rv_b[c]], [KVb])
                        for hr in range(4):
                            hs = slice(hr * 128, (hr + 1) * 128)
                            stt(state[:, hr, :], state[:, hr, :], cd[hr], KV[:, hs], ALU.mult, ALU.add, [STB, KVb], [STB])
                        cp("pool", state_bf[:], state[:], [STB], [STBB])

                    nblk = 4 * g + 4
                    for hp in (range(4) if 'sb' not in DISABLE else []):
                        Bt = [PB[B_BANKS[0]], PB[B_BANKS[1]]]
                        Ot = [PB[OUT_BANKS[0]], PB[OUT_BANKS[1]]]
                        for hh in range(2):
                            mm(Bt[hh][0][:, :], ones_bf, zeros[:], True, False, [CBB, ZB], [Bt[hh][1]])
                            mm(Ot[hh][0][:, :], ones_bf, zeros[:], True, False, [CBB, ZB], [Ot[hh][1]])
                        items = list(reversed(range(nblk)))
                        n_it = len(items)
                        ctx = [None] * n_it

                        def S1(s):
                            j = items[s]
                            jj = j - 4 * g
                            c0 = max(jj, 0) * 128
                            kT = kc[:, hp, j * 128:(j + 1) * 128]
                            kb = KCB[hp][j // 4]
                            cs_ = []
                            for hh in range(2):
                                qa = (A_SQ if hh == 0 else A_SQN) + hp
                                Z, Zb = rbank()
                                mm(Z[:, c0:512], kT, arena[:, qa, c0:512], True, True, [kb, AB[qa]], [Zb])
                                cs_.append(dict(j=j, hh=hh, c0=c0, Z=Z, Zb=Zb))
                            for c in cs_:
                                e_, eb = ftile()
                                act(e_[:, c0:512], c["Z"][:, c0:512], AF.Exp, [c["Zb"]], [eb])
                                if jj >= 0:
                                    tt("pool", e_[:, c0:c0 + 128], e_[:, c0:c0 + 128], cf[:, CF_MASK:CF_MASK + 128], ALU.mult, [eb, CFB], [eb])
                                c["e"] = e_
                                c["eb"] = eb
                            ctx[s] = cs_

                        def S1b(s):
                            for c in ctx[s]:
                                c0 = c["c0"]
                                l_, lb = btile()
                                act(l_[:, c0:512], c["e"][:, c0:512], AF.Ln, [c["eb"]], [lb], bias=1.0)
                                c["l"] = l_
                                c["lb"] = lb

                        def S2(s):
                            for c in ctx[s]:
                                B_, Bb = Bt[c["hh"]]
                                c0 = c["c0"]
                                mm(B_[:, c0:512], lincl, c["l"][:, c0:512], False, False, [CBB, c["lb"]], [Bb])
                            for c in ctx[s]:
                                B_, Bb = Bt[c["hh"]]
                                c0 = c["c0"]
                                en, enb = ftile()
                                act(en[:, c0:512], B_[:, c0:512], AF.Exp, [Bb], [enb], scale=-1.0)
                                a_, ab_ = btile()
                                tt("dve", a_[:, c0:512], c["e"][:, c0:512], en[:, c0:512], ALU.mult, [c["eb"], enb], [ab_])
                                c["a"] = a_
                                c["ab"] = ab_

                        def S3(s):
                            for c in ctx[s]:
                                B_, Bb = Bt[c["hh"]]
                                O_, Ob_ = Ot[c["hh"]]
                                c0 = c["c0"]
                                j = c["j"]
                                mm(B_[:, c0:512], lcomp, c["l"][:, c0:512], False, False, [CBB, c["lb"]], [Bb])
                                mm(O_[:, c0:512], vc[:, j, hp * 128:(hp + 1) * 128], c["a"][:, c0:512], False, j == 0,
                                   [VCB[j], c["ab"]], [Ob_])
                            ctx[s] = None

                        for s_ in range(n_it + 2):
                            if s_ < n_it:
                                S1(s_)
                            if 0 <= s_ - 2 < n_it:
                                S3(s_ - 2)
                            if 0 <= s_ - 1 < n_it:
                                S2(s_ - 1)
                            if s_ < n_it:
                                S1b(s_)
                        act(brT[0:64, BR_SB + hp, :], Ot[0][0][0:64, :], AF.Copy, [Ot[0][1]], [BRB[BR_SB + hp]])
                        act(brT[64:128, BR_SB + hp, :], Ot[1][0][64:128, :], AF.Copy, [Ot[1][1]], [BRB[BR_SB + hp]])

                    for dg in (range(2) if 'merge' not in DISABLE else []):
                        for n in range(3):
                            def gate_consume(ci, pst, psb, n=n):
                                bc = pv_off(l, "bgate") + n * 8 + dg * 4 + ci
                                act(arena[:, A_G + n * 4 + ci, :], pst[:, :], AF.Sigmoid, [psb, PVB], [AB[A_G + n * 4 + ci]],
                                    bias=pv[:, bc:bc + 1])
                            linear_fm("in", l, win, 5120 + n * 1024 + dg * 512, 4, 8, h_rhs, gate_consume)
                        mt = [(marena[:, i * 512:(i + 1) * 512], None) for i in range(4)]
                        for n in range(3):
                            def up_consume(ci, pst, psb, n=n):
                                m_ = mt[ci][0]
                                mbs = [AB[20 + 2 * ci], AB[21 + 2 * ci]]
                                gsl = arena[:, A_G + n * 4 + ci, :]
                                gb = AB[A_G + n * 4 + ci]
                                if n == 0:
                                    tt("dve", m_[:, 0:512], pst[:, :], gsl, ALU.mult, [psb, gb], mbs)
                                else:
                                    t1, t1b = ftile()
                                    tt("dve", t1[:, 0:512], pst[:, :], gsl, ALU.mult, [psb, gb], [t1b])
                                    if n == 1:
                                        tt("pool", m_[:, 0:512], m_[:, 0:512], t1[:, 0:512], ALU.add, mbs + [t1b], mbs)
                                    else:
                                        d = dg * 4 + ci
                                        tt("pool", arena[:, A_MG + d, :], m_[:, 0:512], t1[:, 0:512], ALU.add, mbs + [t1b], [AB[A_MG + d]])
                            br_rhs = lambda k, n=n: (brT[:, n * 4 + k, :], BRB[n * 4 + k])
                            linear_fm("br", l, wb_br[l, n], dg * 512, 4, 4, br_rhs, up_consume)

                    def resid_consume(ci, pst, psb):
                        tt("dve", xt[:, ci, :], pst[:, :], xt[:, ci, :], ALU.add, [psb, XB[ci]], [XB[ci]])

                    mg_rhs = lambda k: (arena[:, A_MG + k, :], AB[A_MG + k])
                    if "merge" not in DISABLE:
                        linear_fm("o", l, wb_o[l], 0, 8, 8, mg_rhs, resid_consume)

                    if 'xattn' not in DISABLE:
                        rmsnorm(xt, XB, TG, pv_off(l, "nxa"), ht, HB)

                        def q2_consume(ci, pst, psb):
                            act(arena[:, A_Q2 + ci, :], pst[:, :], AF.Copy, [psb], [AB[A_Q2 + ci]], scale=1.0 / 16.0)

                        linear_fm("xq", l, wb_xq[l], 0, 8, 8, h_rhs, q2_consume)
                        for hx in range(4):
                            pts = []
                            for mb in range(2):
                                SC, SCb = rbank()
                                for kk in range(2):
                                    mm(SC[:, :], kmem[:, 2 * hx + kk, mb * 128:(mb + 1) * 128], arena[:, A_Q2 + 2 * hx + kk, :], kk == 0, kk == 1,
                                       [KMB, AB[A_Q2 + 2 * hx + kk]], [SCb])
                                p_, pb_ = btile()
                                act(p_[:, :], SC[:, :], AF.Exp, [SCb], [pb_])
                                pts.append((p_, pb_))
                            DEN, DENb = rbank()
                            for mb in range(2):
                                mm(DEN[:, :], ones_bf, pts[mb][0][:, :], mb == 0, mb == 1, [CBB, pts[mb][1]], [DENb])
                            rd_, rdb = ftile()
                            P.op("dve", lambda e, rd_=rd_, DEN=DEN: e.reciprocal(out=rd_[:, 0:512], in_=DEN[:, :]), reads=[DENb], writes=[rdb])
                            for kk in range(2):
                                OT, OTb = rbank()
                                dch = 2 * hx + kk
                                for mb in range(2):
                                    mm(OT[:, :], vmem[:, mb, dch * 128:(dch + 1) * 128], pts[mb][0][:, :], mb == 0, mb == 1, [VMB, pts[mb][1]], [OTb])
                                tt("dve", arena[:, A_OT + dch, :], OT[:, :], rd_[:, 0:512], ALU.mult, [OTb, rdb], [AB[A_OT + dch]])
                        ot_rhs = lambda k: (arena[:, A_OT + k, :], AB[A_OT + k])
                        linear_fm("xo", l, wb_xo[l], 0, 8, 8, ot_rhs, resid_consume)

                    if 'mlp' not in DISABLE:
                        rmsnorm(xt, XB, TG, pv_off(l, "nmlp"), ht, HB)

                        def up_mlp_consume(ci, pst, psb):
                            t1, t1b = ftile()
                            act(t1[:, 0:512], pst[:, :], AF.Relu, [psb], [t1b])
                            tt("pool", arena[:, ci, :], t1[:, 0:512], t1[:, 0:512], ALU.mult, [t1b], [AB[ci]])

                        linear_fm("up", l, wb_up[l], 0, 32, 8, h_rhs, up_mlp_consume)
                        for d in range(8):
                            slot, wsb = wslot()
                            view = slot[:, :].rearrange("p (c n) -> p c n", n=128)
                            load("sp", slot[:, :], wb_dn[l, d], wsb, R=[WB[("dn", l)]])
                            pst, psb = rbank()
                            for k in range(32):
                                mm(pst[:, :], view[:, k, :], arena[:, k, :], k == 0, k == 31, [wsb, AB[k]], [psb])
                            resid_consume(d, pst, psb)
                            if l < L - 1:
                                P.dma("pool", lambda e, xt=xt, t0=t0, d=d: [e.dma_start(out=xs_d[d * 128:(d + 1) * 128, t0:t0 + TG], in_=xt[:, d, :])],
                                      XB[d], reads=[XB[d]], writes=[XSD[g][d]])

                    if l == L - 1:
                        rmsnorm_final(P, xt, XB, sq, SQB, None, None, pv, PVB, L * PV_L, rbank, mm, act, stt, tt, ones_bf, CBB,
                                      ftile, out_d, t0, load)
                    else:
                        pass


        except StopBuild:
            pass
        P.op("sp", lambda e: None, reads=OUTD)
        P.generate(block)
        build.stats = (P.stats, P.sbytes)
    return nc


OUTD = []


def rmsnorm_final(P, xt, XB, sq, SQB, _u1, _u2, pv, PVB, gcol, rbank, mm, act, stt, tt, ones_bf, CBB, ftile, out_d, t0, load):
    for c in range(8):
        tt("dve", sq[:, c, :], xt[:, c, :], xt[:, c, :], ALU.mult, [XB[c]], [SQB[c]])
    pst, psb = rbank()
    rstd_f, RSB = ftile()
    rstd = rstd_f[:, 0:512]
    for c in range(8):
        mm(pst[:, :], ones_bf, sq[:, c, :], c == 0, c == 7, [SQB[c], CBB], [psb])
    act(rstd, pst[:, :], AF.Ln, [psb], [RSB], scale=1.0 / 1024.0, bias=EPS)
    act(rstd, rstd, AF.Exp, [RSB], [RSB], scale=-0.5)
    for c in range(8):
        o, ob = ftile()
        if ob is RSB:
            o, ob = ftile()
        stt(o[:, 0:512], xt[:, c, :], pv[:, gcol + c:gcol + c + 1], rstd, ALU.mult, ALU.mult, [XB[c], PVB, RSB], [ob])
        od = P.buf("outd")
        P.dma("pool", lambda e, o=o, c=c: [e.dma_start(out=out_d[c * 128:(c + 1) * 128, t0:t0 + 512], in_=o[:, 0:512])],
              ob, reads=[ob], writes=[od])
        OUTD.append(od)


_CACHE = {}


def run(inputs, T, L=2, n_cores=8, active=ACTIVE_CORES):
    f32 = np.float32
    x = np.asarray(inputs["x"], f32)
    mem = np.asarray(inputs["mem"], f32)
    B = x.shape[0]
    cf, cb, rot, cd = host_constants(T)
    key = (T, L)
    OUTD.clear()
    nc = build(T, L, cd)
    pvec = host_pvec(inputs, L)
    rgain = np.ascontiguousarray(np.broadcast_to(np.asarray(inputs["ret_norm_g"], f32)[:, None, :], (L, 128, 512)))
    common = {
        "pv": pvec, "cf": cf, "cb": cb, "rot": rot, "rgain": rgain,
        "w_in": np.asarray(inputs["w_in"], f32), "w_branch": np.asarray(inputs["w_branch"], f32),
        "w_o": np.asarray(inputs["w_o"], f32), "w_xq": np.asarray(inputs["w_xq"], f32),
        "w_xkv": np.asarray(inputs["w_xkv"], f32), "w_xo": np.asarray(inputs["w_xo"], f32),
        "w_up": np.asarray(inputs["w_up"], f32), "w_down": np.asarray(inputs["w_down"], f32),
    }
    in_maps = []
    zx = np.zeros((1024, T), f32)
    zm = np.zeros((1024, MEM_LEN), f32)
    slot = {c: i for i, c in enumerate(active)}
    for c in range(n_cores):
        m = dict(common)
        if c in slot and slot[c] < B:
            b = slot[c]
            m["xT"] = np.ascontiguousarray(x[b].T)
            m["memT"] = np.ascontiguousarray(mem[b].T)
        else:
            m["xT"] = zx
            m["memT"] = zm
        in_maps.append(m)
    res = run_bass_kernel_spmd(nc, in_maps, core_ids=list(range(n_cores)))
    out = np.empty((B, T, 1024), f32)
    for c, b in slot.items():
        if b < B:
            out[b] = np.asarray(res.results[c]["outT"], f32).T
    return out


def kernel(**inputs):
    T = inputs["x"].shape[1]
    return run(inputs, T)
```
